# Optimizing a Trainium2 kernel written in Bass

```python
import jax, jax.numpy as jnp
from jax import lax
import numpy as np

D_MODEL = 2048
BATCH = 4
SEQ = 2048
DEPTH = 1
DEC_BATCH = 128
DEC_SEQ = 8
PAST_LEN = 16384
PAGE_SIZE = 128

LRU_WIDTH = D_MODEL
LRU_HEADS = 16
LRU_BLOCK = LRU_WIDTH // LRU_HEADS
LRU_C = 8.0
CONV_WIDTH = 4
SSD_WIDTH = D_MODEL
SSD_HEADDIM = 64
SSD_HEADS = SSD_WIDTH // SSD_HEADDIM
SSD_GROUPS = 2
SSD_STATE = 128
SSD_CHUNK = 128
SSD_CONV_DIM = SSD_WIDTH + 2 * SSD_GROUPS * SSD_STATE
D_MIX = LRU_WIDTH + SSD_WIDTH
IN_SPLITS = [LRU_WIDTH, LRU_WIDTH, SSD_WIDTH, SSD_CONV_DIM, SSD_HEADS]
D_IN = sum(IN_SPLITS)
D_FF = ((8 * D_MODEL // 3 + 255) // 256) * 256
N_MEM = 256
XATTN_HEADS = 4
XATTN_HEAD_DIM = D_MODEL // XATTN_HEADS
EPS = 1e-6

kernel_name = "hymba_rglru_ssd_macaron_memxattn_step"


def rmsnorm(x, g):
    xf = x.astype(jnp.float32)
    xf = xf * lax.rsqrt(jnp.mean(xf * xf, axis=-1, keepdims=True) + EPS)
    return (xf * g.astype(jnp.float32)).astype(x.dtype)


def group_rmsnorm(x, g, n_groups):
    shp = x.shape
    xf = x.astype(jnp.float32).reshape(shp[:-1] + (n_groups, shp[-1] // n_groups))
    xf = xf * lax.rsqrt(jnp.mean(xf * xf, axis=-1, keepdims=True) + EPS)
    return xf.reshape(shp) * g.astype(jnp.float32)


def swiglu(x, w_gate, w_up, w_down):
    return (jax.nn.silu(x @ w_gate) * (x @ w_up)) @ w_down


def causal_conv(u, buf, w, b):
    ext = jnp.concatenate([buf.astype(u.dtype), u], axis=1)
    T = u.shape[1]
    y = b + ext[:, 0:T] * w[0]
    for k in range(1, CONV_WIDTH):
        y = y + ext[:, k:k + T] * w[k]
    return y, ext[:, -(CONV_WIDTH - 1):]


def linear_recurrence(a, b, h0):
    def combine(l, r):
        return (l[0] * r[0], r[0] * l[1] + r[1])
    a_cum, b_cum = lax.associative_scan(combine, (a, b), axis=1)
    h = a_cum * h0[:, None] + b_cum
    return h, h[:, -1]


def ssd_chunked(x, dt, A, B, C, h0):
    bsz, T, H, P = x.shape
    G, N = B.shape[2], B.shape[3]
    Q = H // G
    L = min(SSD_CHUNK, T)
    pad = (-T) % L
    if pad:
        pw = lambda t: jnp.pad(t, [(0, 0), (0, pad)] + [(0, 0)] * (t.ndim - 2))
        x, dt, B, C = pw(x), pw(dt), pw(B), pw(C)
    nc = (T + pad) // L
    x = x.reshape(bsz, nc, L, G, Q, P)
    dt = dt.reshape(bsz, nc, L, G, Q)
    B = B.reshape(bsz, nc, L, G, N)
    C = C.reshape(bsz, nc, L, G, N)
    cs = jnp.cumsum(dt * A.reshape(G, Q), axis=2)
    cs_t = jnp.moveaxis(cs, 2, -1)
    causal = jnp.tril(jnp.ones((L, L), dtype=bool))
    decay = jnp.exp(jnp.where(causal, cs_t[..., :, None] - cs_t[..., None, :], -jnp.inf))
    cb = jnp.einsum('bclgn,bcsgn->bcgls', C, B)
    dt_t = jnp.moveaxis(dt, 2, -1)
    w_diag = cb[:, :, :, None] * decay * dt_t[..., None, :]
    y_diag = jnp.einsum('bcgqls,bcsgqp->bclgqp', w_diag, x)
    to_end = jnp.exp(cs[:, :, -1:] - cs) * dt
    chunk_states = jnp.einsum('bclgn,bclgq,bclgqp->bcgqpn', B, to_end, x)
    chunk_decay = jnp.exp(cs[:, :, -1])

    def step(h, inp):
        dec, st = inp
        return dec[..., None, None] * h + st, h

    h_last, h_in = lax.scan(step, h0.reshape(bsz, G, Q, P, N),
                            (jnp.moveaxis(chunk_decay, 1, 0), jnp.moveaxis(chunk_states, 1, 0)))
    h_in = jnp.moveaxis(h_in, 0, 1)
    y_off = jnp.einsum('bclgn,bcgqpn,bclgq->bclgqp', C, h_in, jnp.exp(cs))
    y = (y_diag + y_off).reshape(bsz, nc * L, H, P)[:, :T]
    return y, h_last.reshape(bsz, H, P, N)


def mem_kv(mem, g, w_k, w_v):
    m = rmsnorm(mem, g)
    b = m.shape[0]
    k = (m @ w_k).reshape(b, -1, XATTN_HEADS, XATTN_HEAD_DIM)
    v = (m @ w_v).reshape(b, -1, XATTN_HEADS, XATTN_HEAD_DIM)
    return k, v


def cross_attn(h, k, v, w_q, w_o):
    b, T, _ = h.shape
    q = (h @ w_q).reshape(b, T, XATTN_HEADS, XATTN_HEAD_DIM)
    s = jnp.einsum('bthd,bmhd->bhtm', q.astype(jnp.float32), k.astype(jnp.float32)) * (XATTN_HEAD_DIM ** -0.5)
    p = jax.nn.softmax(s, axis=-1)
    o = jnp.einsum('bhtm,bmhd->bthd', p, v.astype(jnp.float32)).reshape(b, T, D_MODEL)
    return o.astype(h.dtype) @ w_o


def layer(x, k_mem, v_mem, lru_conv0, lru_h0, ssd_conv0, ssd_h0, w):
    (ffn1_g, ffn1_wg, ffn1_wu, ffn1_wd, mix_g, w_in,
     lru_conv_w, lru_conv_b, lru_w_a, lru_b_a, lru_w_x, lru_b_x, lru_lambda, lru_out_g,
     ssd_conv_w, ssd_conv_b, ssd_dt_bias, ssd_a_log, ssd_d, ssd_out_g, w_out,
     xattn_g, w_q, w_o, ffn2_g, ffn2_wg, ffn2_wu, ffn2_wd) = w
    f32 = jnp.float32
    x = x + 0.5 * swiglu(rmsnorm(x, ffn1_g), ffn1_wg, ffn1_wu, ffn1_wd)
    h = rmsnorm(x, mix_g)
    proj = h @ w_in
    idx = [int(v) for v in np.cumsum(IN_SPLITS)[:-1]]
    x_lru, g_lru, z, xbc, dt_raw = jnp.split(proj, idx, axis=-1)
    bsz, T = x.shape[0], x.shape[1]
    u, lru_conv1 = causal_conv(x_lru, lru_conv0, lru_conv_w, lru_conv_b)
    uf = u.astype(f32)
    uh = uf.reshape(bsz, T, LRU_HEADS, LRU_BLOCK)
    r = jax.nn.sigmoid(jnp.einsum('bthi,hij->bthj', uh, lru_w_a.astype(f32)).reshape(bsz, T, LRU_WIDTH) + lru_b_a)
    i = jax.nn.sigmoid(jnp.einsum('bthi,hij->bthj', uh, lru_w_x.astype(f32)).reshape(bsz, T, LRU_WIDTH) + lru_b_x)
    log_a = -LRU_C * r * jax.nn.softplus(-lru_lambda.astype(f32))
    a = jnp.exp(log_a)
    beta = jnp.sqrt(-jnp.expm1(2.0 * log_a))
    hs, lru_h1 = linear_recurrence(a, beta * i * uf, lru_h0.astype(f32))
    y_lru = rmsnorm(hs * jax.nn.gelu(g_lru.astype(f32)), lru_out_g)
    xbc_c, ssd_conv1 = causal_conv(xbc, ssd_conv0, ssd_conv_w, ssd_conv_b)
    xbc_c = jax.nn.silu(xbc_c.astype(f32))
    xs, Bm, Cm = jnp.split(xbc_c, [SSD_WIDTH, SSD_WIDTH + SSD_GROUPS * SSD_STATE], axis=-1)
    dt = jax.nn.softplus(dt_raw.astype(f32) + ssd_dt_bias)
    A = -jnp.exp(ssd_a_log.astype(f32))
    xh = xs.reshape(bsz, T, SSD_HEADS, SSD_HEADDIM)
    y, ssd_h1 = ssd_chunked(xh, dt, A,
                            Bm.reshape(bsz, T, SSD_GROUPS, SSD_STATE),
                            Cm.reshape(bsz, T, SSD_GROUPS, SSD_STATE),
                            ssd_h0.astype(f32))
    y = (y + ssd_d.astype(f32)[:, None] * xh).reshape(bsz, T, SSD_WIDTH)
    y_ssd = group_rmsnorm(y * jax.nn.silu(z.astype(f32)), ssd_out_g, SSD_GROUPS)
    mixed = jnp.concatenate([y_lru, y_ssd], axis=-1).astype(x.dtype) @ w_out
    x = x + mixed
    x = x + cross_attn(rmsnorm(x, xattn_g), k_mem, v_mem, w_q, w_o)
    x = x + 0.5 * swiglu(rmsnorm(x, ffn2_g), ffn2_wg, ffn2_wu, ffn2_wd)
    return x, (lru_conv1, lru_h1, ssd_conv1, ssd_h1)


def setup_inputs(seed: int = 0) -> dict:
    key = jax.random.key(seed)
    ks = iter(jax.random.split(key, 64))
    f32 = jnp.float32

    def normal(shape, scale):
        return scale * jax.random.normal(next(ks), shape, f32)

    def gain(shape):
        return 1.0 + normal(shape, 0.01)

    Lz = DEPTH
    a_init = jax.random.uniform(next(ks), (Lz, LRU_WIDTH), f32, 0.9, 0.999)
    s = a_init ** (1.0 / LRU_C)
    lru_lambda = jnp.log(s) - jnp.log1p(-s)
    dt_init = jnp.exp(jax.random.uniform(next(ks), (Lz, SSD_HEADS), f32, np.log(1e-3), np.log(1e-1)))
    ssd_dt_bias = dt_init + jnp.log(-jnp.expm1(-dt_init))
    ssd_a_log = jnp.log(jax.random.uniform(next(ks), (Lz, SSD_HEADS), f32, 1.0, 16.0))
    return {
        'x_prompt': normal((BATCH, SEQ, D_MODEL), 1.0),
        'mem_prompt': normal((BATCH, N_MEM, D_MODEL), 1.0),
        'x_sample': normal((DEC_BATCH, DEC_SEQ, D_MODEL), 1.0),
        'cache_mem_k': normal((Lz, DEC_BATCH, N_MEM, XATTN_HEADS, XATTN_HEAD_DIM), 1.0),
        'cache_mem_v': normal((Lz, DEC_BATCH, N_MEM, XATTN_HEADS, XATTN_HEAD_DIM), 1.0),
        'state_lru_conv': normal((Lz, DEC_BATCH, CONV_WIDTH - 1, LRU_WIDTH), 1.0),
        'state_lru_h': normal((Lz, DEC_BATCH, LRU_WIDTH), 0.5),
        'state_ssd_conv': normal((Lz, DEC_BATCH, CONV_WIDTH - 1, SSD_CONV_DIM), 1.0),
        'state_ssd_h': normal((Lz, DEC_BATCH, SSD_HEADS, SSD_HEADDIM, SSD_STATE), 0.1),
        'ffn1_norm_g': gain((Lz, D_MODEL)),
        'ffn1_w_gate': normal((Lz, D_MODEL, D_FF), D_MODEL ** -0.5),
        'ffn1_w_up': normal((Lz, D_MODEL, D_FF), D_MODEL ** -0.5),
        'ffn1_w_down': normal((Lz, D_FF, D_MODEL), D_FF ** -0.5),
        'mix_norm_g': gain((Lz, D_MODEL)),
        'w_in': normal((Lz, D_MODEL, D_IN), D_MODEL ** -0.5),
        'lru_conv_w': normal((Lz, CONV_WIDTH, LRU_WIDTH), CONV_WIDTH ** -0.5),
        'lru_conv_b': normal((Lz, LRU_WIDTH), 0.01),
        'lru_w_a': normal((Lz, LRU_HEADS, LRU_BLOCK, LRU_BLOCK), LRU_BLOCK ** -0.5),
        'lru_b_a': normal((Lz, LRU_WIDTH), 0.01),
        'lru_w_x': normal((Lz, LRU_HEADS, LRU_BLOCK, LRU_BLOCK), LRU_BLOCK ** -0.5),
        'lru_b_x': normal((Lz, LRU_WIDTH), 0.01),
        'lru_lambda': lru_lambda,
        'lru_out_norm_g': gain((Lz, LRU_WIDTH)),
        'ssd_conv_w': normal((Lz, CONV_WIDTH, SSD_CONV_DIM), CONV_WIDTH ** -0.5),
        'ssd_conv_b': normal((Lz, SSD_CONV_DIM), 0.01),
        'ssd_dt_bias': ssd_dt_bias,
        'ssd_a_log': ssd_a_log,
        'ssd_d': gain((Lz, SSD_HEADS)),
        'ssd_out_norm_g': gain((Lz, SSD_WIDTH)),
        'w_out': normal((Lz, D_MIX, D_MODEL), D_MIX ** -0.5),
        'xattn_norm_g': gain((Lz, D_MODEL)),
        'mem_norm_g': gain((Lz, D_MODEL)),
        'xattn_w_q': normal((Lz, D_MODEL, D_MODEL), D_MODEL ** -0.5),
        'xattn_w_k': normal((Lz, D_MODEL, D_MODEL), D_MODEL ** -0.5),
        'xattn_w_v': normal((Lz, D_MODEL, D_MODEL), D_MODEL ** -0.5),
        'xattn_w_o': normal((Lz, D_MODEL, D_MODEL), D_MODEL ** -0.5),
        'ffn2_norm_g': gain((Lz, D_MODEL)),
        'ffn2_w_gate': normal((Lz, D_MODEL, D_FF), D_MODEL ** -0.5),
        'ffn2_w_up': normal((Lz, D_MODEL, D_FF), D_MODEL ** -0.5),
        'ffn2_w_down': normal((Lz, D_FF, D_MODEL), D_FF ** -0.5),
        'final_norm_g': gain((D_MODEL,)),
    }


def reference(x_prompt, mem_prompt, x_sample, cache_mem_k, cache_mem_v,
              state_lru_conv, state_lru_h, state_ssd_conv, state_ssd_h,
              ffn1_norm_g, ffn1_w_gate, ffn1_w_up, ffn1_w_down, mix_norm_g, w_in,
              lru_conv_w, lru_conv_b, lru_w_a, lru_b_a, lru_w_x, lru_b_x, lru_lambda, lru_out_norm_g,
              ssd_conv_w, ssd_conv_b, ssd_dt_bias, ssd_a_log, ssd_d, ssd_out_norm_g, w_out,
              xattn_norm_g, mem_norm_g, xattn_w_q, xattn_w_k, xattn_w_v, xattn_w_o,
              ffn2_norm_g, ffn2_w_gate, ffn2_w_up, ffn2_w_down, final_norm_g):
    layer_weights = (ffn1_norm_g, ffn1_w_gate, ffn1_w_up, ffn1_w_down, mix_norm_g, w_in,
                     lru_conv_w, lru_conv_b, lru_w_a, lru_b_a, lru_w_x, lru_b_x, lru_lambda, lru_out_norm_g,
                     ssd_conv_w, ssd_conv_b, ssd_dt_bias, ssd_a_log, ssd_d, ssd_out_norm_g, w_out,
                     xattn_norm_g, xattn_w_q, xattn_w_o, ffn2_norm_g, ffn2_w_gate, ffn2_w_up, ffn2_w_down)
    bp = x_prompt.shape[0]
    xp, xs = x_prompt, x_sample
    p_lc, p_lh, p_sc, p_sh, p_mk, p_mv = [], [], [], [], [], []
    s_lc, s_lh, s_sc, s_sh = [], [], [], []
    for l in range(DEPTH):
        w = tuple(t[l] for t in layer_weights)
        mk, mv = mem_kv(mem_prompt, mem_norm_g[l], xattn_w_k[l], xattn_w_v[l])
        xp, (lc, lh, sc, sh) = layer(
            xp, mk, mv,
            jnp.zeros((bp, CONV_WIDTH - 1, LRU_WIDTH), xp.dtype),
            jnp.zeros((bp, LRU_WIDTH), jnp.float32),
            jnp.zeros((bp, CONV_WIDTH - 1, SSD_CONV_DIM), xp.dtype),
            jnp.zeros((bp, SSD_HEADS, SSD_HEADDIM, SSD_STATE), jnp.float32),
            w)
        p_lc.append(lc); p_lh.append(lh); p_sc.append(sc); p_sh.append(sh); p_mk.append(mk); p_mv.append(mv)
        xs, (lc, lh, sc, sh) = layer(
            xs, cache_mem_k[l], cache_mem_v[l],
            state_lru_conv[l], state_lru_h[l], state_ssd_conv[l], state_ssd_h[l], w)
        s_lc.append(lc); s_lh.append(lh); s_sc.append(sc); s_sh.append(sh)
    y_prompt = rmsnorm(xp, final_norm_g)
    y_sample = rmsnorm(xs, final_norm_g)
    return (y_prompt, y_sample,
            jnp.stack(p_lc), jnp.stack(p_lh), jnp.stack(p_sc), jnp.stack(p_sh), jnp.stack(p_mk), jnp.stack(p_mv),
            jnp.stack(s_lc), jnp.stack(s_lh), jnp.stack(s_sc), jnp.stack(s_sh))
```

```python
import os
import numpy as np
import concourse.bass as bass
import concourse.mybir as mybir
from concourse.bass_utils import run_bass_kernel_spmd

F32 = mybir.dt.float32
BF16 = mybir.dt.bfloat16
AF = mybir.ActivationFunctionType
ALU = mybir.AluOpType

D = 2048
FF = 5632
NHID = FF // 128
HG = 4
HPG = NHID // HG
DIN = 8736
NP1 = 1024
NS = 128
NT2 = NP1 + NS
NSEQ = 16
EPS = 1e-6


class KB:
    ENG = ("pe", "act", "dve", "pool", "sp")

    def __init__(self, nc):
        self.nc = nc
        self.ops = {e: [] for e in self.ENG}
        self.sem = {e: nc.alloc_semaphore("s_" + e) for e in ("pe", "act", "dve", "pool")}
        self.cnt = {e: 0 for e in ("pe", "act", "dve", "pool")}
        self.waited = {e: {} for e in self.ENG}
        self.lastw = {}
        self.readers = {}
        self.dsem = {}
        self.dcnt = {}
        self.n_inst = 0

    def _deps(self, reads, writes):
        deps = {}

        def add(tok):
            if tok is None:
                return
            s, v = tok
            if s in self.dcnt:
                v = self.dcnt[s]
            if deps.get(s, 0) < v:
                deps[s] = v

        for k in reads:
            add(self.lastw.get(k))
        for k in writes:
            add(self.lastw.get(k))
            for tok in self.readers.get(k, {}).items():
                add(tok)
        return deps

    def _commit(self, tok, reads, writes):
        for k in reads:
            r = self.readers.setdefault(k, {})
            if r.get(tok[0], 0) < tok[1]:
                r[tok[0]] = tok[1]
        for k in writes:
            self.lastw[k] = tok
            self.readers[k] = {}

    def _waits(self, eng, deps):
        ws = []
        for s, v in deps.items():
            if self.waited[eng].get(s, 0) >= v:
                continue
            self.waited[eng][s] = v
            ws.append((s, v))
        return ws

    def _semh(self, s):
        return self.sem[s] if s in self.sem else self.dsem[s]

    def op(self, eng, fn, reads=(), writes=(), inc=True):
        inc = True
        deps = self._deps(reads, writes)
        ws = self._waits(eng, deps)
        if inc:
            self.cnt[eng] += 1
            tok = (eng, self.cnt[eng])
        else:
            tok = (eng, self.cnt[eng] + 1)
        self.ops[eng].append((ws, fn, (self.sem[eng], 1) if inc else None))
        self._commit(tok, reads, writes)
        self.n_inst += 1

    def dma(self, q, out, in_, slot, reads=(), writes=()):
        if slot not in self.dsem:
            self.dsem[slot] = self.nc.alloc_semaphore("d%d" % len(self.dsem))
            self.dcnt[slot] = 0
        deps = self._deps(reads, writes)
        ws = self._waits(q, deps)
        self.dcnt[slot] += 16
        tok = (slot, self.dcnt[slot])
        self.ops[q].append((ws, lambda e, o=out, i=in_: e.dma_start(out=o, in_=i), (self.dsem[slot], 16)))
        self._commit(tok, reads, writes)
        self.n_inst += 1

    def barrier(self):
        toks = {e: v for e, v in self.cnt.items() if v > 0}
        toks.update({s: v for s, v in self.dcnt.items() if v > 0})
        for e in self.ENG:
            ws = self._waits(e, dict(toks))
            if ws:
                self.ops[e].append((ws, None, None))

    def final_wait(self, eng, keys):
        deps = self._deps(keys, keys)
        ws = self._waits(eng, deps)
        self.ops[eng].append((ws, None, None))

    def emit(self):
        nc = self.nc
        hmap = {"pe": "tensor", "act": "scalar", "dve": "vector", "pool": "gpsimd", "sp": "sync"}
        with nc.Block() as block:
            for e in self.ENG:
                ops = self.ops[e]

                def body(eng, ops=ops):
                    for ws, fn, inc in ops:
                        for s, v in ws:
                            eng.wait_ge(self._semh(s), v)
                        if fn is None:
                            continue
                        ins = fn(eng)
                        if inc is not None:
                            ins.then_inc(inc[0], inc[1])

                getattr(block, hmap[e])(body)


W2D = {
    "ffn1_wg": (D, FF), "ffn1_wu": (D, FF), "ffn1_wd": (FF, D),
    "ffn2_wg": (D, FF), "ffn2_wu": (D, FF), "ffn2_wd": (FF, D),
    "w_in": (D, DIN), "w_out": (2 * D, D),
    "w_q": (D, D), "w_k": (D, D), "w_v": (D, D), "w_o": (D, D),
    "lru_wa": (D, 128), "lru_wx": (D, 128),
}
PFM = {
    "ffn1_g": (16,), "mix_g": (16,), "xattn_g": (16,), "mem_g": (16,), "ffn2_g": (16,), "final_g": (16,),
    "lru_cw": (16, 4), "lru_cb": (16,), "lru_ba": (16,), "lru_bx": (16,), "lru_lam": (16,), "lru_og": (16,),
    "ssd_cw": (20, 4), "ssd_cb": (20,), "ssd_og": (16,), "ssd_dfm": (16,),
}
CONSTS = {"c_ident": (128,), "c_ut": (128,), "c_slt": (128,), "c_uts": (128,), "c_same": (128,),
          "c_ones": (128,), "c_seq": (16,)}


def build_program(stage=99):
    nc = bass.Bass("TRN2", target_bir_lowering=False)
    kb = KB(nc)

    kb.inputs = []

    def din(name, shape):
        kb.inputs.append(name)
        return nc.dram_tensor(name, list(shape), F32, kind="ExternalInput").ap()

    def dout(name, shape):
        return nc.dram_tensor(name, list(shape), F32, kind="ExternalOutput").ap()

    def sb(name, shape, dt=F32):
        return nc.alloc_sbuf_tensor(name, list(shape), dt)

    xa_d = din("xa", [128, 16, NP1])
    xb_d = din("xb", [128, 16, NT2])
    flag_d = din("flag", [128, 1])
    mem_d = din("memT", [128, 16, 256])
    ckT_d = din("ckT", [NSEQ, 128, 16, 256]) if stage >= 7 else None
    cv_d = din("cv", [NSEQ, 128, 2, D]) if stage >= 7 else None
    slc_d = din("slc", [128, 16, NSEQ, 3])
    slh_d = din("slh", [128, 16, NSEQ])
    ssc_d = din("ssc", [128, 20, NSEQ, 3])
    sshN_d = din("sshN", [16, 128, NSEQ, 128]) if stage >= 6 else None
    sshT_d = din("sshT", [16, 128, NSEQ, 128]) if stage >= 6 else None
    dtb_d = din("dtb", [32, 1])
    alog_d = din("alog", [32, 1])
    class _LazyW(dict):
        def __missing__(self, k):
            self[k] = din(k, W2D[k])
            return self[k]

    Wd = _LazyW()
    if stage >= 99:
        for k in W2D:
            Wd[k]
    Pd = {k: din(k, (128,) + v) for k, v in PFM.items()}
    Cd = {k: din(k, (128,) + v) for k, v in CONSTS.items()}

    y_d = dout("y", [128, 16, NT2])
    o_plc = dout("o_plc", [128, 16, 3])
    o_plh = dout("o_plh", [128, 16])
    o_psc = dout("o_psc", [128, 20, 3])
    o_pshT = dout("o_pshT", [128, 16, 128])
    o_mk = dout("o_mkT", [128, 16, 256])
    o_mv = dout("o_mvT", [128, 16, 256])
    o_slc = dout("o_slc", [128, 16, NSEQ, 3])
    o_slh = dout("o_slh", [128, 16, NSEQ])
    o_ssc = dout("o_ssc", [128, 20, NSEQ, 3])
    o_ssh = dout("o_ssh", [16, 128, NSEQ, 128])
    outkeys = []

    ARENA_BYTES = 212000
    ARENA = nc.alloc_sbuf_tensor("arena", [128, ARENA_BYTES // 4], F32)
    bump = {"p": 0}
    OFF = {}

    def view(off, shape, dt):
        n = int(np.prod(shape))
        isz = 4 if dt is F32 else 2
        assert off % 4 == 0
        nb = (n * isz + 31) // 32 * 32
        assert off + nb <= ARENA_BYTES, ("SBUF overflow", off, nb)
        v = ARENA[:, off // 4:off // 4 + (n * isz + 3) // 4]
        if dt is not F32:
            v = v.bitcast(dt)[:, 0:n]
        if len(shape) == 2:
            v = v.rearrange("p (a b) -> p a b", b=shape[1])
        elif len(shape) == 3:
            v = v.rearrange("p (a b c) -> p a b c", b=shape[1], c=shape[2])
        return v, nb

    def sb(name, shape, dt=F32):
        shape = list(shape)[1:]
        v, nb = view(bump["p"], shape, dt)
        OFF[name] = bump["p"]
        bump["p"] += nb
        return v

    X = sb("X", [128, 16, NT2])
    XN = sb("XN", [128, 16, NT2], BF16)
    MIX = sb("MIX", [128, 16, NT2], BF16)
    HID = MIX
    NWS = 4
    wslots = [sb("ws%d" % i, [128, 16, 128], BF16) for i in range(NWS)]
    prm = {k: sb("p_" + k, (128,) + v) for k, v in PFM.items()}
    cst = {k: sb(k, (128,) + v) for k, v in CONSTS.items()}
    cst_bf = {k: sb(k + "_bf", (128,) + CONSTS[k], BF16) for k in ("c_ones", "c_ident")}
    FLAG = sb("FLAG", [128, 1])
    EPSC = sb("EPSC", [128, 1])
    ONEC = sb("ONEC", [128, 1])
    LRUC = sb("LRUC", [128, 16])
    LRUC2 = sb("LRUC2", [128, 16])
    TAIL_L = sb("TAIL_L", [128, 16, 3])
    TAIL_S = sb("TAIL_S", [128, 20, 3])
    HEND1 = sb("HEND1", [128, 16])
    HINIT = sb("HINIT", [128, 16])
    H0S_L = sb("H0S_L", [128, 16, NSEQ])
    O_PLC = sb("O_PLC", [128, 16, 3])
    O_PLH = sb("O_PLH", [128, 16])
    O_PSC = sb("O_PSC", [128, 20, 3])
    O_SLH = sb("O_SLH", [128, 16, NSEQ])
    DTBA = sb("DTBA", [128, 2])
    SCR0 = bump["p"]
    T0 = sb("T0", [128, NT2])
    SQ = [sb("SQ%d" % i, [128, 512], BF16) for i in range(2)]
    SCR1 = bump["p"]
    print("SBUF persistent bytes", SCR0, "scratch avail", ARENA_BYTES - SCR1)
    RSTD = T0

    def scratch(base=None):
        bump["p"] = SCR1 if base is None else base

    hspill = nc.dram_tensor("hspill", [128, 16, 128], F32).ap()

    PA = nc.alloc_psum_tensor("PA", [128, 2048], F32)
    PB = nc.alloc_psum_tensor("PB", [128, 2048], F32)

    def pk(P, c0, c1):
        nm = "PA" if P is PA else "PB"
        return [(nm, b) for b in range(c0 // 512, (c1 - 1) // 512 + 1)]

    def sslot(i):
        P = PA if i < 4 else PB
        b = i % 4
        return P[:, b * 512:b * 512 + 128], ("PA" if i < 4 else "PB", b)

    wq = []
    wstate = {"issued": 0, "taken": 0}

    def wplan(name, r0, KC, c0, ncol=128):
        ap = Wd[name][r0:r0 + KC * 128, c0:c0 + ncol].rearrange("(kc p) f -> p kc f", p=128)
        wq.append((ap, KC, ncol))

    def wget():
        while wstate["issued"] < min(len(wq), wstate["taken"] + NWS - 1):
            i = wstate["issued"]
            ap, KC, ncol = wq[i]
            s = i % NWS
            kb.dma("pool", wslots[s][:, 0:KC, 0:ncol], ap, ("wd", s), writes=[("w", s)])
            wstate["issued"] += 1
        assert wstate["taken"] < len(wq)
        s = wstate["taken"] % NWS
        wstate["taken"] += 1
        return wslots[s], ("w", s)

    def mm(out, lhsT, rhs, start, stop, reads, writes, inc):
        kb.op("pe", lambda e: e.matmul(out, lhsT=lhsT, rhs=rhs, start=start, stop=stop),
              reads=reads, writes=writes, inc=inc)

    def proj(w, wkey, KC, rhs_fn, rkeys, P, segs, M=128):
        for (c0, n) in segs:
            for kc in range(KC):
                mm(P[0:M, c0:c0 + n], w[:, kc, 0:M], rhs_fn(kc, c0, n), kc == 0, kc == KC - 1,
                   [wkey] + rkeys, pk(P, c0, c0 + n), kc == KC - 1)

    def act(out, in_, func, reads, writes, bias=None, scale=None, eng="act"):
        kw = {}
        if bias is not None:
            kw["bias"] = bias
        if scale is not None:
            kw["scale"] = scale
        kb.op("act", lambda e: e.activation(out=out, in_=in_, func=func, **kw), reads=reads, writes=writes)

    def tt(eng, out, a, b, op, reads, writes):
        kb.op(eng, lambda e: e.tensor_tensor(out=out, in0=a, in1=b, op=op), reads=reads, writes=writes)

    def ts(eng, out, a, s1, s2, op0, op1, reads, writes):
        if op1 is None:
            kb.op(eng, lambda e: e.tensor_scalar(out=out, in0=a, scalar1=s1, scalar2=None, op0=op0),
                  reads=reads, writes=writes)
        else:
            kb.op(eng, lambda e: e.tensor_scalar(out=out, in0=a, scalar1=s1, scalar2=s2, op0=op0, op1=op1),
                  reads=reads, writes=writes)

    def stt(eng, out, a, s, b, op0, op1, reads, writes):
        eng = "dve"
        kb.op(eng, lambda e: e.scalar_tensor_tensor(out=out, in0=a, scalar=s, in1=b, op0=op0, op1=op1),
              reads=reads, writes=writes)

    def cp(eng, out, in_, reads, writes):
        if eng == "act":
            kb.op("act", lambda e: e.copy(out=out, in_=in_), reads=reads, writes=writes)
        else:
            kb.op(eng, lambda e: e.tensor_copy(out=out, in_=in_), reads=reads, writes=writes)

    def memset(eng, ap, val, writes):
        kb.op(eng, lambda e: e.memset(ap, val), writes=writes)

    for k in PFM:
        kb.dma("sp", prm[k][:], Pd[k], "ld0", writes=["p_" + k])
    for k in CONSTS:
        kb.dma("sp", cst[k][:], Cd[k], "ld0", writes=[k])
    kb.dma("sp", FLAG[:], flag_d, "ld0", writes=["FLAG"])
    kb.dma("sp", DTBA[0:32, 0:1], dtb_d, "ld0", writes=["DTBA"])
    kb.dma("sp", DTBA[0:32, 1:2], alog_d, "ld0", writes=["DTBA"])
    kb.dma("sp", H0S_L[:], slh_d, "ld0", writes=["H0S_L"])
    memset("dve", EPSC[:], EPS, ["EPSC"])
    memset("dve", ONEC[:], 1.0, ["ONEC"])
    for k in ("c_ones", "c_ident"):
        cp("dve", cst_bf[k][:], cst[k][:], [k], [k + "_bf"])
    act(DTBA[0:32, 1:2], DTBA[0:32, 1:2], AF.Exp, ["DTBA"], ["DTBA"])
    ts("dve", DTBA[0:32, 1:2], DTBA[0:32, 1:2], -1.0, None, ALU.mult, None, ["DTBA"], ["DTBA"])
    act(LRUC[:], prm["lru_lam"][:], AF.Exp, ["p_lru_lam"], ["LRUC"], scale=-1.0)
    act(LRUC[:], LRUC[:], AF.Ln, ["LRUC", "ONEC"], ["LRUC"], bias=ONEC[:, 0:1])
    ts("dve", LRUC2[:], LRUC[:], -16.0, None, ALU.mult, None, ["LRUC"], ["LRUC2"])
    ts("dve", LRUC[:], LRUC[:], -8.0, None, ALU.mult, None, ["LRUC"], ["LRUC"])

    ones_bf = cst_bf["c_ones"]

    def xkeys(c=None):
        return [("X", c)] if c is not None else [("X", i) for i in range(16)]

    sqi = {"i": 0}

    def rms_rstd(src_fn, skeys_fn, nchunk, NT, segs, scale=1.0 / D):
        for c in range(nchunk):
            for (c0, n) in segs:
                i = sqi["i"] % 2
                sqi["i"] += 1
                act(SQ[i][:, 0:n], src_fn(c)[:, c0:c0 + n], AF.Square, skeys_fn(c), [("SQ", i)])
                mm(PA[:, c0:c0 + n], ones_bf[:, :], SQ[i][:, 0:n], c == 0, c == nchunk - 1,
                   [("SQ", i), "c_ones_bf"], pk(PA, c0, c0 + n), True)
        act(RSTD[:, 0:NT], PA[:, 0:NT], AF.Sqrt, pk(PA, 0, NT) + ["EPSC"], [("T", 0)], bias=EPSC[:, 0:1], scale=scale)
        kb.op("dve", lambda e: e.reciprocal(out=RSTD[:, 0:NT], in_=RSTD[:, 0:NT]), reads=[("T", 0)], writes=[("T", 0)])

    def norm_to_xn(gname, NT, segs):
        rms_rstd(lambda c: X[:, c, :], lambda c: xkeys(c), 16, NT, segs)
        for c in range(16):
            stt("dve" if c % 2 == 0 else "pool", XN[:, c, 0:NT], X[:, c, 0:NT], prm[gname][:, c:c + 1], RSTD[:, 0:NT],
                ALU.mult, ALU.mult, xkeys(c) + [("T", 0), "p_" + gname], [("XN", c)])

    def xn_rhs(kc, c0, n):
        return XN[:, kc, c0:c0 + n]

    xnkeys = [("XN", c) for c in range(16)]
    mixkeys = [("MIX", c) for c in range(16)]

    def ffn(pre, gname, NT, segs):
        if os.environ.get("DBG_SKIP") == "f":
            return
        scratch()
        T1 = sb("T1", [128, NT2])
        Ts = [T0, T1]
        SUB = int(os.environ.get("DBG_SUB", "99"))
        if SUB == 0:
            rms_rstd(lambda c: X[:, c, :], lambda c: xkeys(c), 16, NT, segs)
            return
        norm_to_xn(gname, NT, segs)
        if SUB == 1:
            return
        for hg in range(HG):
            for j in range(HPG):
                f = hg * HPG + j
                wplan(pre + "_wg", 0, 16, f * 128)
                wplan(pre + "_wu", 0, 16, f * 128)
            for dc in range(16):
                wplan(pre + "_wd", hg * HPG * 128, HPG, dc * 128)
        for hg in range(HG):
            for j in range(HPG):
                wg, kg = wget()
                proj(wg, kg, 16, xn_rhs, xnkeys, PA, segs)
                wu, ku = wget()
                proj(wu, ku, 16, xn_rhs, xnkeys, PB, segs)
                if SUB == 2:
                    return
                sg = Ts[j % 2]
                act(sg[:, 0:NT], PA[:, 0:NT], AF.Silu, pk(PA, 0, NT), [("T", j % 2)])
                tt("dve", HID[:, j, 0:NT], sg[:, 0:NT], PB[:, 0:NT], ALU.mult,
                   [("T", j % 2)] + pk(PB, 0, NT), [("MIX", j)])
                if SUB == 3:
                    return
            if SUB == 4:
                return
            for dc in range(16):
                wd, kd = wget()
                for si, (c0, n) in enumerate(segs):
                    P = PA if (dc * 3 + si) % 2 == 0 else PB
                    for j in range(HPG):
                        mm(P[:, 1536:1536 + n], wd[:, j, :], HID[:, j, c0:c0 + n], j == 0, j == HPG - 1,
                           [kd, ("MIX", j)], pk(P, 1536, 2048), j == HPG - 1)
                    stt("dve", X[:, dc, c0:c0 + n], P[:, 1536:1536 + n], 0.5, X[:, dc, c0:c0 + n], ALU.mult, ALU.add,
                        pk(P, 1536, 2048) + xkeys(dc), xkeys(dc))
        kb.barrier()

    def conv_silu(P, cw, cb, ci, tail_init, tname, convs, cname, NT, full, out_u, ukey, tail_save, tsname,
                  o_p, opname, o_s, osname, silu, EXTP, EXTS):
        pkeys = pk(P, 0, NT)
        if tail_init is None:
            memset("pool", EXTP[:, 0:3], 0.0, ["EXTP"])
        else:
            ts("pool", EXTP[:, 0:3], tail_init[:, ci, :], FLAG[:, 0:1], None, ALU.mult, None, [tname, "FLAG"], ["EXTP"])
        cp("act", EXTP[:, 3:3 + NP1], P[:, 0:NP1], pkeys, ["EXTP"])
        if tail_save is not None:
            cp("pool", tail_save[:, ci, :], EXTP[:, NP1:NP1 + 3], ["EXTP"], [tsname])
        if o_p is not None:
            cp("pool", o_p[:, ci, :], EXTP[:, NP1:NP1 + 3], ["EXTP"], [opname])
        up = out_u[:, 0:NP1]
        ts("dve", up, EXTP[:, 0:NP1], cw[:, ci, 0:1], cb[:, ci:ci + 1], ALU.mult, ALU.add, ["EXTP"], [ukey])
        for k in range(1, 4):
            stt("dve", up, EXTP[:, k:k + NP1], cw[:, ci, k:k + 1], up, ALU.mult, ALU.add, ["EXTP", ukey], [ukey])
        if full:
            kb.dma("sp", EXTS[:, :, 0:3], convs[:, ci, :, :], "ld_cv3", writes=["EXTS"])
            cp("act", EXTS[:, :, 3:11], P[:, NP1:NT2].rearrange("p (s t) -> p s t", t=8), pkeys, ["EXTS"])
            kb.dma("sp", o_s[:, ci, :, :], EXTS[:, :, 8:11], "st_cv3", reads=["EXTS"])
            us = out_u[:, NP1:NT2].rearrange("p (s t) -> p s t", t=8)
            ts("dve", us, EXTS[:, :, 0:8], cw[:, ci, 0:1], cb[:, ci:ci + 1], ALU.mult, ALU.add, ["EXTS", ukey], [ukey])
            for k in range(1, 4):
                stt("dve", us, EXTS[:, :, k:k + 8], cw[:, ci, k:k + 1], us, ALU.mult, ALU.add, ["EXTS", ukey], [ukey])
        if silu:
            act(out_u[:, 0:NT], out_u[:, 0:NT], AF.Silu, [ukey], [ukey])

    def wout_half(r0, NT, segs, wname="w_out"):
        for dc in range(16):
            wplan(wname, r0, 16, dc * 128)
        for dc in range(16):
            w, k = wget()
            P = PA if dc % 2 == 0 else PB
            proj(w, k, 16, lambda kc, c0, n: MIX[:, kc, c0:c0 + n], mixkeys, P, segs)
            tt("dve", X[:, dc, 0:NT], X[:, dc, 0:NT], P[:, 0:NT], ALU.add, xkeys(dc) + pk(P, 0, NT), xkeys(dc))

    def lru_phase(blk, NT, segs):
        full = blk == 2
        scratch()
        T1 = sb("T1", [128, NT2])
        T2 = sb("T2", [128, NT2])
        T3 = sb("T3", [128, NT2])
        UB = sb("UB", [128, NT2], BF16)
        GL = sb("GL", [128, NT2], BF16)
        EXTP = sb("EXTP", [128, 3 + NP1])
        EXTS = sb("EXTS", [128, NSEQ, 11])
        for j in range(16):
            if full:
                wplan("w_in", 0, 16, 2048 + j * 128)
            wplan("w_in", 0, 16, j * 128)
            wplan("lru_wa", j * 128, 1, 0)
            wplan("lru_wx", j * 128, 1, 0)
        if full:
            ts("dve", HINIT[:, :], HEND1[:, :], FLAG[:, 0:1], None, ALU.mult, None, ["HEND1", "FLAG"], ["HINIT"])
        for j in range(16):
            if full:
                wg_, kg_ = wget()
                proj(wg_, kg_, 16, xn_rhs, xnkeys, PB, segs)
                act(GL[:, 0:NT], PB[:, 0:NT], AF.Gelu, pk(PB, 0, NT), ["GL"])
            wx_, kx_ = wget()
            proj(wx_, kx_, 16, xn_rhs, xnkeys, PA, segs)
            U = T0
            conv_silu(PA, prm["lru_cw"], prm["lru_cb"], j, TAIL_L if full else None, "TAIL_L", slc_d, "CONVS_L", NT, full,
                      U, ("T", 0), None if full else TAIL_L, "TAIL_L", O_PLC if full else None, "O_PLC",
                      o_slc if full else None, "O_SLC", False, EXTP, EXTS)
            cp("pool", UB[:, 0:NT], U[:, 0:NT], [("T", 0)], ["UB"])
            wa_, ka_ = wget()
            wxx_, kxx_ = wget()
            for (c0, n) in segs:
                mm(PA[:, c0:c0 + n], wa_[:, 0, :], UB[:, c0:c0 + n], True, True, [ka_, "UB"], pk(PA, c0, c0 + n), True)
            for (c0, n) in segs:
                mm(PB[:, c0:c0 + n], wxx_[:, 0, :], UB[:, c0:c0 + n], True, True, [kxx_, "UB"], pk(PB, c0, c0 + n), True)
            R, I, A, HS = T1, T2, T3, T0
            act(R[:, 0:NT], PA[:, 0:NT], AF.Sigmoid, pk(PA, 0, NT), [("T", 1)], bias=prm["lru_ba"][:, j:j + 1])
            act(I[:, 0:NT], PB[:, 0:NT], AF.Sigmoid, pk(PB, 0, NT), [("T", 2)], bias=prm["lru_bx"][:, j:j + 1])
            act(A[:, 0:NT], R[:, 0:NT], AF.Exp, [("T", 1), "LRUC"], [("T", 3)], scale=LRUC[:, j:j + 1])
            act(R[:, 0:NT], R[:, 0:NT], AF.Exp, [("T", 1), "LRUC2"], [("T", 1)], scale=LRUC2[:, j:j + 1])
            act(R[:, 0:NT], R[:, 0:NT], AF.Sqrt, [("T", 1), "ONEC"], [("T", 1)], bias=ONEC[:, 0:1], scale=-1.0)
            tt("pool", I[:, 0:NT], I[:, 0:NT], U[:, 0:NT], ALU.mult, [("T", 2), ("T", 0)], [("T", 2)])
            tt("dve", R[:, 0:NT], R[:, 0:NT], I[:, 0:NT], ALU.mult, [("T", 1), ("T", 2)], [("T", 1)])
            init = HINIT[:, j:j + 1] if full else 0.0
            kb.op("dve", lambda e, init=init: e.tensor_tensor_scan(out=HS[:, 0:NP1], data0=A[:, 0:NP1], data1=R[:, 0:NP1],
                                                                  initial=init, op0=ALU.mult, op1=ALU.add),
                  reads=[("T", 3), ("T", 1), "HINIT"], writes=[("T", 0)])
            if not full:
                cp("pool", HEND1[:, j:j + 1], HS[:, NP1 - 1:NP1], [("T", 0)], ["HEND1"])
                continue
            cp("pool", O_PLH[:, j:j + 1], HS[:, NP1 - 1:NP1], [("T", 0)], ["O_PLH"])
            for s in range(NSEQ):
                c0 = NP1 + 8 * s
                kb.op("dve", lambda e, c0=c0, s=s, j=j: e.tensor_tensor_scan(
                    out=HS[:, c0:c0 + 8], data0=A[:, c0:c0 + 8], data1=R[:, c0:c0 + 8],
                    initial=H0S_L[:, j, s:s + 1], op0=ALU.mult, op1=ALU.add),
                    reads=[("T", 3), ("T", 1), "H0S_L"], writes=[("T", 0)])
            cp("pool", O_SLH[:, j, :], HS[:, NP1:NT2].rearrange("p (s t) -> p s t", t=8)[:, :, 7], [("T", 0)], ["O_SLH"])
            tt("dve", MIX[:, j, 0:NT], HS[:, 0:NT], GL[:, 0:NT], ALU.mult, [("T", 0), "GL"], [("MIX", j)])
        kb.barrier()
        if not full:
            return
        rms_rstd(lambda c: MIX[:, c, :], lambda c: [("MIX", c)], 16, NT, segs)
        for c in range(16):
            stt("dve" if c % 2 == 0 else "pool", MIX[:, c, 0:NT], MIX[:, c, 0:NT], prm["lru_og"][:, c:c + 1], RSTD[:, 0:NT],
                ALU.mult, ALU.mult, [("MIX", c), ("T", 0)], [("MIX", c)])
        wout_half(0, NT, segs)
        kb.barrier()

    def ssd_phase(blk, NT, segs):
        full = blk == 2
        ntile = NT // 128
        scratch()
        ZS = sb("ZS", [128, NT2], BF16)
        EXTP = sb("EXTP", [128, 3 + NP1])
        EXTS = sb("EXTS", [128, NSEQ, 11])
        CT1 = sb("CT1", [128, NT2], BF16)
        BTOK1 = sb("BTOK1", [128, 9, 128], BF16)
        CBM1 = sb("CBM1", [128, 9, 128], BF16)
        DT_TOK = sb("DT_TOK", [128, 9, 32])
        DA_TOK = sb("DA_TOK", [128, 9, 32])
        CS_TOK = sb("CS_TOK", [128, 9, 32])
        CSL_BC = sb("CSL_BC", [128, 9, 32])
        TOEND = sb("TOEND", [128, 9, 32])
        DEC_BC = sb("DEC_BC", [128, 9, 32])
        XPAD = sb("XPAD", [128, 2, 128], BF16)
        XSC = sb("XSC", [128, 128], BF16)
        XSCM = sb("XSCM", [128, 2, 128], BF16)
        SM = [sb("SM%d" % i, [128, 128]) for i in range(8)]
        SMB = [sb("SMB%d" % i, [128, 128], BF16) for i in range(2)]
        HT = sb("HT", [128, 128])
        HTB = sb("HTB", [128, 128], BF16)
        H0N = sb("H0N", [128, 2, 128])
        H0T = sb("H0T", [128, 2, 128])
        H0TB = sb("H0TB", [128, 2, 128], BF16)
        H1N = sb("H1N", [128, 2, 128])
        DECF = sb("DECF", [128, NSEQ])
        csz = NT2 * 2
        DTFv, _ = view(OFF["MIX"] + 7 * csz, [NT2], F32)
        DAFv, _ = view(OFF["MIX"] + 9 * csz, [NT2], F32)
        BTs = [MIX[:, 11, :], MIX[:, 12, :]]
        CTs = [MIX[:, 13, :], CT1[:, :]]
        BTOKs = [MIX[:, 14, :].rearrange("p (t n) -> p t n", n=128), BTOK1]
        CBMs = [MIX[:, 15, :].rearrange("p (t n) -> p t n", n=128), CBM1]
        kDTF = [("MIX", 7), ("MIX", 8)]
        kDAF = [("MIX", 9), ("MIX", 10)]
        kBT = [[("MIX", 11)], [("MIX", 12)]]
        kCT = [[("MIX", 13)], ["CT1"]]
        kBTOK = [[("MIX", 14)], ["BTOK1"]]
        kCBM = [[("MIX", 15)], ["CBM1"]]
        ut, slt, uts, same, onesf, seqm, ident = (cst["c_ut"], cst["c_slt"], cst["c_uts"], cst["c_same"],
                                                  cst["c_ones"], cst["c_seq"], cst["c_ident"])
        XS = T0
        for ci in (16, 17, 18, 19):
            wplan("w_in", 0, 16, 6144 + ci * 128)
        if os.environ.get("DBG_NODT") is None:
            wplan("w_in", 0, 16, 8704, 32)
        for pr in range(16):
            wplan("w_in", 0, 16, 6144 + pr * 128)
            if full:
                wplan("w_in", 0, 16, 4096 + pr * 128)
        memset("pool", XPAD[:, :, :], 0.0, ["XPAD"])

        def do_conv(P, ci):
            conv_silu(P, prm["ssd_cw"], prm["ssd_cb"], ci, TAIL_S if full else None, "TAIL_S", ssc_d, "CONVS_S", NT, full,
                      XS, ("T", 0), None if full else TAIL_S, "TAIL_S", O_PSC if full else None, "O_PSC",
                      o_ssc if full else None, "O_SSC", True, EXTP, EXTS)

        for ci in (16, 17, 18, 19):
            w, k = wget()
            proj(w, k, 16, xn_rhs, xnkeys, PA, segs)
            do_conv(PA, ci)
            g = (ci - 16) % 2
            if ci < 18:
                cp("pool", BTs[g][:, 0:NT], XS[:, 0:NT], [("T", 0)], kBT[g])
                for t in range(ntile):
                    so, sk = sslot(t % 8)
                    kb.op("pe", lambda e, so=so, t=t: e.transpose(so, XS[:, t * 128:(t + 1) * 128], ident[:, :]),
                          reads=[("T", 0), "c_ident"], writes=[sk])
                    cp("act" if t % 2 == 0 else "dve", BTOKs[g][:, t, :], so, [sk], kBTOK[g])
            else:
                cp("pool", CTs[g][:, 0:NT], XS[:, 0:NT], [("T", 0)], kCT[g])
        SUB = int(os.environ.get("DBG_SUB", "99"))
        if SUB == 0:
            return
        w, k = wget()
        proj(w, k, 16, xn_rhs, xnkeys, PA, segs, M=32)
        if SUB == 1:
            return
        act(DTFv[0:32, 0:NT], PA[0:32, 0:NT], AF.Exp, pk(PA, 0, NT) + ["DTBA"], kDTF, bias=DTBA[0:32, 0:1])
        act(DTFv[0:32, 0:NT], DTFv[0:32, 0:NT], AF.Ln, kDTF + ["ONEC"], kDTF, bias=ONEC[0:32, 0:1])
        ts("dve", DAFv[0:32, 0:NT], DTFv[0:32, 0:NT], DTBA[0:32, 1:2], None, ALU.mult, None, kDTF + ["DTBA"], kDAF)
        if SUB == 2:
            return
        for t in range(ntile):
            so, sk = sslot(t % 8)
            kb.op("pe", lambda e, so=so, t=t: e.transpose(so[:, 0:32], DTFv[0:32, t * 128:(t + 1) * 128], ident[0:32, 0:32]),
                  reads=kDTF + ["c_ident"], writes=[sk], inc=False)
            kb.op("pe", lambda e, so=so, t=t: e.transpose(so[:, 32:64], DAFv[0:32, t * 128:(t + 1) * 128], ident[0:32, 0:32]),
                  reads=kDAF + ["c_ident"], writes=[sk])
            cp("act", DT_TOK[:, t, :], so[:, 0:32], [sk], ["DT_TOK"])
            cp("act", DA_TOK[:, t, :], so[:, 32:64], [sk], ["DA_TOK"])
        if SUB == 3:
            return
        for t in range(ntile):
            samp = full and t == 8
            so, sk = sslot(t % 8)
            mm(so[:, 0:32], (uts if samp else ut)[:, :], DA_TOK[:, t, :], True, True, ["DA_TOK", "c_ut", "c_uts"], [sk], False)
            mm(so[:, 32:64], (same if samp else onesf)[:, :], DA_TOK[:, t, :], True, True, ["DA_TOK", "c_same", "c_ones"], [sk], True)
            cp("act", CS_TOK[:, t, :], so[:, 0:32], [sk], ["CS_TOK"])
            cp("act", CSL_BC[:, t, :], so[:, 32:64], [sk], ["CSL_BC"])
        tt("dve", TOEND[:, 0:ntile, :], CSL_BC[:, 0:ntile, :], CS_TOK[:, 0:ntile, :], ALU.subtract, ["CSL_BC", "CS_TOK"], ["TOEND"])
        act(TOEND[:, 0:ntile, :], TOEND[:, 0:ntile, :], AF.Exp, ["TOEND"], ["TOEND"])
        tt("dve", TOEND[:, 0:ntile, :], TOEND[:, 0:ntile, :], DT_TOK[:, 0:ntile, :], ALU.mult, ["TOEND", "DT_TOK"], ["TOEND"])
        act(DEC_BC[:, 0:ntile, :], CSL_BC[:, 0:ntile, :], AF.Exp, ["CSL_BC"], ["DEC_BC"])
        if SUB == 4:
            return
        if full:
            for g in range(2):
                for t in range(ntile):
                    samp = t == 8
                    so, sk = sslot((g * ntile + t) % 8)
                    mm(so, BTs[g][:, t * 128:(t + 1) * 128], CTs[g][:, t * 128:(t + 1) * 128], True, True,
                       kBT[g] + kCT[g], [sk], True)
                    tt("dve", CBMs[g][:, t, :], so, (uts if samp else ut)[:, :], ALU.mult, [sk, "c_ut", "c_uts"], kCBM[g])
        if SUB == 5:
            return
        for pr in range(16):
            if SUB in (6, 7) and pr == 1:
                return
            g = pr // 8
            h0 = 2 * pr
            w, k = wget()
            proj(w, k, 16, xn_rhs, xnkeys, PA, segs)
            do_conv(PA, pr)
            if full:
                w, k = wget()
                proj(w, k, 16, xn_rhs, xnkeys, PB, segs)
                act(ZS[:, 0:NT], PB[:, 0:NT], AF.Silu, pk(PB, 0, NT), ["ZS"])
                if os.environ.get("DBG_NOHSP"):
                    memset("dve", HT[:, :], 0.0, ["HT"])
                else:
                    kb.dma("sp", HT[:, :], hspill[:, pr, :], "ld_hsp", reads=["hspill"], writes=["HT"])
                    ts("dve", HT[:, :], HT[:, :], FLAG[:, 0:1], None, ALU.mult, None, ["HT", "FLAG"], ["HT"])
            else:
                memset("dve", HT[:, :], 0.0, ["HT"])
            for t in range(ntile):
                samp = full and t == 8
                if samp and SUB == 6:
                    continue
                U_ = uts if samp else ut
                cols = slice(t * 128, (t + 1) * 128)
                sx, kx = sslot(6 + t % 2)
                kb.op("pe", lambda e, sx=sx, t=t: e.transpose(sx, XS[:, t * 128:(t + 1) * 128], ident[:, :]),
                      reads=[("T", 0), "c_ident"], writes=[kx])
                if full and not os.environ.get("DBG_NOXPAD"):
                    cp("act", XPAD[:, 0, 0:64], sx[:, 0:64], [kx], ["XPAD"])
                    cp("act", XPAD[:, 1, 64:128], sx[:, 64:128], [kx], ["XPAD"])
                for hh_ in range(2):
                    act(XSC[:, hh_ * 64:(hh_ + 1) * 64], sx[:, hh_ * 64:(hh_ + 1) * 64], AF.Copy, [kx, "TOEND"], ["XSC"],
                        scale=TOEND[:, t, h0 + hh_:h0 + hh_ + 1])
                if SUB == 61:
                    return
                if full:
                    DABC, ECS = SM[0], SM[1]
                    cp("dve", DABC[:, :].rearrange("p (h q) -> p h q", q=64),
                       DA_TOK[:, t, h0:h0 + 2].unsqueeze(2).to_broadcast([128, 2, 64]), ["DA_TOK"], [("SM", 0)])
                    s0, k0 = sslot(0)
                    mm(s0, DABC[:, :], U_[:, :], True, True, [("SM", 0), "c_ut", "c_uts"], [k0], True)
                    act(ECS[:, :], s0, AF.Exp, [k0], [("SM", 1)])
                    if SUB == 62:
                        return
                    sy, ky = sslot(3)
                    for hh in range(2):
                        LM, EH, WDT = SM[2 + hh], SM[4 + hh], SMB[hh]
                        ts("pool", LM[:, :], slt[:, :], DA_TOK[:, t, h0 + hh:h0 + hh + 1], None, ALU.mult, None,
                           ["c_slt", "DA_TOK"], [("SM", 2 + hh)])
                        sd, kd = sslot(1 + hh)
                        mm(sd, LM[:, :], U_[:, :], True, True, [("SM", 2 + hh), "c_ut", "c_uts"], [kd], True)
                        act(EH[:, :], sd, AF.Exp, [kd], [("SM", 4 + hh)])
                        stt("dve", WDT[:, :], EH[:, :], DT_TOK[:, t, h0 + hh:h0 + hh + 1], CBMs[g][:, t, :], ALU.mult, ALU.mult,
                            [("SM", 4 + hh), "DT_TOK"] + kCBM[g], [("SMB", hh)])
                        mm(sy, XPAD[:, hh, :], WDT[:, :], hh == 0, hh == 1, ["XPAD", ("SMB", hh)], [ky], hh == 1)
                    if SUB == 63:
                        return
                    sr, kr = sslot(4)
                    if not samp:
                        cp("act", HTB[:, :], HT[:, :], ["HT"], ["HTB"])
                        mm(sr, HTB[:, :], CTs[g][:, cols], True, True, ["HTB"] + kCT[g], [kr], True)
                if SUB == 64:
                    return
                if not samp:
                    sh, kh = sslot(5)
                    mm(sh, BTOKs[g][:, t, :], XSC[:, :], True, True, kBTOK[g] + ["XSC"], [kh], True)
                    tt("dve", HT[:, :].rearrange("p (h q) -> p h q", q=64), HT[:, :].rearrange("p (h q) -> p h q", q=64),
                       DEC_BC[:, t, h0:h0 + 2].unsqueeze(2).to_broadcast([128, 2, 64]), ALU.mult, ["HT", "DEC_BC"], ["HT"])
                    tt("dve", HT[:, :], HT[:, :], sh, ALU.add, ["HT", kh], ["HT"])
                else:
                    s5, k5 = sslot(5)
                    mm(s5[:, 0:NSEQ], DABC[:, :], seqm[:, :], True, True, [("SM", 0), "c_seq"], [k5], True)
                    act(DECF[:, :], s5[:, 0:NSEQ], AF.Exp, [k5], ["DECF"])
                    for q in range(8):
                        kb.dma("sp", H0T[:, :, :], sshT_d[pr][:, 2 * q:2 * q + 2, :], "ld_h0t", writes=["H0T"])
                        kb.dma("sp", H0N[:, :, :], sshN_d[pr][:, 2 * q:2 * q + 2, :], "ld_h0n", writes=["H0N"])
                        cp("act", H0TB[:, :, :], H0T[:, :, :], ["H0T"], ["H0TB"])
                        for u in range(2):
                            s = 2 * q + u
                            mm(sr[:, 8 * s:8 * s + 8], H0TB[:, u, :], CTs[g][:, NP1 + 8 * s:NP1 + 8 * s + 8], True, True,
                               ["H0TB"] + kCT[g], [kr], s == NSEQ - 1 or u == 1)
                        tt("dve", XSCM[:, :, :], XSC[:, :].unsqueeze(1).to_broadcast([128, 2, 128]),
                           seqm[:, 2 * q:2 * q + 2].unsqueeze(2).to_broadcast([128, 2, 128]), ALU.mult, ["XSC", "c_seq"], ["XSCM"])
                        bo = (q % 2) * 512
                        for u in range(2):
                            mm(PA[:, bo + u * 128:bo + (u + 1) * 128], XSCM[:, u, :], BTOKs[g][:, 8, :], True, True,
                               ["XSCM"] + kBTOK[g], pk(PA, bo, bo + 512), u == 1)
                        for u in range(2):
                            s = 2 * q + u
                            stt("dve", H1N[:, u, :], H0N[:, u, :], DECF[:, s:s + 1], PA[:, bo + u * 128:bo + (u + 1) * 128],
                                ALU.mult, ALU.add, ["H0N", "DECF"] + pk(PA, bo, bo + 512), ["H1N"])
                        kb.dma("sp", o_ssh[pr][:, 2 * q:2 * q + 2, :], H1N[:, :, :], "st_h1", reads=["H1N"])
                    outkeys.append("H1N")
                if full:
                    T1_, Y2 = SM[6], SM[7]
                    tt("dve", T1_[:, :], sr, ECS[:, :], ALU.mult, [kr, ("SM", 1)], [("SM", 6)])
                    stt("dve", Y2[:, :], XS[:, cols], prm["ssd_dfm"][:, pr:pr + 1], sy, ALU.mult, ALU.add,
                        [("T", 0), ky], [("SM", 7)])
                    tt("pool", Y2[:, :], Y2[:, :], T1_[:, :], ALU.add, [("SM", 7), ("SM", 6)], [("SM", 7)])
                    tt("pool", MIX[:, pr, cols], Y2[:, :], ZS[:, cols], ALU.mult, [("SM", 7), "ZS"], [("MIX", pr)])
            if full:
                kb.dma("sp", o_pshT[:, pr, :], HT[:, :], "st_psh", reads=["HT"])
                outkeys.append("HT")
            else:
                kb.dma("sp", hspill[:, pr, :], HT[:, :], "st_hsp", reads=["HT"], writes=["hspill"])
        kb.barrier()
        if not full:
            return
        for g in range(2):
            rms_rstd(lambda c, g=g: MIX[:, g * 8 + c, :], lambda c, g=g: [("MIX", g * 8 + c)], 8, NT, segs, scale=1.0 / 1024)
            for c in range(8 * g, 8 * g + 8):
                stt("dve" if c % 2 == 0 else "pool", MIX[:, c, 0:NT], MIX[:, c, 0:NT], prm["ssd_og"][:, c:c + 1], RSTD[:, 0:NT],
                    ALU.mult, ALU.mult, [("MIX", c), ("T", 0)], [("MIX", c)])
        wout_half(2048, NT, segs)
        kb.barrier()

    def xattn(NT, segs):
        norm_to_xn("xattn_g", NT, segs)
        for dc in range(16):
            wplan("w_q", 0, 16, dc * 128)
        for nm in ("w_k", "w_v"):
            for dc in range(16):
                wplan(nm, 0, 16, dc * 128)
        ident = cst["c_ident"]
        for dc in range(16):
            w, k = wget()
            P = PA if dc % 2 == 0 else PB
            proj(w, k, 16, xn_rhs, xnkeys, P, segs)
            cp("act" if dc % 2 == 0 else "dve", MIX[:, dc, 0:NT], P[:, 0:NT], pk(P, 0, NT), [("MIX", dc)])
        kb.barrier()
        scratch(OFF["XN"])
        KT = sb("KT", [128, 16, 256], BF16)
        VTOK = sb("VTOK", [128, 2, D], BF16)
        MEMN = sb("MEMN", [128, 16, 256], BF16)
        ET = sb("ET", [128, 2, NT2], BF16)
        assert bump["p"] <= OFF["XN"] + 16 * NT2 * 2
        scratch()
        MRS = sb("MRS", [128, 256])
        MEMX = [sb("MEMX%d" % i, [128, 256]) for i in range(2)]
        STG = [sb("STG%d" % i, [128, 256]) for i in range(2)]
        KTS = [sb("KTS%d" % i, [128, 4, 256], BF16) for i in range(2)]
        VS = [sb("VS%d" % i, [128, 2, 512], BF16) for i in range(2)]
        for c in range(16):
            kb.dma("sp", MEMX[c % 2][:, :], mem_d[:, c, :], ("ld_mem", c % 2), writes=[("MEMX", c % 2)])
            act(SQ[c % 2][:, 0:256], MEMX[c % 2][:, :], AF.Square, [("MEMX", c % 2)], [("SQ", c % 2)])
            mm(PA[:, 0:256], ones_bf[:, :], SQ[c % 2][:, 0:256], c == 0, c == 15, [("SQ", c % 2), "c_ones_bf"], pk(PA, 0, 256), True)
        act(MRS[:, :], PA[:, 0:256], AF.Sqrt, pk(PA, 0, 256) + ["EPSC"], ["MRS"], bias=EPSC[:, 0:1], scale=1.0 / D)
        kb.op("dve", lambda e: e.reciprocal(out=MRS[:, :], in_=MRS[:, :]), reads=["MRS"], writes=["MRS"])
        for c in range(16):
            kb.dma("sp", MEMX[c % 2][:, :], mem_d[:, c, :], ("ld_mem", c % 2), writes=[("MEMX", c % 2)])
            stt("dve", MEMN[:, c, :], MEMX[c % 2][:, :], prm["mem_g"][:, c:c + 1], MRS[:, :], ALU.mult, ALU.mult,
                [("MEMX", c % 2), "MRS"], ["MEMN"])
        msegs = [(0, 256)]
        si = 0
        for nm in ("w_k", "w_v"):
            for dc in range(16):
                w, k = wget()
                P = PA if dc % 2 == 0 else PB
                proj(w, k, 16, lambda kc, c0, n: MEMN[:, kc, c0:c0 + n], ["MEMN"], P, msegs)
                st = STG[si % 2]
                sk_ = ("STG", si % 2)
                si += 1
                cp("act", st[:, :], P[:, 0:256], pk(P, 0, 256), [sk_])
                if nm == "w_k":
                    cp("dve", KT[:, dc, :], st[:, :], [sk_], ["KT"])
                    kb.dma("sp", o_mk[:, dc, :], st[:, :], ("st_kv", (si - 1) % 2), reads=[sk_])
                else:
                    kb.dma("sp", o_mv[:, dc, :], st[:, :], ("st_kv", (si - 1) % 2), reads=[sk_])
                    for mt in range(2):
                        so, sk = sslot((dc * 2 + mt) % 8)
                        kb.op("pe", lambda e, so=so, st=st, mt=mt: e.transpose(so, st[:, mt * 128:(mt + 1) * 128], ident[:, :]),
                              reads=[sk_, "c_ident"], writes=[sk])
                        cp("act" if mt == 0 else "pool" if False else "dve", VTOK[:, mt, dc * 128:(dc + 1) * 128], so, [sk], ["VTOK"])
        outkeys.extend([("STG", 0), ("STG", 1)])
        sc = 512.0 ** -0.5
        psegs = [(0, 512), (512, 512)]
        li = 0
        for h in range(4):
            for mt in range(2):
                P = PA if mt == 0 else PB
                for (c0, n) in psegs:
                    for kc in range(4):
                        mm(P[:, c0:c0 + n], KT[:, 4 * h + kc, mt * 128:(mt + 1) * 128], MIX[:, 4 * h + kc, c0:c0 + n],
                           kc == 0, kc == 3, ["KT", ("MIX", 4 * h + kc)], pk(P, c0, c0 + n), kc == 3)
            for s in range(NSEQ):
                b = li % 2
                li += 1
                kb.dma("pool", KTS[b][:, :, :], ckT_d[s][:, 4 * h:4 * h + 4, :], ("ld_ck", b), writes=[("KTS", b)])
                for mt in range(2):
                    P = PA if mt == 0 else PB
                    for kc in range(4):
                        mm(P[:, 1024 + 8 * s:1024 + 8 * s + 8], KTS[b][:, kc, mt * 128:(mt + 1) * 128],
                           MIX[:, 4 * h + kc, NP1 + 8 * s:NP1 + 8 * s + 8], kc == 0, kc == 3,
                           [("KTS", b), ("MIX", 4 * h + kc)], pk(P, 1024, 1152), kc == 3)
            for mt in range(2):
                P = PA if mt == 0 else PB
                act(ET[:, mt, 0:NT], P[:, 0:NT], AF.Exp, pk(P, 0, NT), [("ET", mt)], scale=sc)
            for (c0, n) in segs:
                for mt in range(2):
                    mm(PA[:, c0:c0 + n], ones_bf[:, :], ET[:, mt, c0:c0 + n], mt == 0, mt == 1,
                       [("ET", mt), "c_ones_bf"], pk(PA, c0, c0 + n), mt == 1)
            kb.op("dve", lambda e: e.reciprocal(out=RSTD[:, 0:NT], in_=PA[:, 0:NT]), reads=pk(PA, 0, NT), writes=[("T", 0)])
            for s in range(NSEQ):
                b = s % 2
                kb.dma("pool", VS[b][:, :, :], cv_d[s][:, :, h * 512:(h + 1) * 512], ("ld_cv", b), writes=[("VS", b)])
                for dcl in range(4):
                    for mt in range(2):
                        mm(PB[:, 1024 + dcl * 128 + 8 * s:1024 + dcl * 128 + 8 * s + 8], VS[b][:, mt, dcl * 128:(dcl + 1) * 128],
                           ET[:, mt, NP1 + 8 * s:NP1 + 8 * s + 8], mt == 0, mt == 1, [("VS", b), ("ET", mt)],
                           pk(PB, 1024, 1536), mt == 1)
            for dcl in range(4):
                dc = 4 * h + dcl
                tt("dve", MIX[:, dc, NP1:NT2], PB[:, 1024 + dcl * 128:1024 + dcl * 128 + 128], RSTD[:, NP1:NT2], ALU.mult,
                   pk(PB, 1024, 1536) + [("T", 0)], [("MIX", dc)])
            for dcl in range(4):
                dc = 4 * h + dcl
                for (c0, n) in psegs:
                    for mt in range(2):
                        mm(PB[:, c0:c0 + n], VTOK[:, mt, dc * 128:(dc + 1) * 128], ET[:, mt, c0:c0 + n], mt == 0, mt == 1,
                           ["VTOK", ("ET", mt)], pk(PB, c0, c0 + n), mt == 1)
                tt("dve", MIX[:, dc, 0:NP1], PB[:, 0:NP1], RSTD[:, 0:NP1], ALU.mult, pk(PB, 0, NP1) + [("T", 0)], [("MIX", dc)])
        kb.barrier()
        wout_half(0, NT, segs, wname="w_o")
        kb.barrier()

    segs1 = [(0, 512), (512, 512)]
    segs2 = [(0, 512), (512, 512), (1024, 128)]

    def finish(dump=None):
        if dump is not None:
            kb.barrier()
            ncd = NP1 if stage < 4 else NT2
            for c in range(16):
                kb.dma("sp", y_d[:, c, 0:ncd], X[:, c, 0:ncd], "st_dbg", reads=xkeys(c))
            outkeys.extend(xkeys())
            if dump == "xn" or dump == "mix":
                src = XN if dump == "xn" else MIX
                for c in range(16):
                    kb.dma("pool", dbg_d[:, c, 0:ncd], src[:, c, 0:ncd], "st_dbg2", reads=[(dump.upper(), c)])
                outkeys.extend([(dump.upper(), c) for c in range(16)])
        olist = ((o_plc, O_PLC, "O_PLC"), (o_plh, O_PLH, "O_PLH"), (o_psc, O_PSC, "O_PSC"), (o_slh, O_SLH, "O_SLH"))
        if stage in (2, 3):
            olist = ((o_plc, TAIL_L, "TAIL_L"), (o_plh, HEND1, "HEND1"), (o_psc, TAIL_S, "TAIL_S"))
            if stage == 2:
                olist = olist[0:2]
            if stage == 3 and os.environ.get("DBG_SUB") is None:
                kb.dma("sp", o_pshT, hspill, "st_misc", reads=["hspill"])
        for (dst, src, key) in olist:
            kb.dma("sp", dst, src, "st_misc", reads=[key])
            outkeys.append(key)
        outkeys.append("EXTS")
        kb.final_wait("sp", outkeys)
        kb.emit()
        return nc, kb

    dbg_d = dout("dbg", [128, 16, NT2]) if stage < 99 else None
    for c in range(16):
        kb.dma("sp", X[:, c, 0:NP1], xa_d[:, c, :], "ld_x", writes=xkeys(c))
    if stage == 0:
        return finish("x")
    SKIP = os.environ.get("DBG_SKIP") is not None
    if not SKIP:
        ffn("ffn1", "ffn1_g", NP1, segs1)
    if stage == 1:
        return finish("xn")
    norm_to_xn("mix_g", NP1, segs1)
    if not SKIP or os.environ.get("DBG_SKIP") == "f":
        lru_phase(1, NP1, segs1)
    if stage == 2:
        return finish("xn")
    ssd_phase(1, NP1, segs1)
    if stage == 3:
        return finish("xn")
    for c in range(16):
        kb.dma("sp", X[:, c, :], xb_d[:, c, :], "ld_x", writes=xkeys(c))
    ffn("ffn1", "ffn1_g", NT2, segs2)
    if stage == 4:
        return finish("xn")
    norm_to_xn("mix_g", NT2, segs2)
    lru_phase(2, NT2, segs2)
    if stage == 5:
        return finish("mix")
    ssd_phase(2, NT2, segs2)
    if stage == 6:
        return finish("mix")
    xattn(NT2, segs2)
    if stage == 7:
        return finish("mix")
    ffn("ffn2", "ffn2_g", NT2, segs2)
    scratch()
    OST = [sb("OST%d" % i, [128, NT2]) for i in range(3)]
    rms_rstd(lambda c: X[:, c, :], lambda c: xkeys(c), 16, NT2, segs2)
    for c in range(16):
        o = OST[c % 3]
        stt("dve", o[:, :], X[:, c, :], prm["final_g"][:, c:c + 1], RSTD[:, :], ALU.mult, ALU.mult,
            xkeys(c) + [("T", 0)], [("OST", c % 3)])
        kb.dma("sp", y_d[:, c, :], o[:, :], ("st_y", c % 3), reads=[("OST", c % 3)])
    outkeys.extend([("OST", i) for i in range(3)])
    return finish(None)


_CACHE = {}


def _fm(v, shape_tail=()):
    v = np.asarray(v, np.float32)
    C = v.shape[0] // 128
    return np.ascontiguousarray(np.moveaxis(v.reshape((C, 128) + v.shape[1:]), 0, 1))


def _tok_fm(x):
    T_ = x.shape[0]
    return np.ascontiguousarray(x.reshape(T_, 16, 128).transpose(2, 1, 0))


def _fm_tok(y):
    return np.ascontiguousarray(y.transpose(2, 1, 0).reshape(y.shape[2], -1))


def make_in_maps(inp):
    f = lambda k: np.asarray(inp[k], np.float32)

    x_prompt, mem_prompt, x_sample = f("x_prompt"), f("mem_prompt"), f("x_sample")
    ck, cv = f("cache_mem_k")[0], f("cache_mem_v")[0]
    slc, slh, ssc, ssh = f("state_lru_conv")[0], f("state_lru_h")[0], f("state_ssd_conv")[0], f("state_ssd_h")[0]

    shared = {
        "ffn1_wg": f("ffn1_w_gate")[0], "ffn1_wu": f("ffn1_w_up")[0], "ffn1_wd": f("ffn1_w_down")[0],
        "ffn2_wg": f("ffn2_w_gate")[0], "ffn2_wu": f("ffn2_w_up")[0], "ffn2_wd": f("ffn2_w_down")[0],
        "w_in": f("w_in")[0], "w_out": f("w_out")[0],
        "w_q": f("xattn_w_q")[0], "w_k": f("xattn_w_k")[0], "w_v": f("xattn_w_v")[0], "w_o": f("xattn_w_o")[0],
        "lru_wa": f("lru_w_a")[0].reshape(D, 128), "lru_wx": f("lru_w_x")[0].reshape(D, 128),
        "ffn1_g": _fm(f("ffn1_norm_g")[0]), "mix_g": _fm(f("mix_norm_g")[0]), "xattn_g": _fm(f("xattn_norm_g")[0]),
        "mem_g": _fm(f("mem_norm_g")[0]), "ffn2_g": _fm(f("ffn2_norm_g")[0]), "final_g": _fm(f("final_norm_g")),
        "lru_cw": _fm(f("lru_conv_w")[0].T), "lru_cb": _fm(f("lru_conv_b")[0]), "lru_ba": _fm(f("lru_b_a")[0]),
        "lru_bx": _fm(f("lru_b_x")[0]), "lru_lam": _fm(f("lru_lambda")[0]), "lru_og": _fm(f("lru_out_norm_g")[0]),
        "ssd_cw": _fm(f("ssd_conv_w")[0].T), "ssd_cb": _fm(f("ssd_conv_b")[0]), "ssd_og": _fm(f("ssd_out_norm_g")[0]),
        "ssd_dfm": _fm(np.repeat(f("ssd_d")[0], 64)),
        "dtb": f("ssd_dt_bias")[0].reshape(32, 1).copy(), "alog": f("ssd_a_log")[0].reshape(32, 1).copy(),
    }
    j = np.arange(128)
    seq_of = j // 8
    shared["c_ident"] = np.eye(128, dtype=np.float32)
    shared["c_ut"] = (j[:, None] <= j[None, :]).astype(np.float32)
    shared["c_slt"] = (j[:, None] > j[None, :]).astype(np.float32)
    shared["c_same"] = (seq_of[:, None] == seq_of[None, :]).astype(np.float32)
    shared["c_uts"] = shared["c_ut"] * shared["c_same"]
    shared["c_ones"] = np.ones((128, 128), np.float32)
    shared["c_seq"] = (seq_of[:, None] == np.arange(16)[None, :]).astype(np.float32)
    shared = {k: np.ascontiguousarray(v, dtype=np.float32) for k, v in shared.items()}

    in_maps = []
    for c in range(8):
        b, half = c // 2, c % 2
        m = dict(shared)
        m["xa"] = _tok_fm(x_prompt[b, 0:NP1])
        xs = x_sample[16 * c:16 * c + 16].reshape(NS, D)
        m["xb"] = _tok_fm(np.concatenate([x_prompt[b, half * NP1:(half + 1) * NP1], xs], 0))
        m["flag"] = np.full((128, 1), float(half), np.float32)
        m["memT"] = _tok_fm(mem_prompt[b])
        sl = slice(16 * c, 16 * c + 16)
        kk = ck[sl].reshape(NSEQ, 256, 16, 128)
        m["ckT"] = np.ascontiguousarray(kk.transpose(0, 3, 2, 1))
        vv = cv[sl].reshape(NSEQ, 2, 128, D)
        m["cv"] = np.ascontiguousarray(vv.transpose(0, 2, 1, 3))
        m["slc"] = np.ascontiguousarray(slc[sl].reshape(NSEQ, 3, 16, 128).transpose(3, 2, 0, 1))
        m["slh"] = np.ascontiguousarray(slh[sl].reshape(NSEQ, 16, 128).transpose(2, 1, 0))
        m["ssc"] = np.ascontiguousarray(ssc[sl].reshape(NSEQ, 3, 20, 128).transpose(3, 2, 0, 1))
        hh = ssh[sl].reshape(NSEQ, 16, 128, 128)
        m["sshN"] = np.ascontiguousarray(hh.transpose(1, 2, 0, 3))
        m["sshT"] = np.ascontiguousarray(hh.transpose(1, 3, 0, 2))
        in_maps.append(m)

    return in_maps


def kernel(**inp):
    if "nc" not in _CACHE:
        _CACHE["nc"] = build_program()
    nc, kb = _CACHE["nc"]
    in_maps = make_in_maps(inp)
    res = run_bass_kernel_spmd(nc, in_maps, core_ids=list(range(8)))
    R = res.results

    y_prompt = np.zeros((4, 2048, D), np.float32)
    y_sample = np.zeros((128, 8, D), np.float32)
    p_lc = np.zeros((1, 4, 3, D), np.float32)
    p_lh = np.zeros((1, 4, D), np.float32)
    p_sc = np.zeros((1, 4, 3, 2560), np.float32)
    p_sh = np.zeros((1, 4, 32, 64, 128), np.float32)
    p_mk = np.zeros((1, 4, 256, 4, 512), np.float32)
    p_mv = np.zeros((1, 4, 256, 4, 512), np.float32)
    s_lc = np.zeros((1, 128, 3, D), np.float32)
    s_lh = np.zeros((1, 128, D), np.float32)
    s_sc = np.zeros((1, 128, 3, 2560), np.float32)
    s_sh = np.zeros((1, 128, 32, 64, 128), np.float32)
    for c in range(8):
        b, half = c // 2, c % 2
        r = R[c]
        yt = _fm_tok(r["y"])
        y_prompt[b, half * NP1:(half + 1) * NP1] = yt[0:NP1]
        y_sample[16 * c:16 * c + 16] = yt[NP1:].reshape(16, 8, D)
        sl = slice(16 * c, 16 * c + 16)
        s_lc[0, sl] = r["o_slc"].transpose(2, 3, 1, 0).reshape(16, 3, D)
        s_lh[0, sl] = r["o_slh"].transpose(2, 1, 0).reshape(16, D)
        s_sc[0, sl] = r["o_ssc"].transpose(2, 3, 1, 0).reshape(16, 3, 2560)
        s_sh[0, sl] = r["o_ssh"].transpose(2, 0, 1, 3).reshape(16, 32, 64, 128)
        if half == 1:
            p_lc[0, b] = r["o_plc"].transpose(2, 1, 0).reshape(3, D)
            p_lh[0, b] = r["o_plh"].T.reshape(D)
            p_sc[0, b] = r["o_psc"].transpose(2, 1, 0).reshape(3, 2560)
            p_sh[0, b] = r["o_pshT"].transpose(1, 2, 0).reshape(32, 64, 128)
        else:
            p_mk[0, b] = _fm_tok(r["o_mkT"]).reshape(256, 4, 512)
            p_mv[0, b] = _fm_tok(r["o_mvT"]).reshape(256, 4, 512)
    return (y_prompt, y_sample, p_lc, p_lh, p_sc, p_sh, p_mk, p_mv, s_lc, s_lh, s_sc, s_sh)
```

```python
import os
import numpy as np
import concourse.bass as bass
import concourse.mybir as mybir
from concourse.bass_utils import run_bass_kernel_spmd

F32 = mybir.dt.float32
BF16 = mybir.dt.bfloat16
AF = mybir.ActivationFunctionType
ALU = mybir.AluOpType

D = 2048
FF = 5632
NHID = FF // 128
HG = 4
HPG = NHID // HG
DIN = 8736
NP1 = 1024
NS = 128
NT2 = NP1 + NS
NSEQ = 16
EPS = 1e-6


class KB:
    ENG = ("pe", "act", "dve", "pool", "sp")

    def __init__(self, nc):
        self.nc = nc
        self.ops = {e: [] for e in self.ENG}
        self.sem = {e: nc.alloc_semaphore("s_" + e) for e in ("pe", "act", "dve", "pool")}
        self.cnt = {e: 0 for e in ("pe", "act", "dve", "pool")}
        self.waited = {e: {} for e in self.ENG}
        self.lastw = {}
        self.readers = {}
        self.dsem = {}
        self.dcnt = {}
        self.n_inst = 0

    def _deps(self, reads, writes):
        deps = {}

        def add(tok):
            if tok is None:
                return
            s, v = tok
            if s in self.dcnt:
                v = self.dcnt[s]
            if deps.get(s, 0) < v:
                deps[s] = v

        for k in reads:
            add(self.lastw.get(k))
        for k in writes:
            add(self.lastw.get(k))
            for tok in self.readers.get(k, {}).items():
                add(tok)
        return deps

    def _commit(self, tok, reads, writes):
        for k in reads:
            r = self.readers.setdefault(k, {})
            if r.get(tok[0], 0) < tok[1]:
                r[tok[0]] = tok[1]
        for k in writes:
            self.lastw[k] = tok
            self.readers[k] = {}

    def _waits(self, eng, deps):
        ws = []
        for s, v in deps.items():
            if self.waited[eng].get(s, 0) >= v:
                continue
            self.waited[eng][s] = v
            ws.append((s, v))
        return ws

    def _semh(self, s):
        return self.sem[s] if s in self.sem else self.dsem[s]

    def op(self, eng, fn, reads=(), writes=(), inc=True):
        inc = True
        deps = self._deps(reads, writes)
        ws = self._waits(eng, deps)
        if inc:
            self.cnt[eng] += 1
            tok = (eng, self.cnt[eng])
        else:
            tok = (eng, self.cnt[eng] + 1)
        self.ops[eng].append((ws, fn, (self.sem[eng], 1) if inc else None))
        self._commit(tok, reads, writes)
        self.n_inst += 1

    def dma(self, q, out, in_, slot, reads=(), writes=()):
        if slot not in self.dsem:
            self.dsem[slot] = self.nc.alloc_semaphore("d%d" % len(self.dsem))
            self.dcnt[slot] = 0
        deps = self._deps(reads, writes)
        ws = self._waits(q, deps)
        self.dcnt[slot] += 16
        tok = (slot, self.dcnt[slot])
        self.ops[q].append((ws, lambda e, o=out, i=in_: e.dma_start(out=o, in_=i), (self.dsem[slot], 16)))
        self._commit(tok, reads, writes)
        self.n_inst += 1

    def barrier(self):
        toks = {e: v for e, v in self.cnt.items() if v > 0}
        toks.update({s: v for s, v in self.dcnt.items() if v > 0})
        for e in self.ENG:
            ws = self._waits(e, dict(toks))
            if ws:
                self.ops[e].append((ws, None, None))

    def final_wait(self, eng, keys):
        deps = self._deps(keys, keys)
        ws = self._waits(eng, deps)
        self.ops[eng].append((ws, None, None))

    def emit(self):
        nc = self.nc
        hmap = {"pe": "tensor", "act": "scalar", "dve": "vector", "pool": "gpsimd", "sp": "sync"}
        with nc.Block() as block:
            for e in self.ENG:
                ops = self.ops[e]

                def body(eng, ops=ops):
                    for ws, fn, inc in ops:
                        for s, v in ws:
                            eng.wait_ge(self._semh(s), v)
                        if fn is None:
                            continue
                        ins = fn(eng)
                        if inc is not None:
                            ins.then_inc(inc[0], inc[1])

                getattr(block, hmap[e])(body)


W2D = {
    "ffn1_wg": (D, FF), "ffn1_wu": (D, FF), "ffn1_wd": (FF, D),
    "ffn2_wg": (D, FF), "ffn2_wu": (D, FF), "ffn2_wd": (FF, D),
    "w_in": (D, DIN), "w_out": (2 * D, D),
    "w_q": (D, D), "w_k": (D, D), "w_v": (D, D), "w_o": (D, D),
    "lru_wa": (D, 128), "lru_wx": (D, 128),
}
PFM = {
    "ffn1_g": (16,), "mix_g": (16,), "xattn_g": (16,), "mem_g": (16,), "ffn2_g": (16,), "final_g": (16,),
    "lru_cw": (16, 4), "lru_cb": (16,), "lru_ba": (16,), "lru_bx": (16,), "lru_lam": (16,), "lru_og": (16,),
    "ssd_cw": (20, 4), "ssd_cb": (20,), "ssd_og": (16,), "ssd_dfm": (16,),
}
CONSTS = {"c_ident": (128,), "c_ut": (128,), "c_slt": (128,), "c_uts": (128,), "c_same": (128,),
          "c_ones": (128,), "c_seq": (16,)}


def build_program(stage=99):
    nc = bass.Bass("TRN2", target_bir_lowering=False)
    kb = KB(nc)

    kb.inputs = []

    def din(name, shape):
        kb.inputs.append(name)
        return nc.dram_tensor(name, list(shape), F32, kind="ExternalInput").ap()

    def dout(name, shape):
        return nc.dram_tensor(name, list(shape), F32, kind="ExternalOutput").ap()

    def sb(name, shape, dt=F32):
        return nc.alloc_sbuf_tensor(name, list(shape), dt)

    xa_d = din("xa", [128, 16, NP1])
    xb_d = din("xb", [128, 16, NT2])
    flag_d = din("flag", [128, 1])
    mem_d = din("memT", [128, 16, 256])
    ckT_d = din("ckT", [NSEQ, 128, 16, 256]) if stage >= 7 else None
    cv_d = din("cv", [NSEQ, 128, 2, D]) if stage >= 7 else None
    slc_d = din("slc", [128, 16, NSEQ, 3])
    slh_d = din("slh", [128, 16, NSEQ])
    ssc_d = din("ssc", [128, 20, NSEQ, 3])
    sshN_d = din("sshN", [16, 128, NSEQ, 128]) if stage >= 6 else None
    sshT_d = din("sshT", [16, 128, NSEQ, 128]) if stage >= 6 else None
    dtb_d = din("dtb", [32, 1])
    alog_d = din("alog", [32, 1])
    class _LazyW(dict):
        def __missing__(self, k):
            self[k] = din(k, W2D[k])
            return self[k]

    Wd = _LazyW()
    if stage >= 99:
        for k in W2D:
            Wd[k]
    Pd = {k: din(k, (128,) + v) for k, v in PFM.items()}
    Cd = {k: din(k, (128,) + v) for k, v in CONSTS.items()}

    y_d = dout("y", [128, 16, NT2])
    o_plc = dout("o_plc", [128, 16, 3])
    o_plh = dout("o_plh", [128, 16])
    o_psc = dout("o_psc", [128, 20, 3])
    o_pshT = dout("o_pshT", [128, 16, 128])
    o_mk = dout("o_mkT", [128, 16, 256])
    o_mv = dout("o_mvT", [128, 16, 256])
    o_slc = dout("o_slc", [128, 16, NSEQ, 3])
    o_slh = dout("o_slh", [128, 16, NSEQ])
    o_ssc = dout("o_ssc", [128, 20, NSEQ, 3])
    o_ssh = dout("o_ssh", [16, 128, NSEQ, 128])
    outkeys = []

    ARENA_BYTES = 212000
    ARENA = nc.alloc_sbuf_tensor("arena", [128, ARENA_BYTES // 4], F32)
    bump = {"p": 0}
    OFF = {}

    def view(off, shape, dt):
        n = int(np.prod(shape))
        isz = 4 if dt is F32 else 2
        assert off % 4 == 0
        nb = (n * isz + 31) // 32 * 32
        assert off + nb <= ARENA_BYTES, ("SBUF overflow", off, nb)
        v = ARENA[:, off // 4:off // 4 + (n * isz + 3) // 4]
        if dt is not F32:
            v = v.bitcast(dt)[:, 0:n]
        if len(shape) == 2:
            v = v.rearrange("p (a b) -> p a b", b=shape[1])
        elif len(shape) == 3:
            v = v.rearrange("p (a b c) -> p a b c", b=shape[1], c=shape[2])
        return v, nb

    def sb(name, shape, dt=F32):
        shape = list(shape)[1:]
        v, nb = view(bump["p"], shape, dt)
        OFF[name] = bump["p"]
        bump["p"] += nb
        return v

    X = sb("X", [128, 16, NT2])
    XN = sb("XN", [128, 16, NT2], BF16)
    MIX = sb("MIX", [128, 16, NT2], BF16)
    HID = MIX
    NWS = 4
    wslots = [sb("ws%d" % i, [128, 16, 128], BF16) for i in range(NWS)]
    prm = {k: sb("p_" + k, (128,) + v) for k, v in PFM.items()}
    cst = {k: sb(k, (128,) + v) for k, v in CONSTS.items()}
    cst_bf = {k: sb(k + "_bf", (128,) + CONSTS[k], BF16) for k in ("c_ones", "c_ident")}
    FLAG = sb("FLAG", [128, 1])
    EPSC = sb("EPSC", [128, 1])
    ONEC = sb("ONEC", [128, 1])
    LRUC = sb("LRUC", [128, 16])
    LRUC2 = sb("LRUC2", [128, 16])
    TAIL_L = sb("TAIL_L", [128, 16, 3])
    TAIL_S = sb("TAIL_S", [128, 20, 3])
    HEND1 = sb("HEND1", [128, 16])
    HINIT = sb("HINIT", [128, 16])
    H0S_L = sb("H0S_L", [128, 16, NSEQ])
    O_PLC = sb("O_PLC", [128, 16, 3])
    O_PLH = sb("O_PLH", [128, 16])
    O_PSC = sb("O_PSC", [128, 20, 3])
    O_SLH = sb("O_SLH", [128, 16, NSEQ])
    DTBA = sb("DTBA", [128, 2])
    SCR0 = bump["p"]
    T0 = sb("T0", [128, NT2])
    SQ = [sb("SQ%d" % i, [128, 512], BF16) for i in range(2)]
    SCR1 = bump["p"]
    print("SBUF persistent bytes", SCR0, "scratch avail", ARENA_BYTES - SCR1)
    RSTD = T0

    def scratch(base=None):
        bump["p"] = SCR1 if base is None else base

    hspill = nc.dram_tensor("hspill", [128, 16, 128], F32).ap()

    PA = nc.alloc_psum_tensor("PA", [128, 2048], F32)
    PB = nc.alloc_psum_tensor("PB", [128, 2048], F32)

    def pk(P, c0, c1):
        nm = "PA" if P is PA else "PB"
        return [(nm, b) for b in range(c0 // 512, (c1 - 1) // 512 + 1)]

    def sslot(i):
        P = PA if i < 4 else PB
        b = i % 4
        return P[:, b * 512:b * 512 + 128], ("PA" if i < 4 else "PB", b)

    wq = []
    wstate = {"issued": 0, "taken": 0}

    def wplan(name, r0, KC, c0, ncol=128):
        ap = Wd[name][r0:r0 + KC * 128, c0:c0 + ncol].rearrange("(kc p) f -> p kc f", p=128)
        wq.append((ap, KC, ncol))

    def wget():
        while wstate["issued"] < min(len(wq), wstate["taken"] + NWS - 1):
            i = wstate["issued"]
            ap, KC, ncol = wq[i]
            s = i % NWS
            kb.dma("pool", wslots[s][:, 0:KC, 0:ncol], ap, ("wd", s), writes=[("w", s)])
            wstate["issued"] += 1
        assert wstate["taken"] < len(wq)
        s = wstate["taken"] % NWS
        wstate["taken"] += 1
        return wslots[s], ("w", s)

    def mm(out, lhsT, rhs, start, stop, reads, writes, inc):
        kb.op("pe", lambda e: e.matmul(out, lhsT=lhsT, rhs=rhs, start=start, stop=stop),
              reads=reads, writes=writes, inc=inc)

    def proj(w, wkey, KC, rhs_fn, rkeys, P, segs, M=128):
        for kc in range(KC):
            for (c0, n) in segs:
                mm(P[0:M, c0:c0 + n], w[:, kc, 0:M], rhs_fn(kc, c0, n), kc == 0, kc == KC - 1,
                   [wkey] + rkeys, pk(P, c0, c0 + n), kc == KC - 1)

    def act(out, in_, func, reads, writes, bias=None, scale=None, eng="act"):
        kw = {}
        if bias is not None:
            kw["bias"] = bias
        if scale is not None:
            kw["scale"] = scale
        kb.op("act", lambda e: e.activation(out=out, in_=in_, func=func, **kw), reads=reads, writes=writes)

    def tt(eng, out, a, b, op, reads, writes):
        kb.op(eng, lambda e: e.tensor_tensor(out=out, in0=a, in1=b, op=op), reads=reads, writes=writes)

    def ts(eng, out, a, s1, s2, op0, op1, reads, writes):
        if op1 is None:
            kb.op(eng, lambda e: e.tensor_scalar(out=out, in0=a, scalar1=s1, scalar2=None, op0=op0),
                  reads=reads, writes=writes)
        else:
            kb.op(eng, lambda e: e.tensor_scalar(out=out, in0=a, scalar1=s1, scalar2=s2, op0=op0, op1=op1),
                  reads=reads, writes=writes)

    def stt(eng, out, a, s, b, op0, op1, reads, writes):
        eng = "dve"
        kb.op(eng, lambda e: e.scalar_tensor_tensor(out=out, in0=a, scalar=s, in1=b, op0=op0, op1=op1),
              reads=reads, writes=writes)

    def cp(eng, out, in_, reads, writes):
        if eng == "act":
            kb.op("act", lambda e: e.copy(out=out, in_=in_), reads=reads, writes=writes)
        else:
            kb.op(eng, lambda e: e.tensor_copy(out=out, in_=in_), reads=reads, writes=writes)

    def memset(eng, ap, val, writes):
        kb.op(eng, lambda e: e.memset(ap, val), writes=writes)

    for k in PFM:
        kb.dma("sp", prm[k][:], Pd[k], "ld0", writes=["p_" + k])
    for k in CONSTS:
        kb.dma("sp", cst[k][:], Cd[k], "ld0", writes=[k])
    kb.dma("sp", FLAG[:], flag_d, "ld0", writes=["FLAG"])
    kb.dma("sp", DTBA[0:32, 0:1], dtb_d, "ld0", writes=["DTBA"])
    kb.dma("sp", DTBA[0:32, 1:2], alog_d, "ld0", writes=["DTBA"])
    kb.dma("sp", H0S_L[:], slh_d, "ld0", writes=["H0S_L"])
    memset("dve", EPSC[:], EPS, ["EPSC"])
    memset("dve", ONEC[:], 1.0, ["ONEC"])
    for k in ("c_ones", "c_ident"):
        cp("dve", cst_bf[k][:], cst[k][:], [k], [k + "_bf"])
    act(DTBA[0:32, 1:2], DTBA[0:32, 1:2], AF.Exp, ["DTBA"], ["DTBA"])
    ts("dve", DTBA[0:32, 1:2], DTBA[0:32, 1:2], -1.0, None, ALU.mult, None, ["DTBA"], ["DTBA"])
    act(LRUC[:], prm["lru_lam"][:], AF.Exp, ["p_lru_lam"], ["LRUC"], scale=-1.0)
    act(LRUC[:], LRUC[:], AF.Ln, ["LRUC", "ONEC"], ["LRUC"], bias=ONEC[:, 0:1])
    ts("dve", LRUC2[:], LRUC[:], -16.0, None, ALU.mult, None, ["LRUC"], ["LRUC2"])
    ts("dve", LRUC[:], LRUC[:], -8.0, None, ALU.mult, None, ["LRUC"], ["LRUC"])

    ones_bf = cst_bf["c_ones"]

    def xkeys(c=None):
        return [("X", c)] if c is not None else [("X", i) for i in range(16)]

    sqi = {"i": 0}

    def rms_rstd(src_fn, skeys_fn, nchunk, NT, segs, scale=1.0 / D):
        for c in range(nchunk):
            for (c0, n) in segs:
                i = sqi["i"] % 2
                sqi["i"] += 1
                act(SQ[i][:, 0:n], src_fn(c)[:, c0:c0 + n], AF.Square, skeys_fn(c), [("SQ", i)])
                mm(PA[:, c0:c0 + n], ones_bf[:, :], SQ[i][:, 0:n], c == 0, c == nchunk - 1,
                   [("SQ", i), "c_ones_bf"], pk(PA, c0, c0 + n), True)
        act(RSTD[:, 0:NT], PA[:, 0:NT], AF.Sqrt, pk(PA, 0, NT) + ["EPSC"], [("T", 0)], bias=EPSC[:, 0:1], scale=scale)
        kb.op("dve", lambda e: e.reciprocal(out=RSTD[:, 0:NT], in_=RSTD[:, 0:NT]), reads=[("T", 0)], writes=[("T", 0)])

    def norm_to_xn(gname, NT, segs):
        rms_rstd(lambda c: X[:, c, :], lambda c: xkeys(c), 16, NT, segs)
        for c in range(16):
            stt("dve" if c % 2 == 0 else "pool", XN[:, c, 0:NT], X[:, c, 0:NT], prm[gname][:, c:c + 1], RSTD[:, 0:NT],
                ALU.mult, ALU.mult, xkeys(c) + [("T", 0), "p_" + gname], [("XN", c)])

    def xn_rhs(kc, c0, n):
        return XN[:, kc, c0:c0 + n]

    xnkeys = [("XN", c) for c in range(16)]
    mixkeys = [("MIX", c) for c in range(16)]

    def ffn(pre, gname, NT, segs):
        if os.environ.get("DBG_SKIP") == "f":
            return
        scratch()
        T1 = sb("T1", [128, NT2])
        Ts = [T0, T1]
        SUB = int(os.environ.get("DBG_SUB", "99"))
        if SUB == 0:
            rms_rstd(lambda c: X[:, c, :], lambda c: xkeys(c), 16, NT, segs)
            return
        norm_to_xn(gname, NT, segs)
        if SUB == 1:
            return
        for hg in range(HG):
            for j in range(HPG):
                f = hg * HPG + j
                wplan(pre + "_wg", 0, 16, f * 128)
                wplan(pre + "_wu", 0, 16, f * 128)
            for dc in range(16):
                wplan(pre + "_wd", hg * HPG * 128, HPG, dc * 128)
        for hg in range(HG):
            for j in range(HPG):
                wg, kg = wget()
                proj(wg, kg, 16, xn_rhs, xnkeys, PA, segs)
                wu, ku = wget()
                proj(wu, ku, 16, xn_rhs, xnkeys, PB, segs)
                if SUB == 2:
                    return
                sg = Ts[j % 2]
                act(sg[:, 0:NT], PA[:, 0:NT], AF.Silu, pk(PA, 0, NT), [("T", j % 2)])
                tt("dve", HID[:, j, 0:NT], sg[:, 0:NT], PB[:, 0:NT], ALU.mult,
                   [("T", j % 2)] + pk(PB, 0, NT), [("MIX", j)])
                if SUB == 3:
                    return
            if SUB == 4:
                return
            for dc in range(16):
                wd, kd = wget()
                for si, (c0, n) in enumerate(segs):
                    P = PA if (dc * 3 + si) % 2 == 0 else PB
                    for j in range(HPG):
                        mm(P[:, 1536:1536 + n], wd[:, j, :], HID[:, j, c0:c0 + n], j == 0, j == HPG - 1,
                           [kd, ("MIX", j)], pk(P, 1536, 2048), j == HPG - 1)
                    stt("dve", X[:, dc, c0:c0 + n], P[:, 1536:1536 + n], 0.5, X[:, dc, c0:c0 + n], ALU.mult, ALU.add,
                        pk(P, 1536, 2048) + xkeys(dc), xkeys(dc))
        kb.barrier()

    def conv_silu(P, cw, cb, ci, tail_init, tname, convs, cname, NT, full, out_u, ukey, tail_save, tsname,
                  o_p, opname, o_s, osname, silu, EXTP, EXTS):
        pkeys = pk(P, 0, NT)
        if tail_init is None:
            memset("pool", EXTP[:, 0:3], 0.0, ["EXTP"])
        else:
            ts("pool", EXTP[:, 0:3], tail_init[:, ci, :], FLAG[:, 0:1], None, ALU.mult, None, [tname, "FLAG"], ["EXTP"])
        cp("act", EXTP[:, 3:3 + NP1], P[:, 0:NP1], pkeys, ["EXTP"])
        if tail_save is not None:
            cp("pool", tail_save[:, ci, :], EXTP[:, NP1:NP1 + 3], ["EXTP"], [tsname])
        if o_p is not None:
            cp("pool", o_p[:, ci, :], EXTP[:, NP1:NP1 + 3], ["EXTP"], [opname])
        up = out_u[:, 0:NP1]
        ts("dve", up, EXTP[:, 0:NP1], cw[:, ci, 0:1], cb[:, ci:ci + 1], ALU.mult, ALU.add, ["EXTP"], [ukey])
        for k in range(1, 4):
            stt("dve", up, EXTP[:, k:k + NP1], cw[:, ci, k:k + 1], up, ALU.mult, ALU.add, ["EXTP", ukey], [ukey])
        if full:
            kb.dma("sp", EXTS[:, :, 0:3], convs[:, ci, :, :], "ld_cv3", writes=["EXTS"])
            cp("act", EXTS[:, :, 3:11], P[:, NP1:NT2].rearrange("p (s t) -> p s t", t=8), pkeys, ["EXTS"])
            kb.dma("sp", o_s[:, ci, :, :], EXTS[:, :, 8:11], "st_cv3", reads=["EXTS"])
            us = out_u[:, NP1:NT2].rearrange("p (s t) -> p s t", t=8)
            ts("dve", us, EXTS[:, :, 0:8], cw[:, ci, 0:1], cb[:, ci:ci + 1], ALU.mult, ALU.add, ["EXTS", ukey], [ukey])
            for k in range(1, 4):
                stt("dve", us, EXTS[:, :, k:k + 8], cw[:, ci, k:k + 1], us, ALU.mult, ALU.add, ["EXTS", ukey], [ukey])
        if silu:
            act(out_u[:, 0:NT], out_u[:, 0:NT], AF.Silu, [ukey], [ukey])

    def wout_half(r0, NT, segs, wname="w_out"):
        for dc in range(16):
            wplan(wname, r0, 16, dc * 128)
        for dc in range(16):
            w, k = wget()
            P = PA if dc % 2 == 0 else PB
            proj(w, k, 16, lambda kc, c0, n: MIX[:, kc, c0:c0 + n], mixkeys, P, segs)
            tt("dve", X[:, dc, 0:NT], X[:, dc, 0:NT], P[:, 0:NT], ALU.add, xkeys(dc) + pk(P, 0, NT), xkeys(dc))

    def lru_phase(blk, NT, segs):
        full = blk == 2
        scratch()
        T1 = sb("T1", [128, NT2])
        T2 = sb("T2", [128, NT2])
        T3 = sb("T3", [128, NT2])
        UB = sb("UB", [128, NT2], BF16)
        GL = sb("GL", [128, NT2], BF16)
        EXTP = sb("EXTP", [128, 3 + NP1])
        EXTS = sb("EXTS", [128, NSEQ, 11])
        for j in range(16):
            if full:
                wplan("w_in", 0, 16, 2048 + j * 128)
            wplan("w_in", 0, 16, j * 128)
            wplan("lru_wa", j * 128, 1, 0)
            wplan("lru_wx", j * 128, 1, 0)
        if full:
            ts("dve", HINIT[:, :], HEND1[:, :], FLAG[:, 0:1], None, ALU.mult, None, ["HEND1", "FLAG"], ["HINIT"])
        for j in range(16):
            if full:
                wg_, kg_ = wget()
                proj(wg_, kg_, 16, xn_rhs, xnkeys, PB, segs)
                act(GL[:, 0:NT], PB[:, 0:NT], AF.Gelu, pk(PB, 0, NT), ["GL"])
            wx_, kx_ = wget()
            proj(wx_, kx_, 16, xn_rhs, xnkeys, PA, segs)
            U = T0
            conv_silu(PA, prm["lru_cw"], prm["lru_cb"], j, TAIL_L if full else None, "TAIL_L", slc_d, "CONVS_L", NT, full,
                      U, ("T", 0), None if full else TAIL_L, "TAIL_L", O_PLC if full else None, "O_PLC",
                      o_slc if full else None, "O_SLC", False, EXTP, EXTS)
            cp("pool", UB[:, 0:NT], U[:, 0:NT], [("T", 0)], ["UB"])
            wa_, ka_ = wget()
            wxx_, kxx_ = wget()
            for (c0, n) in segs:
                mm(PA[:, c0:c0 + n], wa_[:, 0, :], UB[:, c0:c0 + n], True, True, [ka_, "UB"], pk(PA, c0, c0 + n), True)
            for (c0, n) in segs:
                mm(PB[:, c0:c0 + n], wxx_[:, 0, :], UB[:, c0:c0 + n], True, True, [kxx_, "UB"], pk(PB, c0, c0 + n), True)
            R, I, A, HS = T1, T2, T3, T0
            act(R[:, 0:NT], PA[:, 0:NT], AF.Sigmoid, pk(PA, 0, NT), [("T", 1)], bias=prm["lru_ba"][:, j:j + 1])
            act(I[:, 0:NT], PB[:, 0:NT], AF.Sigmoid, pk(PB, 0, NT), [("T", 2)], bias=prm["lru_bx"][:, j:j + 1])
            act(A[:, 0:NT], R[:, 0:NT], AF.Exp, [("T", 1), "LRUC"], [("T", 3)], scale=LRUC[:, j:j + 1])
            act(R[:, 0:NT], R[:, 0:NT], AF.Exp, [("T", 1), "LRUC2"], [("T", 1)], scale=LRUC2[:, j:j + 1])
            act(R[:, 0:NT], R[:, 0:NT], AF.Sqrt, [("T", 1), "ONEC"], [("T", 1)], bias=ONEC[:, 0:1], scale=-1.0)
            tt("pool", I[:, 0:NT], I[:, 0:NT], U[:, 0:NT], ALU.mult, [("T", 2), ("T", 0)], [("T", 2)])
            tt("dve", R[:, 0:NT], R[:, 0:NT], I[:, 0:NT], ALU.mult, [("T", 1), ("T", 2)], [("T", 1)])
            init = HINIT[:, j:j + 1] if full else 0.0
            kb.op("dve", lambda e, init=init: e.tensor_tensor_scan(out=HS[:, 0:NP1], data0=A[:, 0:NP1], data1=R[:, 0:NP1],
                                                                  initial=init, op0=ALU.mult, op1=ALU.add),
                  reads=[("T", 3), ("T", 1), "HINIT"], writes=[("T", 0)])
            if not full:
                cp("pool", HEND1[:, j:j + 1], HS[:, NP1 - 1:NP1], [("T", 0)], ["HEND1"])
                continue
            cp("pool", O_PLH[:, j:j + 1], HS[:, NP1 - 1:NP1], [("T", 0)], ["O_PLH"])
            for s in range(NSEQ):
                c0 = NP1 + 8 * s
                kb.op("dve", lambda e, c0=c0, s=s, j=j: e.tensor_tensor_scan(
                    out=HS[:, c0:c0 + 8], data0=A[:, c0:c0 + 8], data1=R[:, c0:c0 + 8],
                    initial=H0S_L[:, j, s:s + 1], op0=ALU.mult, op1=ALU.add),
                    reads=[("T", 3), ("T", 1), "H0S_L"], writes=[("T", 0)])
            cp("pool", O_SLH[:, j, :], HS[:, NP1:NT2].rearrange("p (s t) -> p s t", t=8)[:, :, 7], [("T", 0)], ["O_SLH"])
            tt("dve", MIX[:, j, 0:NT], HS[:, 0:NT], GL[:, 0:NT], ALU.mult, [("T", 0), "GL"], [("MIX", j)])
        kb.barrier()
        if not full:
            return
        rms_rstd(lambda c: MIX[:, c, :], lambda c: [("MIX", c)], 16, NT, segs)
        for c in range(16):
            stt("dve" if c % 2 == 0 else "pool", MIX[:, c, 0:NT], MIX[:, c, 0:NT], prm["lru_og"][:, c:c + 1], RSTD[:, 0:NT],
                ALU.mult, ALU.mult, [("MIX", c), ("T", 0)], [("MIX", c)])
        wout_half(0, NT, segs)
        kb.barrier()

    def ssd_phase(blk, NT, segs):
        full = blk == 2
        ntile = NT // 128
        scratch()
        ZS = sb("ZS", [128, NT2], BF16)
        EXTP = sb("EXTP", [128, 3 + NP1])
        EXTS = sb("EXTS", [128, NSEQ, 11])
        CT1 = sb("CT1", [128, NT2], BF16)
        BTOK1 = sb("BTOK1", [128, 9, 128], BF16)
        CBM1 = sb("CBM1", [128, 9, 128], BF16)
        DT_TOK = sb("DT_TOK", [128, 9, 32])
        DA_TOK = sb("DA_TOK", [128, 9, 32])
        CS_TOK = sb("CS_TOK", [128, 9, 32])
        CSL_BC = sb("CSL_BC", [128, 9, 32])
        TOEND = sb("TOEND", [128, 9, 32])
        DEC_BC = sb("DEC_BC", [128, 9, 32])
        XPAD = sb("XPAD", [128, 2, 128], BF16)
        XSC = sb("XSC", [128, 128], BF16)
        XSCM = sb("XSCM", [128, 2, 128], BF16)
        SM = [sb("SM%d" % i, [128, 128]) for i in range(8)]
        SMB = [sb("SMB%d" % i, [128, 128], BF16) for i in range(2)]
        HT = sb("HT", [128, 128])
        HTB = sb("HTB", [128, 128], BF16)
        H0N = sb("H0N", [128, 2, 128])
        H0T = sb("H0T", [128, 2, 128])
        H0TB = sb("H0TB", [128, 2, 128], BF16)
        H1N = sb("H1N", [128, 2, 128])
        DECF = sb("DECF", [128, NSEQ])
        csz = NT2 * 2
        DTFv, _ = view(OFF["MIX"] + 7 * csz, [NT2], F32)
        DAFv, _ = view(OFF["MIX"] + 9 * csz, [NT2], F32)
        BTs = [MIX[:, 11, :], MIX[:, 12, :]]
        CTs = [MIX[:, 13, :], CT1[:, :]]
        BTOKs = [MIX[:, 14, :].rearrange("p (t n) -> p t n", n=128), BTOK1]
        CBMs = [MIX[:, 15, :].rearrange("p (t n) -> p t n", n=128), CBM1]
        kDTF = [("MIX", 7), ("MIX", 8)]
        kDAF = [("MIX", 9), ("MIX", 10)]
        kBT = [[("MIX", 11)], [("MIX", 12)]]
        kCT = [[("MIX", 13)], ["CT1"]]
        kBTOK = [[("MIX", 14)], ["BTOK1"]]
        kCBM = [[("MIX", 15)], ["CBM1"]]
        ut, slt, uts, same, onesf, seqm, ident = (cst["c_ut"], cst["c_slt"], cst["c_uts"], cst["c_same"],
                                                  cst["c_ones"], cst["c_seq"], cst["c_ident"])
        XS = T0
        for ci in (16, 17, 18, 19):
            wplan("w_in", 0, 16, 6144 + ci * 128)
        if os.environ.get("DBG_NODT") is None:
            wplan("w_in", 0, 16, 8704, 32)
        for pr in range(16):
            wplan("w_in", 0, 16, 6144 + pr * 128)
            if full:
                wplan("w_in", 0, 16, 4096 + pr * 128)
        memset("pool", XPAD[:, :, :], 0.0, ["XPAD"])

        def do_conv(P, ci):
            conv_silu(P, prm["ssd_cw"], prm["ssd_cb"], ci, TAIL_S if full else None, "TAIL_S", ssc_d, "CONVS_S", NT, full,
                      XS, ("T", 0), None if full else TAIL_S, "TAIL_S", O_PSC if full else None, "O_PSC",
                      o_ssc if full else None, "O_SSC", True, EXTP, EXTS)

        for ci in (16, 17, 18, 19):
            w, k = wget()
            proj(w, k, 16, xn_rhs, xnkeys, PA, segs)
            do_conv(PA, ci)
            g = (ci - 16) % 2
            if ci < 18:
                cp("pool", BTs[g][:, 0:NT], XS[:, 0:NT], [("T", 0)], kBT[g])
                for t in range(ntile):
                    so, sk = sslot(t % 8)
                    kb.op("pe", lambda e, so=so, t=t: e.transpose(so, XS[:, t * 128:(t + 1) * 128], ident[:, :]),
                          reads=[("T", 0), "c_ident"], writes=[sk])
                    cp("act" if t % 2 == 0 else "dve", BTOKs[g][:, t, :], so, [sk], kBTOK[g])
            else:
                cp("pool", CTs[g][:, 0:NT], XS[:, 0:NT], [("T", 0)], kCT[g])
        SUB = int(os.environ.get("DBG_SUB", "99"))
        if SUB == 0:
            return
        w, k = wget()
        proj(w, k, 16, xn_rhs, xnkeys, PA, segs, M=32)
        if SUB == 1:
            return
        act(DTFv[0:32, 0:NT], PA[0:32, 0:NT], AF.Exp, pk(PA, 0, NT) + ["DTBA"], kDTF, bias=DTBA[0:32, 0:1])
        act(DTFv[0:32, 0:NT], DTFv[0:32, 0:NT], AF.Ln, kDTF + ["ONEC"], kDTF, bias=ONEC[0:32, 0:1])
        ts("dve", DAFv[0:32, 0:NT], DTFv[0:32, 0:NT], DTBA[0:32, 1:2], None, ALU.mult, None, kDTF + ["DTBA"], kDAF)
        if SUB == 2:
            return
        for t in range(ntile):
            so, sk = sslot(t % 8)
            kb.op("pe", lambda e, so=so, t=t: e.transpose(so[:, 0:32], DTFv[0:32, t * 128:(t + 1) * 128], ident[0:32, 0:32]),
                  reads=kDTF + ["c_ident"], writes=[sk], inc=False)
            kb.op("pe", lambda e, so=so, t=t: e.transpose(so[:, 32:64], DAFv[0:32, t * 128:(t + 1) * 128], ident[0:32, 0:32]),
                  reads=kDAF + ["c_ident"], writes=[sk])
            cp("act", DT_TOK[:, t, :], so[:, 0:32], [sk], ["DT_TOK"])
            cp("act", DA_TOK[:, t, :], so[:, 32:64], [sk], ["DA_TOK"])
        if SUB == 3:
            return
        for t in range(ntile):
            samp = full and t == 8
            so, sk = sslot(t % 8)
            mm(so[:, 0:32], (uts if samp else ut)[:, :], DA_TOK[:, t, :], True, True, ["DA_TOK", "c_ut", "c_uts"], [sk], False)
            mm(so[:, 32:64], (same if samp else onesf)[:, :], DA_TOK[:, t, :], True, True, ["DA_TOK", "c_same", "c_ones"], [sk], True)
            cp("act", CS_TOK[:, t, :], so[:, 0:32], [sk], ["CS_TOK"])
            cp("act", CSL_BC[:, t, :], so[:, 32:64], [sk], ["CSL_BC"])
        tt("dve", TOEND[:, 0:ntile, :], CSL_BC[:, 0:ntile, :], CS_TOK[:, 0:ntile, :], ALU.subtract, ["CSL_BC", "CS_TOK"], ["TOEND"])
        act(TOEND[:, 0:ntile, :], TOEND[:, 0:ntile, :], AF.Exp, ["TOEND"], ["TOEND"])
        tt("dve", TOEND[:, 0:ntile, :], TOEND[:, 0:ntile, :], DT_TOK[:, 0:ntile, :], ALU.mult, ["TOEND", "DT_TOK"], ["TOEND"])
        act(DEC_BC[:, 0:ntile, :], CSL_BC[:, 0:ntile, :], AF.Exp, ["CSL_BC"], ["DEC_BC"])
        if SUB == 4:
            return
        if full:
            for g in range(2):
                for t in range(ntile):
                    samp = t == 8
                    so, sk = sslot((g * ntile + t) % 8)
                    mm(so, BTs[g][:, t * 128:(t + 1) * 128], CTs[g][:, t * 128:(t + 1) * 128], True, True,
                       kBT[g] + kCT[g], [sk], True)
                    tt("dve", CBMs[g][:, t, :], so, (uts if samp else ut)[:, :], ALU.mult, [sk, "c_ut", "c_uts"], kCBM[g])
        if SUB == 5:
            return
        for pr in range(16):
            if SUB in (6, 7) and pr == 1:
                return
            g = pr // 8
            h0 = 2 * pr
            w, k = wget()
            proj(w, k, 16, xn_rhs, xnkeys, PA, segs)
            do_conv(PA, pr)
            if full:
                w, k = wget()
                proj(w, k, 16, xn_rhs, xnkeys, PB, segs)
                act(ZS[:, 0:NT], PB[:, 0:NT], AF.Silu, pk(PB, 0, NT), ["ZS"])
                if os.environ.get("DBG_NOHSP"):
                    memset("dve", HT[:, :], 0.0, ["HT"])
                else:
                    kb.dma("sp", HT[:, :], hspill[:, pr, :], "ld_hsp", reads=["hspill"], writes=["HT"])
                    ts("dve", HT[:, :], HT[:, :], FLAG[:, 0:1], None, ALU.mult, None, ["HT", "FLAG"], ["HT"])
            else:
                memset("dve", HT[:, :], 0.0, ["HT"])
            for t in range(ntile):
                samp = full and t == 8
                if samp and SUB == 6:
                    continue
                U_ = uts if samp else ut
                cols = slice(t * 128, (t + 1) * 128)
                sx, kx = sslot(6 + t % 2)
                kb.op("pe", lambda e, sx=sx, t=t: e.transpose(sx, XS[:, t * 128:(t + 1) * 128], ident[:, :]),
                      reads=[("T", 0), "c_ident"], writes=[kx])
                if full and not os.environ.get("DBG_NOXPAD"):
                    cp("act", XPAD[:, 0, 0:64], sx[:, 0:64], [kx], ["XPAD"])
                    cp("act", XPAD[:, 1, 64:128], sx[:, 64:128], [kx], ["XPAD"])
                for hh_ in range(2):
                    act(XSC[:, hh_ * 64:(hh_ + 1) * 64], sx[:, hh_ * 64:(hh_ + 1) * 64], AF.Copy, [kx, "TOEND"], ["XSC"],
                        scale=TOEND[:, t, h0 + hh_:h0 + hh_ + 1])
                if SUB == 61:
                    return
                if full:
                    DABC, ECS = SM[0], SM[1]
                    cp("dve", DABC[:, :].rearrange("p (h q) -> p h q", q=64),
                       DA_TOK[:, t, h0:h0 + 2].unsqueeze(2).to_broadcast([128, 2, 64]), ["DA_TOK"], [("SM", 0)])
                    s0, k0 = sslot(0)
                    mm(s0, DABC[:, :], U_[:, :], True, True, [("SM", 0), "c_ut", "c_uts"], [k0], True)
                    act(ECS[:, :], s0, AF.Exp, [k0], [("SM", 1)])
                    if SUB == 62:
                        return
                    sy, ky = sslot(3)
                    for hh in range(2):
                        LM, EH, WDT = SM[2 + hh], SM[4 + hh], SMB[hh]
                        ts("pool", LM[:, :], slt[:, :], DA_TOK[:, t, h0 + hh:h0 + hh + 1], None, ALU.mult, None,
                           ["c_slt", "DA_TOK"], [("SM", 2 + hh)])
                        sd, kd = sslot(1 + hh)
                        mm(sd, LM[:, :], U_[:, :], True, True, [("SM", 2 + hh), "c_ut", "c_uts"], [kd], True)
                        act(EH[:, :], sd, AF.Exp, [kd], [("SM", 4 + hh)])
                        stt("dve", WDT[:, :], EH[:, :], DT_TOK[:, t, h0 + hh:h0 + hh + 1], CBMs[g][:, t, :], ALU.mult, ALU.mult,
                            [("SM", 4 + hh), "DT_TOK"] + kCBM[g], [("SMB", hh)])
                        mm(sy, XPAD[:, hh, :], WDT[:, :], hh == 0, hh == 1, ["XPAD", ("SMB", hh)], [ky], hh == 1)
                    if SUB == 63:
                        return
                    sr, kr = sslot(4)
                    if not samp:
                        cp("act", HTB[:, :], HT[:, :], ["HT"], ["HTB"])
                        mm(sr, HTB[:, :], CTs[g][:, cols], True, True, ["HTB"] + kCT[g], [kr], True)
                if SUB == 64:
                    return
                if not samp:
                    sh, kh = sslot(5)
                    mm(sh, BTOKs[g][:, t, :], XSC[:, :], True, True, kBTOK[g] + ["XSC"], [kh], True)
                    tt("dve", HT[:, :].rearrange("p (h q) -> p h q", q=64), HT[:, :].rearrange("p (h q) -> p h q", q=64),
                       DEC_BC[:, t, h0:h0 + 2].unsqueeze(2).to_broadcast([128, 2, 64]), ALU.mult, ["HT", "DEC_BC"], ["HT"])
                    tt("dve", HT[:, :], HT[:, :], sh, ALU.add, ["HT", kh], ["HT"])
                else:
                    s5, k5 = sslot(5)
                    mm(s5[:, 0:NSEQ], DABC[:, :], seqm[:, :], True, True, [("SM", 0), "c_seq"], [k5], True)
                    act(DECF[:, :], s5[:, 0:NSEQ], AF.Exp, [k5], ["DECF"])
                    for q in range(8):
                        kb.dma("sp", H0T[:, :, :], sshT_d[pr][:, 2 * q:2 * q + 2, :], "ld_h0t", writes=["H0T"])
                        kb.dma("sp", H0N[:, :, :], sshN_d[pr][:, 2 * q:2 * q + 2, :], "ld_h0n", writes=["H0N"])
                        cp("act", H0TB[:, :, :], H0T[:, :, :], ["H0T"], ["H0TB"])
                        for u in range(2):
                            s = 2 * q + u
                            mm(sr[:, 8 * s:8 * s + 8], H0TB[:, u, :], CTs[g][:, NP1 + 8 * s:NP1 + 8 * s + 8], True, True,
                               ["H0TB"] + kCT[g], [kr], s == NSEQ - 1 or u == 1)
                        tt("dve", XSCM[:, :, :], XSC[:, :].unsqueeze(1).to_broadcast([128, 2, 128]),
                           seqm[:, 2 * q:2 * q + 2].unsqueeze(2).to_broadcast([128, 2, 128]), ALU.mult, ["XSC", "c_seq"], ["XSCM"])
                        bo = (q % 2) * 512
                        for u in range(2):
                            mm(PA[:, bo + u * 128:bo + (u + 1) * 128], XSCM[:, u, :], BTOKs[g][:, 8, :], True, True,
                               ["XSCM"] + kBTOK[g], pk(PA, bo, bo + 512), u == 1)
                        for u in range(2):
                            s = 2 * q + u
                            stt("dve", H1N[:, u, :], H0N[:, u, :], DECF[:, s:s + 1], PA[:, bo + u * 128:bo + (u + 1) * 128],
                                ALU.mult, ALU.add, ["H0N", "DECF"] + pk(PA, bo, bo + 512), ["H1N"])
                        kb.dma("sp", o_ssh[pr][:, 2 * q:2 * q + 2, :], H1N[:, :, :], "st_h1", reads=["H1N"])
                    outkeys.append("H1N")
                if full:
                    T1_, Y2 = SM[6], SM[7]
                    tt("dve", T1_[:, :], sr, ECS[:, :], ALU.mult, [kr, ("SM", 1)], [("SM", 6)])
                    stt("dve", Y2[:, :], XS[:, cols], prm["ssd_dfm"][:, pr:pr + 1], sy, ALU.mult, ALU.add,
                        [("T", 0), ky], [("SM", 7)])
                    tt("pool", Y2[:, :], Y2[:, :], T1_[:, :], ALU.add, [("SM", 7), ("SM", 6)], [("SM", 7)])
                    tt("pool", MIX[:, pr, cols], Y2[:, :], ZS[:, cols], ALU.mult, [("SM", 7), "ZS"], [("MIX", pr)])
            if full:
                kb.dma("sp", o_pshT[:, pr, :], HT[:, :], "st_psh", reads=["HT"])
                outkeys.append("HT")
            else:
                kb.dma("sp", hspill[:, pr, :], HT[:, :], "st_hsp", reads=["HT"], writes=["hspill"])
        kb.barrier()
        if not full:
            return
        for g in range(2):
            rms_rstd(lambda c, g=g: MIX[:, g * 8 + c, :], lambda c, g=g: [("MIX", g * 8 + c)], 8, NT, segs, scale=1.0 / 1024)
            for c in range(8 * g, 8 * g + 8):
                stt("dve" if c % 2 == 0 else "pool", MIX[:, c, 0:NT], MIX[:, c, 0:NT], prm["ssd_og"][:, c:c + 1], RSTD[:, 0:NT],
                    ALU.mult, ALU.mult, [("MIX", c), ("T", 0)], [("MIX", c)])
        wout_half(2048, NT, segs)
        kb.barrier()

    def xattn(NT, segs):
        norm_to_xn("xattn_g", NT, segs)
        for dc in range(16):
            wplan("w_q", 0, 16, dc * 128)
        for nm in ("w_k", "w_v"):
            for dc in range(16):
                wplan(nm, 0, 16, dc * 128)
        ident = cst["c_ident"]
        for dc in range(16):
            w, k = wget()
            P = PA if dc % 2 == 0 else PB
            proj(w, k, 16, xn_rhs, xnkeys, P, segs)
            cp("act" if dc % 2 == 0 else "dve", MIX[:, dc, 0:NT], P[:, 0:NT], pk(P, 0, NT), [("MIX", dc)])
        kb.barrier()
        scratch(OFF["XN"])
        KT = sb("KT", [128, 16, 256], BF16)
        VTOK = sb("VTOK", [128, 2, D], BF16)
        MEMN = sb("MEMN", [128, 16, 256], BF16)
        ET = sb("ET", [128, 2, NT2], BF16)
        assert bump["p"] <= OFF["XN"] + 16 * NT2 * 2
        scratch()
        MRS = sb("MRS", [128, 256])
        MEMX = [sb("MEMX%d" % i, [128, 256]) for i in range(2)]
        STG = [sb("STG%d" % i, [128, 256]) for i in range(2)]
        KTS = [sb("KTS%d" % i, [128, 4, 256], BF16) for i in range(2)]
        VS = [sb("VS%d" % i, [128, 2, 512], BF16) for i in range(2)]
        for c in range(16):
            kb.dma("sp", MEMX[c % 2][:, :], mem_d[:, c, :], ("ld_mem", c % 2), writes=[("MEMX", c % 2)])
            act(SQ[c % 2][:, 0:256], MEMX[c % 2][:, :], AF.Square, [("MEMX", c % 2)], [("SQ", c % 2)])
            mm(PA[:, 0:256], ones_bf[:, :], SQ[c % 2][:, 0:256], c == 0, c == 15, [("SQ", c % 2), "c_ones_bf"], pk(PA, 0, 256), True)
        act(MRS[:, :], PA[:, 0:256], AF.Sqrt, pk(PA, 0, 256) + ["EPSC"], ["MRS"], bias=EPSC[:, 0:1], scale=1.0 / D)
        kb.op("dve", lambda e: e.reciprocal(out=MRS[:, :], in_=MRS[:, :]), reads=["MRS"], writes=["MRS"])
        for c in range(16):
            kb.dma("sp", MEMX[c % 2][:, :], mem_d[:, c, :], ("ld_mem", c % 2), writes=[("MEMX", c % 2)])
            stt("dve", MEMN[:, c, :], MEMX[c % 2][:, :], prm["mem_g"][:, c:c + 1], MRS[:, :], ALU.mult, ALU.mult,
                [("MEMX", c % 2), "MRS"], ["MEMN"])
        msegs = [(0, 256)]
        si = 0
        for nm in ("w_k", "w_v"):
            for dc in range(16):
                w, k = wget()
                P = PA if dc % 2 == 0 else PB
                proj(w, k, 16, lambda kc, c0, n: MEMN[:, kc, c0:c0 + n], ["MEMN"], P, msegs)
                st = STG[si % 2]
                sk_ = ("STG", si % 2)
                si += 1
                cp("act", st[:, :], P[:, 0:256], pk(P, 0, 256), [sk_])
                if nm == "w_k":
                    cp("dve", KT[:, dc, :], st[:, :], [sk_], ["KT"])
                    kb.dma("sp", o_mk[:, dc, :], st[:, :], ("st_kv", (si - 1) % 2), reads=[sk_])
                else:
                    kb.dma("sp", o_mv[:, dc, :], st[:, :], ("st_kv", (si - 1) % 2), reads=[sk_])
                    for mt in range(2):
                        so, sk = sslot((dc * 2 + mt) % 8)
                        kb.op("pe", lambda e, so=so, st=st, mt=mt: e.transpose(so, st[:, mt * 128:(mt + 1) * 128], ident[:, :]),
                              reads=[sk_, "c_ident"], writes=[sk])
                        cp("act" if mt == 0 else "pool" if False else "dve", VTOK[:, mt, dc * 128:(dc + 1) * 128], so, [sk], ["VTOK"])
        outkeys.extend([("STG", 0), ("STG", 1)])
        sc = 512.0 ** -0.5
        psegs = [(0, 512), (512, 512)]
        li = 0
        for h in range(4):
            for mt in range(2):
                P = PA if mt == 0 else PB
                for (c0, n) in psegs:
                    for kc in range(4):
                        mm(P[:, c0:c0 + n], KT[:, 4 * h + kc, mt * 128:(mt + 1) * 128], MIX[:, 4 * h + kc, c0:c0 + n],
                           kc == 0, kc == 3, ["KT", ("MIX", 4 * h + kc)], pk(P, c0, c0 + n), kc == 3)
            for s in range(NSEQ):
                b = li % 2
                li += 1
                kb.dma("pool", KTS[b][:, :, :], ckT_d[s][:, 4 * h:4 * h + 4, :], ("ld_ck", b), writes=[("KTS", b)])
                for mt in range(2):
                    P = PA if mt == 0 else PB
                    for kc in range(4):
                        mm(P[:, 1024 + 8 * s:1024 + 8 * s + 8], KTS[b][:, kc, mt * 128:(mt + 1) * 128],
                           MIX[:, 4 * h + kc, NP1 + 8 * s:NP1 + 8 * s + 8], kc == 0, kc == 3,
                           [("KTS", b), ("MIX", 4 * h + kc)], pk(P, 1024, 1152), kc == 3)
            for mt in range(2):
                P = PA if mt == 0 else PB
                act(ET[:, mt, 0:NT], P[:, 0:NT], AF.Exp, pk(P, 0, NT), [("ET", mt)], scale=sc)
            for (c0, n) in segs:
                for mt in range(2):
                    mm(PA[:, c0:c0 + n], ones_bf[:, :], ET[:, mt, c0:c0 + n], mt == 0, mt == 1,
                       [("ET", mt), "c_ones_bf"], pk(PA, c0, c0 + n), mt == 1)
            kb.op("dve", lambda e: e.reciprocal(out=RSTD[:, 0:NT], in_=PA[:, 0:NT]), reads=pk(PA, 0, NT), writes=[("T", 0)])
            for s in range(NSEQ):
                b = s % 2
                kb.dma("pool", VS[b][:, :, :], cv_d[s][:, :, h * 512:(h + 1) * 512], ("ld_cv", b), writes=[("VS", b)])
                for dcl in range(4):
                    for mt in range(2):
                        mm(PB[:, 1024 + dcl * 128 + 8 * s:1024 + dcl * 128 + 8 * s + 8], VS[b][:, mt, dcl * 128:(dcl + 1) * 128],
                           ET[:, mt, NP1 + 8 * s:NP1 + 8 * s + 8], mt == 0, mt == 1, [("VS", b), ("ET", mt)],
                           pk(PB, 1024, 1536), mt == 1)
            for dcl in range(4):
                dc = 4 * h + dcl
                tt("dve", MIX[:, dc, NP1:NT2], PB[:, 1024 + dcl * 128:1024 + dcl * 128 + 128], RSTD[:, NP1:NT2], ALU.mult,
                   pk(PB, 1024, 1536) + [("T", 0)], [("MIX", dc)])
            for dcl in range(4):
                dc = 4 * h + dcl
                for (c0, n) in psegs:
                    for mt in range(2):
                        mm(PB[:, c0:c0 + n], VTOK[:, mt, dc * 128:(dc + 1) * 128], ET[:, mt, c0:c0 + n], mt == 0, mt == 1,
                           ["VTOK", ("ET", mt)], pk(PB, c0, c0 + n), mt == 1)
                tt("dve", MIX[:, dc, 0:NP1], PB[:, 0:NP1], RSTD[:, 0:NP1], ALU.mult, pk(PB, 0, NP1) + [("T", 0)], [("MIX", dc)])
        kb.barrier()
        wout_half(0, NT, segs, wname="w_o")
        kb.barrier()

    segs1 = [(0, 512), (512, 512)]
    segs2 = [(0, 512), (512, 512), (1024, 128)]

    def finish(dump=None):
        if dump is not None:
            kb.barrier()
            ncd = NP1 if stage < 4 else NT2
            for c in range(16):
                kb.dma("sp", y_d[:, c, 0:ncd], X[:, c, 0:ncd], "st_dbg", reads=xkeys(c))
            outkeys.extend(xkeys())
            if dump == "xn" or dump == "mix":
                src = XN if dump == "xn" else MIX
                for c in range(16):
                    kb.dma("pool", dbg_d[:, c, 0:ncd], src[:, c, 0:ncd], "st_dbg2", reads=[(dump.upper(), c)])
                outkeys.extend([(dump.upper(), c) for c in range(16)])
        olist = ((o_plc, O_PLC, "O_PLC"), (o_plh, O_PLH, "O_PLH"), (o_psc, O_PSC, "O_PSC"), (o_slh, O_SLH, "O_SLH"))
        if stage in (2, 3):
            olist = ((o_plc, TAIL_L, "TAIL_L"), (o_plh, HEND1, "HEND1"), (o_psc, TAIL_S, "TAIL_S"))
            if stage == 2:
                olist = olist[0:2]
            if stage == 3 and os.environ.get("DBG_SUB") is None:
                kb.dma("sp", o_pshT, hspill, "st_misc", reads=["hspill"])
        for (dst, src, key) in olist:
            kb.dma("sp", dst, src, "st_misc", reads=[key])
            outkeys.append(key)
        outkeys.append("EXTS")
        kb.final_wait("sp", outkeys)
        kb.emit()
        return nc, kb

    dbg_d = dout("dbg", [128, 16, NT2]) if stage < 99 else None
    for c in range(16):
        kb.dma("sp", X[:, c, 0:NP1], xa_d[:, c, :], "ld_x", writes=xkeys(c))
    if stage == 0:
        return finish("x")
    SKIP = os.environ.get("DBG_SKIP") is not None
    if not SKIP:
        ffn("ffn1", "ffn1_g", NP1, segs1)
    if stage == 1:
        return finish("xn")
    norm_to_xn("mix_g", NP1, segs1)
    if not SKIP or os.environ.get("DBG_SKIP") == "f":
        lru_phase(1, NP1, segs1)
    if stage == 2:
        return finish("xn")
    ssd_phase(1, NP1, segs1)
    if stage == 3:
        return finish("xn")
    for c in range(16):
        kb.dma("sp", X[:, c, :], xb_d[:, c, :], "ld_x", writes=xkeys(c))
    ffn("ffn1", "ffn1_g", NT2, segs2)
    if stage == 4:
        return finish("xn")
    norm_to_xn("mix_g", NT2, segs2)
    lru_phase(2, NT2, segs2)
    if stage == 5:
        return finish("mix")
    ssd_phase(2, NT2, segs2)
    if stage == 6:
        return finish("mix")
    xattn(NT2, segs2)
    if stage == 7:
        return finish("mix")
    ffn("ffn2", "ffn2_g", NT2, segs2)
    scratch()
    OST = [sb("OST%d" % i, [128, NT2]) for i in range(3)]
    rms_rstd(lambda c: X[:, c, :], lambda c: xkeys(c), 16, NT2, segs2)
    for c in range(16):
        o = OST[c % 3]
        stt("dve", o[:, :], X[:, c, :], prm["final_g"][:, c:c + 1], RSTD[:, :], ALU.mult, ALU.mult,
            xkeys(c) + [("T", 0)], [("OST", c % 3)])
        kb.dma("sp", y_d[:, c, :], o[:, :], ("st_y", c % 3), reads=[("OST", c % 3)])
    outkeys.extend([("OST", i) for i in range(3)])
    return finish(None)


_CACHE = {}


def _fm(v, shape_tail=()):
    v = np.asarray(v, np.float32)
    C = v.shape[0] // 128
    return np.ascontiguousarray(np.moveaxis(v.reshape((C, 128) + v.shape[1:]), 0, 1))


def _tok_fm(x):
    T_ = x.shape[0]
    return np.ascontiguousarray(x.reshape(T_, 16, 128).transpose(2, 1, 0))


def _fm_tok(y):
    return np.ascontiguousarray(y.transpose(2, 1, 0).reshape(y.shape[2], -1))


def make_in_maps(inp):
    f = lambda k: np.asarray(inp[k], np.float32)

    x_prompt, mem_prompt, x_sample = f("x_prompt"), f("mem_prompt"), f("x_sample")
    ck, cv = f("cache_mem_k")[0], f("cache_mem_v")[0]
    slc, slh, ssc, ssh = f("state_lru_conv")[0], f("state_lru_h")[0], f("state_ssd_conv")[0], f("state_ssd_h")[0]

    shared = {
        "ffn1_wg": f("ffn1_w_gate")[0], "ffn1_wu": f("ffn1_w_up")[0], "ffn1_wd": f("ffn1_w_down")[0],
        "ffn2_wg": f("ffn2_w_gate")[0], "ffn2_wu": f("ffn2_w_up")[0], "ffn2_wd": f("ffn2_w_down")[0],
        "w_in": f("w_in")[0], "w_out": f("w_out")[0],
        "w_q": f("xattn_w_q")[0], "w_k": f("xattn_w_k")[0], "w_v": f("xattn_w_v")[0], "w_o": f("xattn_w_o")[0],
        "lru_wa": f("lru_w_a")[0].reshape(D, 128), "lru_wx": f("lru_w_x")[0].reshape(D, 128),
        "ffn1_g": _fm(f("ffn1_norm_g")[0]), "mix_g": _fm(f("mix_norm_g")[0]), "xattn_g": _fm(f("xattn_norm_g")[0]),
        "mem_g": _fm(f("mem_norm_g")[0]), "ffn2_g": _fm(f("ffn2_norm_g")[0]), "final_g": _fm(f("final_norm_g")),
        "lru_cw": _fm(f("lru_conv_w")[0].T), "lru_cb": _fm(f("lru_conv_b")[0]), "lru_ba": _fm(f("lru_b_a")[0]),
        "lru_bx": _fm(f("lru_b_x")[0]), "lru_lam": _fm(f("lru_lambda")[0]), "lru_og": _fm(f("lru_out_norm_g")[0]),
        "ssd_cw": _fm(f("ssd_conv_w")[0].T), "ssd_cb": _fm(f("ssd_conv_b")[0]), "ssd_og": _fm(f("ssd_out_norm_g")[0]),
        "ssd_dfm": _fm(np.repeat(f("ssd_d")[0], 64)),
        "dtb": f("ssd_dt_bias")[0].reshape(32, 1).copy(), "alog": f("ssd_a_log")[0].reshape(32, 1).copy(),
    }
    j = np.arange(128)
    seq_of = j // 8
    shared["c_ident"] = np.eye(128, dtype=np.float32)
    shared["c_ut"] = (j[:, None] <= j[None, :]).astype(np.float32)
    shared["c_slt"] = (j[:, None] > j[None, :]).astype(np.float32)
    shared["c_same"] = (seq_of[:, None] == seq_of[None, :]).astype(np.float32)
    shared["c_uts"] = shared["c_ut"] * shared["c_same"]
    shared["c_ones"] = np.ones((128, 128), np.float32)
    shared["c_seq"] = (seq_of[:, None] == np.arange(16)[None, :]).astype(np.float32)
    shared = {k: np.ascontiguousarray(v, dtype=np.float32) for k, v in shared.items()}

    in_maps = []
    for c in range(8):
        b, half = c // 2, c % 2
        m = dict(shared)
        m["xa"] = _tok_fm(x_prompt[b, 0:NP1])
        xs = x_sample[16 * c:16 * c + 16].reshape(NS, D)
        m["xb"] = _tok_fm(np.concatenate([x_prompt[b, half * NP1:(half + 1) * NP1], xs], 0))
        m["flag"] = np.full((128, 1), float(half), np.float32)
        m["memT"] = _tok_fm(mem_prompt[b])
        sl = slice(16 * c, 16 * c + 16)
        kk = ck[sl].reshape(NSEQ, 256, 16, 128)
        m["ckT"] = np.ascontiguousarray(kk.transpose(0, 3, 2, 1))
        vv = cv[sl].reshape(NSEQ, 2, 128, D)
        m["cv"] = np.ascontiguousarray(vv.transpose(0, 2, 1, 3))
        m["slc"] = np.ascontiguousarray(slc[sl].reshape(NSEQ, 3, 16, 128).transpose(3, 2, 0, 1))
        m["slh"] = np.ascontiguousarray(slh[sl].reshape(NSEQ, 16, 128).transpose(2, 1, 0))
        m["ssc"] = np.ascontiguousarray(ssc[sl].reshape(NSEQ, 3, 20, 128).transpose(3, 2, 0, 1))
        hh = ssh[sl].reshape(NSEQ, 16, 128, 128)
        m["sshN"] = np.ascontiguousarray(hh.transpose(1, 2, 0, 3))
        m["sshT"] = np.ascontiguousarray(hh.transpose(1, 3, 0, 2))
        in_maps.append(m)

    return in_maps


def kernel(**inp):
    if "nc" not in _CACHE:
        _CACHE["nc"] = build_program()
    nc, kb = _CACHE["nc"]
    in_maps = make_in_maps(inp)
    res = run_bass_kernel_spmd(nc, in_maps, core_ids=list(range(8)))
    R = res.results

    y_prompt = np.zeros((4, 2048, D), np.float32)
    y_sample = np.zeros((128, 8, D), np.float32)
    p_lc = np.zeros((1, 4, 3, D), np.float32)
    p_lh = np.zeros((1, 4, D), np.float32)
    p_sc = np.zeros((1, 4, 3, 2560), np.float32)
    p_sh = np.zeros((1, 4, 32, 64, 128), np.float32)
    p_mk = np.zeros((1, 4, 256, 4, 512), np.float32)
    p_mv = np.zeros((1, 4, 256, 4, 512), np.float32)
    s_lc = np.zeros((1, 128, 3, D), np.float32)
    s_lh = np.zeros((1, 128, D), np.float32)
    s_sc = np.zeros((1, 128, 3, 2560), np.float32)
    s_sh = np.zeros((1, 128, 32, 64, 128), np.float32)
    for c in range(8):
        b, half = c // 2, c % 2
        r = R[c]
        yt = _fm_tok(r["y"])
        y_prompt[b, half * NP1:(half + 1) * NP1] = yt[0:NP1]
        y_sample[16 * c:16 * c + 16] = yt[NP1:].reshape(16, 8, D)
        sl = slice(16 * c, 16 * c + 16)
        s_lc[0, sl] = r["o_slc"].transpose(2, 3, 1, 0).reshape(16, 3, D)
        s_lh[0, sl] = r["o_slh"].transpose(2, 1, 0).reshape(16, D)
        s_sc[0, sl] = r["o_ssc"].transpose(2, 3, 1, 0).reshape(16, 3, 2560)
        s_sh[0, sl] = r["o_ssh"].transpose(2, 0, 1, 3).reshape(16, 32, 64, 128)
        if half == 1:
            p_lc[0, b] = r["o_plc"].transpose(2, 1, 0).reshape(3, D)
            p_lh[0, b] = r["o_plh"].T.reshape(D)
            p_sc[0, b] = r["o_psc"].transpose(2, 1, 0).reshape(3, 2560)
            p_sh[0, b] = r["o_pshT"].transpose(1, 2, 0).reshape(32, 64, 128)
        else:
            p_mk[0, b] = _fm_tok(r["o_mkT"]).reshape(256, 4, 512)
            p_mv[0, b] = _fm_tok(r["o_mvT"]).reshape(256, 4, 512)
    return (y_prompt, y_sample, p_lc, p_lh, p_sc, p_sh, p_mk, p_mv, s_lc, s_lh, s_sc, s_sh)
```

```python
import os
import numpy as np
import concourse.bass as bass
import concourse.mybir as mybir
from concourse.bass_utils import run_bass_kernel_spmd

F32 = mybir.dt.float32
BF16 = mybir.dt.bfloat16
AF = mybir.ActivationFunctionType
ALU = mybir.AluOpType

D = 2048
FF = 5632
NHID = FF // 128
HG = 4
HPG = NHID // HG
DIN = 8736
NP1 = 1024
NS = 128
NT2 = NP1 + NS
NSEQ = 16
EPS = 1e-6


class KB:
    ENG = ("pe", "act", "dve", "pool", "sp")

    def __init__(self, nc):
        self.nc = nc
        self.ops = {e: [] for e in self.ENG}
        self.sem = {e: nc.alloc_semaphore("s_" + e) for e in ("pe", "act", "dve", "pool")}
        self.cnt = {e: 0 for e in ("pe", "act", "dve", "pool")}
        self.waited = {e: {} for e in self.ENG}
        self.lastw = {}
        self.readers = {}
        self.dsem = {}
        self.dcnt = {}
        self.n_inst = 0

    def _deps(self, reads, writes):
        deps = {}

        def add(tok):
            if tok is None:
                return
            s, v = tok
            if s in self.dcnt:
                v = self.dcnt[s]
            if deps.get(s, 0) < v:
                deps[s] = v

        for k in reads:
            add(self.lastw.get(k))
        for k in writes:
            add(self.lastw.get(k))
            for tok in self.readers.get(k, {}).items():
                add(tok)
        return deps

    def _commit(self, tok, reads, writes):
        for k in reads:
            r = self.readers.setdefault(k, {})
            if r.get(tok[0], 0) < tok[1]:
                r[tok[0]] = tok[1]
        for k in writes:
            self.lastw[k] = tok
            self.readers[k] = {}

    def _waits(self, eng, deps):
        ws = []
        for s, v in deps.items():
            if self.waited[eng].get(s, 0) >= v:
                continue
            self.waited[eng][s] = v
            ws.append((s, v))
        return ws

    def _semh(self, s):
        return self.sem[s] if s in self.sem else self.dsem[s]

    def op(self, eng, fn, reads=(), writes=(), inc=True):
        inc = True
        deps = self._deps(reads, writes)
        ws = self._waits(eng, deps)
        if inc:
            self.cnt[eng] += 1
            tok = (eng, self.cnt[eng])
        else:
            tok = (eng, self.cnt[eng] + 1)
        self.ops[eng].append((ws, fn, (self.sem[eng], 1) if inc else None))
        self._commit(tok, reads, writes)
        self.n_inst += 1

    def dma(self, q, out, in_, slot, reads=(), writes=()):
        if slot not in self.dsem:
            self.dsem[slot] = self.nc.alloc_semaphore("d%d" % len(self.dsem))
            self.dcnt[slot] = 0
        deps = self._deps(reads, writes)
        ws = self._waits(q, deps)
        self.dcnt[slot] += 16
        tok = (slot, self.dcnt[slot])
        self.ops[q].append((ws, lambda e, o=out, i=in_: e.dma_start(out=o, in_=i), (self.dsem[slot], 16)))
        self._commit(tok, reads, writes)
        self.n_inst += 1

    def barrier(self):
        toks = {e: v for e, v in self.cnt.items() if v > 0}
        toks.update({s: v for s, v in self.dcnt.items() if v > 0})
        for e in self.ENG:
            ws = self._waits(e, dict(toks))
            if ws:
                self.ops[e].append((ws, None, None))

    def final_wait(self, eng, keys):
        deps = self._deps(keys, keys)
        ws = self._waits(eng, deps)
        self.ops[eng].append((ws, None, None))

    def emit(self):
        nc = self.nc
        hmap = {"pe": "tensor", "act": "scalar", "dve": "vector", "pool": "gpsimd", "sp": "sync"}
        with nc.Block() as block:
            for e in self.ENG:
                ops = self.ops[e]

                def body(eng, ops=ops):
                    for ws, fn, inc in ops:
                        for s, v in ws:
                            eng.wait_ge(self._semh(s), v)
                        if fn is None:
                            continue
                        ins = fn(eng)
                        if inc is not None:
                            ins.then_inc(inc[0], inc[1])

                getattr(block, hmap[e])(body)


W2D = {
    "ffn1_wg": (D, FF), "ffn1_wu": (D, FF), "ffn1_wd": (FF, D),
    "ffn2_wg": (D, FF), "ffn2_wu": (D, FF), "ffn2_wd": (FF, D),
    "w_in": (D, DIN), "w_out": (2 * D, D),
    "w_q": (D, D), "w_k": (D, D), "w_v": (D, D), "w_o": (D, D),
    "lru_wa": (D, 128), "lru_wx": (D, 128),
}
PFM = {
    "ffn1_g": (16,), "mix_g": (16,), "xattn_g": (16,), "mem_g": (16,), "ffn2_g": (16,), "final_g": (16,),
    "lru_cw": (16, 4), "lru_cb": (16,), "lru_ba": (16,), "lru_bx": (16,), "lru_lam": (16,), "lru_og": (16,),
    "ssd_cw": (20, 4), "ssd_cb": (20,), "ssd_og": (16,), "ssd_dfm": (16,),
}
CONSTS = {"c_ident": (128,), "c_ut": (128,), "c_slt": (128,), "c_uts": (128,), "c_same": (128,),
          "c_ones": (128,), "c_seq": (16,)}


def build_program(stage=99):
    nc = bass.Bass("TRN2", target_bir_lowering=False)
    kb = KB(nc)

    kb.inputs = []

    def din(name, shape):
        kb.inputs.append(name)
        return nc.dram_tensor(name, list(shape), F32, kind="ExternalInput").ap()

    def dout(name, shape):
        return nc.dram_tensor(name, list(shape), F32, kind="ExternalOutput").ap()

    def sb(name, shape, dt=F32):
        return nc.alloc_sbuf_tensor(name, list(shape), dt)

    xa_d = din("xa", [128, 16, NP1])
    xb_d = din("xb", [128, 16, NT2])
    flag_d = din("flag", [128, 1])
    mem_d = din("memT", [128, 16, 256])
    ckT_d = din("ckT", [NSEQ, 128, 16, 256]) if stage >= 7 else None
    cv_d = din("cv", [NSEQ, 128, 2, D]) if stage >= 7 else None
    slc_d = din("slc", [128, 16, NSEQ, 3])
    slh_d = din("slh", [128, 16, NSEQ])
    ssc_d = din("ssc", [128, 20, NSEQ, 3])
    sshN_d = din("sshN", [16, 128, NSEQ, 128]) if stage >= 6 else None
    sshT_d = din("sshT", [16, 128, NSEQ, 128]) if stage >= 6 else None
    dtb_d = din("dtb", [32, 1])
    alog_d = din("alog", [32, 1])
    class _LazyW(dict):
        def __missing__(self, k):
            self[k] = din(k, W2D[k])
            return self[k]

    Wd = _LazyW()
    if stage >= 99:
        for k in W2D:
            Wd[k]
    Pd = {k: din(k, (128,) + v) for k, v in PFM.items()}
    Cd = {k: din(k, (128,) + v) for k, v in CONSTS.items()}

    y_d = dout("y", [128, 16, NT2])
    o_plc = dout("o_plc", [128, 16, 3])
    o_plh = dout("o_plh", [128, 16])
    o_psc = dout("o_psc", [128, 20, 3])
    o_pshT = dout("o_pshT", [128, 16, 128])
    o_mk = dout("o_mkT", [128, 16, 256])
    o_mv = dout("o_mvT", [128, 16, 256])
    o_slc = dout("o_slc", [128, 16, NSEQ, 3])
    o_slh = dout("o_slh", [128, 16, NSEQ])
    o_ssc = dout("o_ssc", [128, 20, NSEQ, 3])
    o_ssh = dout("o_ssh", [16, 128, NSEQ, 128])
    outkeys = []

    ARENA_BYTES = 212000
    ARENA = nc.alloc_sbuf_tensor("arena", [128, ARENA_BYTES // 4], F32)
    bump = {"p": 0}
    OFF = {}

    def view(off, shape, dt):
        n = int(np.prod(shape))
        isz = 4 if dt is F32 else 2
        assert off % 4 == 0
        nb = (n * isz + 31) // 32 * 32
        assert off + nb <= ARENA_BYTES, ("SBUF overflow", off, nb)
        v = ARENA[:, off // 4:off // 4 + (n * isz + 3) // 4]
        if dt is not F32:
            v = v.bitcast(dt)[:, 0:n]
        if len(shape) == 2:
            v = v.rearrange("p (a b) -> p a b", b=shape[1])
        elif len(shape) == 3:
            v = v.rearrange("p (a b c) -> p a b c", b=shape[1], c=shape[2])
        return v, nb

    def sb(name, shape, dt=F32):
        shape = list(shape)[1:]
        v, nb = view(bump["p"], shape, dt)
        OFF[name] = bump["p"]
        bump["p"] += nb
        return v

    X = sb("X", [128, 16, NT2])
    XN = sb("XN", [128, 16, NT2], BF16)
    MIX = sb("MIX", [128, 16, NT2], BF16)
    HID = MIX
    NWS = 4
    wslots = [sb("ws%d" % i, [128, 16, 128], BF16) for i in range(NWS)]
    prm = {k: sb("p_" + k, (128,) + v) for k, v in PFM.items()}
    cst = {k: sb(k, (128,) + v) for k, v in CONSTS.items()}
    cst_bf = {k: sb(k + "_bf", (128,) + CONSTS[k], BF16) for k in ("c_ones", "c_ident")}
    FLAG = sb("FLAG", [128, 1])
    EPSC = sb("EPSC", [128, 1])
    ONEC = sb("ONEC", [128, 1])
    LRUC = sb("LRUC", [128, 16])
    LRUC2 = sb("LRUC2", [128, 16])
    TAIL_L = sb("TAIL_L", [128, 16, 3])
    TAIL_S = sb("TAIL_S", [128, 20, 3])
    HEND1 = sb("HEND1", [128, 16])
    HINIT = sb("HINIT", [128, 16])
    H0S_L = sb("H0S_L", [128, 16, NSEQ])
    O_PLC = sb("O_PLC", [128, 16, 3])
    O_PLH = sb("O_PLH", [128, 16])
    O_PSC = sb("O_PSC", [128, 20, 3])
    O_SLH = sb("O_SLH", [128, 16, NSEQ])
    DTBA = sb("DTBA", [128, 2])
    SCR0 = bump["p"]
    T0 = sb("T0", [128, NT2])
    SQ = [sb("SQ%d" % i, [128, 512], BF16) for i in range(2)]
    SCR1 = bump["p"]
    print("SBUF persistent bytes", SCR0, "scratch avail", ARENA_BYTES - SCR1)
    RSTD = T0

    def scratch(base=None):
        bump["p"] = SCR1 if base is None else base

    hspill = nc.dram_tensor("hspill", [128, 16, 128], F32).ap()

    PA = nc.alloc_psum_tensor("PA", [128, 2048], F32)
    PB = nc.alloc_psum_tensor("PB", [128, 2048], F32)

    def pk(P, c0, c1):
        nm = "PA" if P is PA else "PB"
        return [(nm, b) for b in range(c0 // 512, (c1 - 1) // 512 + 1)]

    def sslot(i):
        P = PA if i < 4 else PB
        b = i % 4
        return P[:, b * 512:b * 512 + 128], ("PA" if i < 4 else "PB", b)

    wq = []
    wstate = {"issued": 0, "taken": 0}

    def wplan(name, r0, KC, c0, ncol=128):
        ap = Wd[name][r0:r0 + KC * 128, c0:c0 + ncol].rearrange("(kc p) f -> p kc f", p=128)
        wq.append((ap, KC, ncol))

    def wget():
        while wstate["issued"] < min(len(wq), wstate["taken"] + NWS - 1):
            i = wstate["issued"]
            ap, KC, ncol = wq[i]
            s = i % NWS
            kb.dma("pool", wslots[s][:, 0:KC, 0:ncol], ap, ("wd", s), writes=[("w", s)])
            wstate["issued"] += 1
        assert wstate["taken"] < len(wq)
        s = wstate["taken"] % NWS
        wstate["taken"] += 1
        return wslots[s], ("w", s)

    def mm(out, lhsT, rhs, start, stop, reads, writes, inc):
        kb.op("pe", lambda e: e.matmul(out, lhsT=lhsT, rhs=rhs, start=start, stop=stop),
              reads=reads, writes=writes, inc=inc)

    def proj(w, wkey, KC, rhs_fn, rkeys, P, segs, M=128):
        for kc in range(KC):
            for (c0, n) in segs:
                mm(P[0:M, c0:c0 + n], w[:, kc, 0:M], rhs_fn(kc, c0, n), kc == 0, kc == KC - 1,
                   [wkey] + rkeys, pk(P, c0, c0 + n), kc == KC - 1)

    def act(out, in_, func, reads, writes, bias=None, scale=None, eng="act"):
        kw = {}
        if bias is not None:
            kw["bias"] = bias
        if scale is not None:
            kw["scale"] = scale
        kb.op("act", lambda e: e.activation(out=out, in_=in_, func=func, **kw), reads=reads, writes=writes)

    def tt(eng, out, a, b, op, reads, writes):
        kb.op(eng, lambda e: e.tensor_tensor(out=out, in0=a, in1=b, op=op), reads=reads, writes=writes)

    def ts(eng, out, a, s1, s2, op0, op1, reads, writes):
        if op1 is None:
            kb.op(eng, lambda e: e.tensor_scalar(out=out, in0=a, scalar1=s1, scalar2=None, op0=op0),
                  reads=reads, writes=writes)
        else:
            kb.op(eng, lambda e: e.tensor_scalar(out=out, in0=a, scalar1=s1, scalar2=s2, op0=op0, op1=op1),
                  reads=reads, writes=writes)

    def stt(eng, out, a, s, b, op0, op1, reads, writes):
        eng = "dve"
        kb.op(eng, lambda e: e.scalar_tensor_tensor(out=out, in0=a, scalar=s, in1=b, op0=op0, op1=op1),
              reads=reads, writes=writes)

    def cp(eng, out, in_, reads, writes):
        if eng == "act":
            kb.op("act", lambda e: e.copy(out=out, in_=in_), reads=reads, writes=writes)
        else:
            kb.op(eng, lambda e: e.tensor_copy(out=out, in_=in_), reads=reads, writes=writes)

    def memset(eng, ap, val, writes):
        kb.op(eng, lambda e: e.memset(ap, val), writes=writes)

    for k in PFM:
        kb.dma("sp", prm[k][:], Pd[k], "ld0", writes=["p_" + k])
    for k in CONSTS:
        kb.dma("sp", cst[k][:], Cd[k], "ld0", writes=[k])
    kb.dma("sp", FLAG[:], flag_d, "ld0", writes=["FLAG"])
    kb.dma("sp", DTBA[0:32, 0:1], dtb_d, "ld0", writes=["DTBA"])
    kb.dma("sp", DTBA[0:32, 1:2], alog_d, "ld0", writes=["DTBA"])
    kb.dma("sp", H0S_L[:], slh_d, "ld0", writes=["H0S_L"])
    memset("dve", EPSC[:], EPS, ["EPSC"])
    memset("dve", ONEC[:], 1.0, ["ONEC"])
    for k in ("c_ones", "c_ident"):
        cp("dve", cst_bf[k][:], cst[k][:], [k], [k + "_bf"])
    act(DTBA[0:32, 1:2], DTBA[0:32, 1:2], AF.Exp, ["DTBA"], ["DTBA"])
    ts("dve", DTBA[0:32, 1:2], DTBA[0:32, 1:2], -1.0, None, ALU.mult, None, ["DTBA"], ["DTBA"])
    act(LRUC[:], prm["lru_lam"][:], AF.Exp, ["p_lru_lam"], ["LRUC"], scale=-1.0)
    act(LRUC[:], LRUC[:], AF.Ln, ["LRUC", "ONEC"], ["LRUC"], bias=ONEC[:, 0:1])
    ts("dve", LRUC2[:], LRUC[:], -16.0, None, ALU.mult, None, ["LRUC"], ["LRUC2"])
    ts("dve", LRUC[:], LRUC[:], -8.0, None, ALU.mult, None, ["LRUC"], ["LRUC"])

    ones_bf = cst_bf["c_ones"]

    def xkeys(c=None):
        return [("X", c)] if c is not None else [("X", i) for i in range(16)]

    sqi = {"i": 0}

    def rms_rstd(src_fn, skeys_fn, nchunk, NT, segs, scale=1.0 / D):
        for c in range(nchunk):
            for (c0, n) in segs:
                i = sqi["i"] % 2
                sqi["i"] += 1
                act(SQ[i][:, 0:n], src_fn(c)[:, c0:c0 + n], AF.Square, skeys_fn(c), [("SQ", i)])
                mm(PA[:, c0:c0 + n], ones_bf[:, :], SQ[i][:, 0:n], c == 0, c == nchunk - 1,
                   [("SQ", i), "c_ones_bf"], pk(PA, c0, c0 + n), True)
        act(RSTD[:, 0:NT], PA[:, 0:NT], AF.Sqrt, pk(PA, 0, NT) + ["EPSC"], [("T", 0)], bias=EPSC[:, 0:1], scale=scale)
        kb.op("dve", lambda e: e.reciprocal(out=RSTD[:, 0:NT], in_=RSTD[:, 0:NT]), reads=[("T", 0)], writes=[("T", 0)])

    def norm_to_xn(gname, NT, segs):
        rms_rstd(lambda c: X[:, c, :], lambda c: xkeys(c), 16, NT, segs)
        for c in range(16):
            stt("dve" if c % 2 == 0 else "pool", XN[:, c, 0:NT], X[:, c, 0:NT], prm[gname][:, c:c + 1], RSTD[:, 0:NT],
                ALU.mult, ALU.mult, xkeys(c) + [("T", 0), "p_" + gname], [("XN", c)])

    def xn_rhs(kc, c0, n):
        return XN[:, kc, c0:c0 + n]

    xnkeys = [("XN", c) for c in range(16)]
    mixkeys = [("MIX", c) for c in range(16)]

    def ffn(pre, gname, NT, segs):
        if os.environ.get("DBG_SKIP") == "f":
            return
        scratch()
        T1 = sb("T1", [128, NT2])
        Ts = [T0, T1]
        SUB = int(os.environ.get("DBG_SUB", "99"))
        if SUB == 0:
            rms_rstd(lambda c: X[:, c, :], lambda c: xkeys(c), 16, NT, segs)
            return
        norm_to_xn(gname, NT, segs)
        if SUB == 1:
            return
        for hg in range(HG):
            for j in range(HPG):
                f = hg * HPG + j
                wplan(pre + "_wg", 0, 16, f * 128)
                wplan(pre + "_wu", 0, 16, f * 128)
            for dc in range(16):
                wplan(pre + "_wd", hg * HPG * 128, HPG, dc * 128)
        for hg in range(HG):
            for j in range(HPG):
                wg, kg = wget()
                proj(wg, kg, 16, xn_rhs, xnkeys, PA, segs)
                wu, ku = wget()
                proj(wu, ku, 16, xn_rhs, xnkeys, PB, segs)
                if SUB == 2:
                    return
                sg = Ts[j % 2]
                act(sg[:, 0:NT], PA[:, 0:NT], AF.Silu, pk(PA, 0, NT), [("T", j % 2)])
                tt("dve", HID[:, j, 0:NT], sg[:, 0:NT], PB[:, 0:NT], ALU.mult,
                   [("T", j % 2)] + pk(PB, 0, NT), [("MIX", j)])
                if SUB == 3:
                    return
            if SUB == 4:
                return
            for dc in range(16):
                wd, kd = wget()
                P = PA if dc % 2 == 0 else PB
                for j in range(HPG):
                    for (c0, n) in segs:
                        mm(P[:, c0:c0 + n], wd[:, j, :], HID[:, j, c0:c0 + n], j == 0, j == HPG - 1,
                           [kd, ("MIX", j)], pk(P, c0, c0 + n), j == HPG - 1)
                stt("dve", X[:, dc, 0:NT], P[:, 0:NT], 0.5, X[:, dc, 0:NT], ALU.mult, ALU.add,
                    pk(P, 0, NT) + xkeys(dc), xkeys(dc))
        kb.barrier()

    def conv_silu(P, cw, cb, ci, tail_init, tname, convs, cname, NT, full, out_u, ukey, tail_save, tsname,
                  o_p, opname, o_s, osname, silu, EXTP, EXTS):
        pkeys = pk(P, 0, NT)
        if tail_init is None:
            memset("pool", EXTP[:, 0:3], 0.0, ["EXTP"])
        else:
            ts("pool", EXTP[:, 0:3], tail_init[:, ci, :], FLAG[:, 0:1], None, ALU.mult, None, [tname, "FLAG"], ["EXTP"])
        cp("act", EXTP[:, 3:3 + NP1], P[:, 0:NP1], pkeys, ["EXTP"])
        if tail_save is not None:
            cp("pool", tail_save[:, ci, :], EXTP[:, NP1:NP1 + 3], ["EXTP"], [tsname])
        if o_p is not None:
            cp("pool", o_p[:, ci, :], EXTP[:, NP1:NP1 + 3], ["EXTP"], [opname])
        up = out_u[:, 0:NP1]
        ts("dve", up, EXTP[:, 0:NP1], cw[:, ci, 0:1], cb[:, ci:ci + 1], ALU.mult, ALU.add, ["EXTP"], [ukey])
        for k in range(1, 4):
            stt("dve", up, EXTP[:, k:k + NP1], cw[:, ci, k:k + 1], up, ALU.mult, ALU.add, ["EXTP", ukey], [ukey])
        if full:
            kb.dma("sp", EXTS[:, :, 0:3], convs[:, ci, :, :], "ld_cv3", writes=["EXTS"])
            cp("act", EXTS[:, :, 3:11], P[:, NP1:NT2].rearrange("p (s t) -> p s t", t=8), pkeys, ["EXTS"])
            kb.dma("sp", o_s[:, ci, :, :], EXTS[:, :, 8:11], "st_cv3", reads=["EXTS"])
            us = out_u[:, NP1:NT2].rearrange("p (s t) -> p s t", t=8)
            ts("dve", us, EXTS[:, :, 0:8], cw[:, ci, 0:1], cb[:, ci:ci + 1], ALU.mult, ALU.add, ["EXTS", ukey], [ukey])
            for k in range(1, 4):
                stt("dve", us, EXTS[:, :, k:k + 8], cw[:, ci, k:k + 1], us, ALU.mult, ALU.add, ["EXTS", ukey], [ukey])
        if silu:
            act(out_u[:, 0:NT], out_u[:, 0:NT], AF.Silu, [ukey], [ukey])

    def wout_half(r0, NT, segs, wname="w_out"):
        for dc in range(16):
            wplan(wname, r0, 16, dc * 128)
        for dc in range(16):
            w, k = wget()
            P = PA if dc % 2 == 0 else PB
            proj(w, k, 16, lambda kc, c0, n: MIX[:, kc, c0:c0 + n], mixkeys, P, segs)
            tt("dve", X[:, dc, 0:NT], X[:, dc, 0:NT], P[:, 0:NT], ALU.add, xkeys(dc) + pk(P, 0, NT), xkeys(dc))

    def lru_phase(blk, NT, segs):
        full = blk == 2
        scratch()
        T1 = sb("T1", [128, NT2])
        T2 = sb("T2", [128, NT2])
        T3 = sb("T3", [128, NT2])
        UB = sb("UB", [128, NT2], BF16)
        GL = sb("GL", [128, NT2], BF16)
        EXTP = sb("EXTP", [128, 3 + NP1])
        EXTS = sb("EXTS", [128, NSEQ, 11])
        for j in range(16):
            if full:
                wplan("w_in", 0, 16, 2048 + j * 128)
            wplan("w_in", 0, 16, j * 128)
            wplan("lru_wa", j * 128, 1, 0)
            wplan("lru_wx", j * 128, 1, 0)
        if full:
            ts("dve", HINIT[:, :], HEND1[:, :], FLAG[:, 0:1], None, ALU.mult, None, ["HEND1", "FLAG"], ["HINIT"])
        for j in range(16):
            if full:
                wg_, kg_ = wget()
                proj(wg_, kg_, 16, xn_rhs, xnkeys, PB, segs)
                act(GL[:, 0:NT], PB[:, 0:NT], AF.Gelu, pk(PB, 0, NT), ["GL"])
            wx_, kx_ = wget()
            proj(wx_, kx_, 16, xn_rhs, xnkeys, PA, segs)
            U = T0
            conv_silu(PA, prm["lru_cw"], prm["lru_cb"], j, TAIL_L if full else None, "TAIL_L", slc_d, "CONVS_L", NT, full,
                      U, ("T", 0), None if full else TAIL_L, "TAIL_L", O_PLC if full else None, "O_PLC",
                      o_slc if full else None, "O_SLC", False, EXTP, EXTS)
            cp("act", UB[:, 0:NT], U[:, 0:NT], [("T", 0)], ["UB"])
            wa_, ka_ = wget()
            wxx_, kxx_ = wget()
            for (c0, n) in segs:
                mm(PA[:, c0:c0 + n], wa_[:, 0, :], UB[:, c0:c0 + n], True, True, [ka_, "UB"], pk(PA, c0, c0 + n), True)
            for (c0, n) in segs:
                mm(PB[:, c0:c0 + n], wxx_[:, 0, :], UB[:, c0:c0 + n], True, True, [kxx_, "UB"], pk(PB, c0, c0 + n), True)
            R, I, A, HS = T1, T2, T3, T0
            act(R[:, 0:NT], PA[:, 0:NT], AF.Sigmoid, pk(PA, 0, NT), [("T", 1)], bias=prm["lru_ba"][:, j:j + 1])
            act(I[:, 0:NT], PB[:, 0:NT], AF.Sigmoid, pk(PB, 0, NT), [("T", 2)], bias=prm["lru_bx"][:, j:j + 1])
            act(A[:, 0:NT], R[:, 0:NT], AF.Exp, [("T", 1), "LRUC"], [("T", 3)], scale=LRUC[:, j:j + 1])
            act(R[:, 0:NT], R[:, 0:NT], AF.Exp, [("T", 1), "LRUC2"], [("T", 1)], scale=LRUC2[:, j:j + 1])
            act(R[:, 0:NT], R[:, 0:NT], AF.Sqrt, [("T", 1), "ONEC"], [("T", 1)], bias=ONEC[:, 0:1], scale=-1.0)
            tt("pool", I[:, 0:NT], I[:, 0:NT], U[:, 0:NT], ALU.mult, [("T", 2), ("T", 0)], [("T", 2)])
            tt("dve", R[:, 0:NT], R[:, 0:NT], I[:, 0:NT], ALU.mult, [("T", 1), ("T", 2)], [("T", 1)])
            init = HINIT[:, j:j + 1] if full else 0.0
            kb.op("dve", lambda e, init=init: e.tensor_tensor_scan(out=HS[:, 0:NP1], data0=A[:, 0:NP1], data1=R[:, 0:NP1],
                                                                  initial=init, op0=ALU.mult, op1=ALU.add),
                  reads=[("T", 3), ("T", 1), "HINIT"], writes=[("T", 0)])
            if not full:
                cp("pool", HEND1[:, j:j + 1], HS[:, NP1 - 1:NP1], [("T", 0)], ["HEND1"])
                continue
            cp("pool", O_PLH[:, j:j + 1], HS[:, NP1 - 1:NP1], [("T", 0)], ["O_PLH"])
            for s in range(NSEQ):
                c0 = NP1 + 8 * s
                kb.op("dve", lambda e, c0=c0, s=s, j=j: e.tensor_tensor_scan(
                    out=HS[:, c0:c0 + 8], data0=A[:, c0:c0 + 8], data1=R[:, c0:c0 + 8],
                    initial=H0S_L[:, j, s:s + 1], op0=ALU.mult, op1=ALU.add),
                    reads=[("T", 3), ("T", 1), "H0S_L"], writes=[("T", 0)])
            cp("pool", O_SLH[:, j, :], HS[:, NP1:NT2].rearrange("p (s t) -> p s t", t=8)[:, :, 7], [("T", 0)], ["O_SLH"])
            tt("dve", MIX[:, j, 0:NT], HS[:, 0:NT], GL[:, 0:NT], ALU.mult, [("T", 0), "GL"], [("MIX", j)])
        kb.barrier()
        if not full:
            return
        rms_rstd(lambda c: MIX[:, c, :], lambda c: [("MIX", c)], 16, NT, segs)
        for c in range(16):
            stt("dve" if c % 2 == 0 else "pool", MIX[:, c, 0:NT], MIX[:, c, 0:NT], prm["lru_og"][:, c:c + 1], RSTD[:, 0:NT],
                ALU.mult, ALU.mult, [("MIX", c), ("T", 0)], [("MIX", c)])
        wout_half(0, NT, segs)
        kb.barrier()

    def ssd_phase(blk, NT, segs):
        full = blk == 2
        ntile = NT // 128
        scratch()
        ZS = sb("ZS", [128, NT2], BF16)
        EXTP = sb("EXTP", [128, 3 + NP1])
        EXTS = sb("EXTS", [128, NSEQ, 11])
        CT1 = sb("CT1", [128, NT2], BF16)
        BTOK1 = sb("BTOK1", [128, 9, 128], BF16)
        CBM1 = sb("CBM1", [128, 9, 128], BF16)
        DT_TOK = sb("DT_TOK", [128, 9, 32])
        DA_TOK = sb("DA_TOK", [128, 9, 32])
        CS_TOK = sb("CS_TOK", [128, 9, 32])
        CSL_BC = sb("CSL_BC", [128, 9, 32])
        TOEND = sb("TOEND", [128, 9, 32])
        DEC_BC = sb("DEC_BC", [128, 9, 32])
        XPAD = sb("XPAD", [128, 2, 128], BF16)
        XSC = sb("XSC", [128, 128], BF16)
        XSCM = sb("XSCM", [128, 2, 128], BF16)
        SM = [sb("SM%d" % i, [128, 128]) for i in range(8)]
        SMB = [sb("SMB%d" % i, [128, 128], BF16) for i in range(2)]
        HT = sb("HT", [128, 128])
        HTB = sb("HTB", [128, 128], BF16)
        H0N = sb("H0N", [128, 2, 128])
        H0T = sb("H0T", [128, 2, 128])
        H0TB = sb("H0TB", [128, 2, 128], BF16)
        H1N = sb("H1N", [128, 2, 128])
        DECF = sb("DECF", [128, NSEQ])
        csz = NT2 * 2
        DTFv, _ = view(OFF["MIX"] + 7 * csz, [NT2], F32)
        DAFv, _ = view(OFF["MIX"] + 9 * csz, [NT2], F32)
        BTs = [MIX[:, 11, :], MIX[:, 12, :]]
        CTs = [MIX[:, 13, :], CT1[:, :]]
        BTOKs = [MIX[:, 14, :].rearrange("p (t n) -> p t n", n=128), BTOK1]
        CBMs = [MIX[:, 15, :].rearrange("p (t n) -> p t n", n=128), CBM1]
        kDTF = [("MIX", 7), ("MIX", 8)]
        kDAF = [("MIX", 9), ("MIX", 10)]
        kBT = [[("MIX", 11)], [("MIX", 12)]]
        kCT = [[("MIX", 13)], ["CT1"]]
        kBTOK = [[("MIX", 14)], ["BTOK1"]]
        kCBM = [[("MIX", 15)], ["CBM1"]]
        ut, slt, uts, same, onesf, seqm, ident = (cst["c_ut"], cst["c_slt"], cst["c_uts"], cst["c_same"],
                                                  cst["c_ones"], cst["c_seq"], cst["c_ident"])
        XS = T0
        for ci in (16, 17, 18, 19):
            wplan("w_in", 0, 16, 6144 + ci * 128)
        if os.environ.get("DBG_NODT") is None:
            wplan("w_in", 0, 16, 8704, 32)
        for pr in range(16):
            wplan("w_in", 0, 16, 6144 + pr * 128)
            if full:
                wplan("w_in", 0, 16, 4096 + pr * 128)
        memset("pool", XPAD[:, :, :], 0.0, ["XPAD"])

        def do_conv(P, ci):
            conv_silu(P, prm["ssd_cw"], prm["ssd_cb"], ci, TAIL_S if full else None, "TAIL_S", ssc_d, "CONVS_S", NT, full,
                      XS, ("T", 0), None if full else TAIL_S, "TAIL_S", O_PSC if full else None, "O_PSC",
                      o_ssc if full else None, "O_SSC", True, EXTP, EXTS)

        for ci in (16, 17, 18, 19):
            w, k = wget()
            proj(w, k, 16, xn_rhs, xnkeys, PA, segs)
            do_conv(PA, ci)
            g = (ci - 16) % 2
            if ci < 18:
                cp("pool", BTs[g][:, 0:NT], XS[:, 0:NT], [("T", 0)], kBT[g])
                for t in range(ntile):
                    so, sk = sslot(t % 8)
                    kb.op("pe", lambda e, so=so, t=t: e.transpose(so, XS[:, t * 128:(t + 1) * 128], ident[:, :]),
                          reads=[("T", 0), "c_ident"], writes=[sk])
                    cp("act" if t % 2 == 0 else "dve", BTOKs[g][:, t, :], so, [sk], kBTOK[g])
            else:
                cp("pool", CTs[g][:, 0:NT], XS[:, 0:NT], [("T", 0)], kCT[g])
        SUB = int(os.environ.get("DBG_SUB", "99"))
        if SUB == 0:
            return
        w, k = wget()
        proj(w, k, 16, xn_rhs, xnkeys, PA, segs, M=32)
        if SUB == 1:
            return
        act(DTFv[0:32, 0:NT], PA[0:32, 0:NT], AF.Exp, pk(PA, 0, NT) + ["DTBA"], kDTF, bias=DTBA[0:32, 0:1])
        act(DTFv[0:32, 0:NT], DTFv[0:32, 0:NT], AF.Ln, kDTF + ["ONEC"], kDTF, bias=ONEC[0:32, 0:1])
        ts("dve", DAFv[0:32, 0:NT], DTFv[0:32, 0:NT], DTBA[0:32, 1:2], None, ALU.mult, None, kDTF + ["DTBA"], kDAF)
        if SUB == 2:
            return
        for t in range(ntile):
            so, sk = sslot(t % 8)
            kb.op("pe", lambda e, so=so, t=t: e.transpose(so[:, 0:32], DTFv[0:32, t * 128:(t + 1) * 128], ident[0:32, 0:32]),
                  reads=kDTF + ["c_ident"], writes=[sk], inc=False)
            kb.op("pe", lambda e, so=so, t=t: e.transpose(so[:, 32:64], DAFv[0:32, t * 128:(t + 1) * 128], ident[0:32, 0:32]),
                  reads=kDAF + ["c_ident"], writes=[sk])
            cp("act", DT_TOK[:, t, :], so[:, 0:32], [sk], ["DT_TOK"])
            cp("act", DA_TOK[:, t, :], so[:, 32:64], [sk], ["DA_TOK"])
        if SUB == 3:
            return
        for t in range(ntile):
            samp = full and t == 8
            so, sk = sslot(t % 8)
            mm(so[:, 0:32], (uts if samp else ut)[:, :], DA_TOK[:, t, :], True, True, ["DA_TOK", "c_ut", "c_uts"], [sk], False)
            mm(so[:, 32:64], (same if samp else onesf)[:, :], DA_TOK[:, t, :], True, True, ["DA_TOK", "c_same", "c_ones"], [sk], True)
            cp("act", CS_TOK[:, t, :], so[:, 0:32], [sk], ["CS_TOK"])
            cp("act", CSL_BC[:, t, :], so[:, 32:64], [sk], ["CSL_BC"])
        tt("dve", TOEND[:, 0:ntile, :], CSL_BC[:, 0:ntile, :], CS_TOK[:, 0:ntile, :], ALU.subtract, ["CSL_BC", "CS_TOK"], ["TOEND"])
        act(TOEND[:, 0:ntile, :], TOEND[:, 0:ntile, :], AF.Exp, ["TOEND"], ["TOEND"])
        tt("dve", TOEND[:, 0:ntile, :], TOEND[:, 0:ntile, :], DT_TOK[:, 0:ntile, :], ALU.mult, ["TOEND", "DT_TOK"], ["TOEND"])
        act(DEC_BC[:, 0:ntile, :], CSL_BC[:, 0:ntile, :], AF.Exp, ["CSL_BC"], ["DEC_BC"])
        if SUB == 4:
            return
        if full:
            for g in range(2):
                for t in range(ntile):
                    samp = t == 8
                    so, sk = sslot((g * ntile + t) % 8)
                    mm(so, BTs[g][:, t * 128:(t + 1) * 128], CTs[g][:, t * 128:(t + 1) * 128], True, True,
                       kBT[g] + kCT[g], [sk], True)
                    tt("dve", CBMs[g][:, t, :], so, (uts if samp else ut)[:, :], ALU.mult, [sk, "c_ut", "c_uts"], kCBM[g])
        if SUB == 5:
            return
        for pr in range(16):
            if SUB in (6, 7) and pr == 1:
                return
            g = pr // 8
            h0 = 2 * pr
            w, k = wget()
            proj(w, k, 16, xn_rhs, xnkeys, PA, segs)
            do_conv(PA, pr)
            if full:
                w, k = wget()
                proj(w, k, 16, xn_rhs, xnkeys, PB, segs)
                act(ZS[:, 0:NT], PB[:, 0:NT], AF.Silu, pk(PB, 0, NT), ["ZS"])
                if os.environ.get("DBG_NOHSP"):
                    memset("dve", HT[:, :], 0.0, ["HT"])
                else:
                    kb.dma("sp", HT[:, :], hspill[:, pr, :], "ld_hsp", reads=["hspill"], writes=["HT"])
                    ts("dve", HT[:, :], HT[:, :], FLAG[:, 0:1], None, ALU.mult, None, ["HT", "FLAG"], ["HT"])
            else:
                memset("dve", HT[:, :], 0.0, ["HT"])
            for t in range(ntile):
                samp = full and t == 8
                if samp and SUB == 6:
                    continue
                U_ = uts if samp else ut
                cols = slice(t * 128, (t + 1) * 128)
                sx, kx = sslot(6 + t % 2)
                kb.op("pe", lambda e, sx=sx, t=t: e.transpose(sx, XS[:, t * 128:(t + 1) * 128], ident[:, :]),
                      reads=[("T", 0), "c_ident"], writes=[kx])
                if full and not os.environ.get("DBG_NOXPAD"):
                    cp("act", XPAD[:, 0, 0:64], sx[:, 0:64], [kx], ["XPAD"])
                    cp("act", XPAD[:, 1, 64:128], sx[:, 64:128], [kx], ["XPAD"])
                for hh_ in range(2):
                    act(XSC[:, hh_ * 64:(hh_ + 1) * 64], sx[:, hh_ * 64:(hh_ + 1) * 64], AF.Copy, [kx, "TOEND"], ["XSC"],
                        scale=TOEND[:, t, h0 + hh_:h0 + hh_ + 1])
                if SUB == 61:
                    return
                if full:
                    DABC, ECS = SM[0], SM[1]
                    cp("dve", DABC[:, :].rearrange("p (h q) -> p h q", q=64),
                       DA_TOK[:, t, h0:h0 + 2].unsqueeze(2).to_broadcast([128, 2, 64]), ["DA_TOK"], [("SM", 0)])
                    s0, k0 = sslot(0)
                    mm(s0, DABC[:, :], U_[:, :], True, True, [("SM", 0), "c_ut", "c_uts"], [k0], True)
                    act(ECS[:, :], s0, AF.Exp, [k0], [("SM", 1)])
                    if SUB == 62:
                        return
                    sy, ky = sslot(3)
                    for hh in range(2):
                        LM, EH, WDT = SM[2 + hh], SM[4 + hh], SMB[hh]
                        ts("dve", LM[:, :], slt[:, :], DA_TOK[:, t, h0 + hh:h0 + hh + 1], None, ALU.mult, None,
                           ["c_slt", "DA_TOK"], [("SM", 2 + hh)])
                        sd, kd = sslot(1 + hh)
                        mm(sd, LM[:, :], U_[:, :], True, True, [("SM", 2 + hh), "c_ut", "c_uts"], [kd], True)
                        act(EH[:, :], sd, AF.Exp, [kd], [("SM", 4 + hh)])
                        stt("dve", WDT[:, :], EH[:, :], DT_TOK[:, t, h0 + hh:h0 + hh + 1], CBMs[g][:, t, :], ALU.mult, ALU.mult,
                            [("SM", 4 + hh), "DT_TOK"] + kCBM[g], [("SMB", hh)])
                        mm(sy, XPAD[:, hh, :], WDT[:, :], hh == 0, hh == 1, ["XPAD", ("SMB", hh)], [ky], hh == 1)
                    if SUB == 63:
                        return
                    sr, kr = sslot(4)
                    if not samp:
                        cp("act", HTB[:, :], HT[:, :], ["HT"], ["HTB"])
                        mm(sr, HTB[:, :], CTs[g][:, cols], True, True, ["HTB"] + kCT[g], [kr], True)
                if SUB == 64:
                    return
                if not samp:
                    sh, kh = sslot(5)
                    mm(sh, BTOKs[g][:, t, :], XSC[:, :], True, True, kBTOK[g] + ["XSC"], [kh], True)
                    tt("dve", HT[:, :].rearrange("p (h q) -> p h q", q=64), HT[:, :].rearrange("p (h q) -> p h q", q=64),
                       DEC_BC[:, t, h0:h0 + 2].unsqueeze(2).to_broadcast([128, 2, 64]), ALU.mult, ["HT", "DEC_BC"], ["HT"])
                    tt("dve", HT[:, :], HT[:, :], sh, ALU.add, ["HT", kh], ["HT"])
                else:
                    s5, k5 = sslot(5)
                    mm(s5[:, 0:NSEQ], DABC[:, :], seqm[:, :], True, True, [("SM", 0), "c_seq"], [k5], True)
                    act(DECF[:, :], s5[:, 0:NSEQ], AF.Exp, [k5], ["DECF"])
                    for q in range(8):
                        kb.dma("sp", H0T[:, :, :], sshT_d[pr][:, 2 * q:2 * q + 2, :], "ld_h0t", writes=["H0T"])
                        kb.dma("sp", H0N[:, :, :], sshN_d[pr][:, 2 * q:2 * q + 2, :], "ld_h0n", writes=["H0N"])
                        cp("act", H0TB[:, :, :], H0T[:, :, :], ["H0T"], ["H0TB"])
                        for u in range(2):
                            s = 2 * q + u
                            mm(sr[:, 8 * s:8 * s + 8], H0TB[:, u, :], CTs[g][:, NP1 + 8 * s:NP1 + 8 * s + 8], True, True,
                               ["H0TB"] + kCT[g], [kr], s == NSEQ - 1 or u == 1)
                        tt("dve", XSCM[:, :, :], XSC[:, :].unsqueeze(1).to_broadcast([128, 2, 128]),
                           seqm[:, 2 * q:2 * q + 2].unsqueeze(2).to_broadcast([128, 2, 128]), ALU.mult, ["XSC", "c_seq"], ["XSCM"])
                        bo = (q % 2) * 512
                        for u in range(2):
                            mm(PA[:, bo + u * 128:bo + (u + 1) * 128], XSCM[:, u, :], BTOKs[g][:, 8, :], True, True,
                               ["XSCM"] + kBTOK[g], pk(PA, bo, bo + 512), u == 1)
                        for u in range(2):
                            s = 2 * q + u
                            stt("dve", H1N[:, u, :], H0N[:, u, :], DECF[:, s:s + 1], PA[:, bo + u * 128:bo + (u + 1) * 128],
                                ALU.mult, ALU.add, ["H0N", "DECF"] + pk(PA, bo, bo + 512), ["H1N"])
                        kb.dma("sp", o_ssh[pr][:, 2 * q:2 * q + 2, :], H1N[:, :, :], "st_h1", reads=["H1N"])
                    outkeys.append("H1N")
                if full:
                    T1_, Y2 = SM[6], SM[7]
                    tt("dve", T1_[:, :], sr, ECS[:, :], ALU.mult, [kr, ("SM", 1)], [("SM", 6)])
                    stt("dve", Y2[:, :], XS[:, cols], prm["ssd_dfm"][:, pr:pr + 1], sy, ALU.mult, ALU.add,
                        [("T", 0), ky], [("SM", 7)])
                    tt("pool", Y2[:, :], Y2[:, :], T1_[:, :], ALU.add, [("SM", 7), ("SM", 6)], [("SM", 7)])
                    tt("pool", MIX[:, pr, cols], Y2[:, :], ZS[:, cols], ALU.mult, [("SM", 7), "ZS"], [("MIX", pr)])
            if full:
                kb.dma("sp", o_pshT[:, pr, :], HT[:, :], "st_psh", reads=["HT"])
                outkeys.append("HT")
            else:
                kb.dma("sp", hspill[:, pr, :], HT[:, :], "st_hsp", reads=["HT"], writes=["hspill"])
        kb.barrier()
        if not full:
            return
        for g in range(2):
            rms_rstd(lambda c, g=g: MIX[:, g * 8 + c, :], lambda c, g=g: [("MIX", g * 8 + c)], 8, NT, segs, scale=1.0 / 1024)
            for c in range(8 * g, 8 * g + 8):
                stt("dve" if c % 2 == 0 else "pool", MIX[:, c, 0:NT], MIX[:, c, 0:NT], prm["ssd_og"][:, c:c + 1], RSTD[:, 0:NT],
                    ALU.mult, ALU.mult, [("MIX", c), ("T", 0)], [("MIX", c)])
        wout_half(2048, NT, segs)
        kb.barrier()

    def xattn(NT, segs):
        norm_to_xn("xattn_g", NT, segs)
        for dc in range(16):
            wplan("w_q", 0, 16, dc * 128)
        for nm in ("w_k", "w_v"):
            for dc in range(16):
                wplan(nm, 0, 16, dc * 128)
        ident = cst["c_ident"]
        for dc in range(16):
            w, k = wget()
            P = PA if dc % 2 == 0 else PB
            proj(w, k, 16, xn_rhs, xnkeys, P, segs)
            cp("act" if dc % 2 == 0 else "dve", MIX[:, dc, 0:NT], P[:, 0:NT], pk(P, 0, NT), [("MIX", dc)])
        kb.barrier()
        scratch(OFF["XN"])
        KT = sb("KT", [128, 16, 256], BF16)
        VTOK = sb("VTOK", [128, 2, D], BF16)
        MEMN = sb("MEMN", [128, 16, 256], BF16)
        ET = sb("ET", [128, 2, NT2], BF16)
        assert bump["p"] <= OFF["XN"] + 16 * NT2 * 2
        scratch()
        MRS = sb("MRS", [128, 256])
        MEMX = [sb("MEMX%d" % i, [128, 256]) for i in range(2)]
        STG = [sb("STG%d" % i, [128, 256]) for i in range(2)]
        KTS = [sb("KTS%d" % i, [128, 4, 256], BF16) for i in range(2)]
        VS = [sb("VS%d" % i, [128, 2, 512], BF16) for i in range(2)]
        for c in range(16):
            kb.dma("sp", MEMX[c % 2][:, :], mem_d[:, c, :], ("ld_mem", c % 2), writes=[("MEMX", c % 2)])
            act(SQ[c % 2][:, 0:256], MEMX[c % 2][:, :], AF.Square, [("MEMX", c % 2)], [("SQ", c % 2)])
            mm(PA[:, 0:256], ones_bf[:, :], SQ[c % 2][:, 0:256], c == 0, c == 15, [("SQ", c % 2), "c_ones_bf"], pk(PA, 0, 256), True)
        act(MRS[:, :], PA[:, 0:256], AF.Sqrt, pk(PA, 0, 256) + ["EPSC"], ["MRS"], bias=EPSC[:, 0:1], scale=1.0 / D)
        kb.op("dve", lambda e: e.reciprocal(out=MRS[:, :], in_=MRS[:, :]), reads=["MRS"], writes=["MRS"])
        for c in range(16):
            kb.dma("sp", MEMX[c % 2][:, :], mem_d[:, c, :], ("ld_mem", c % 2), writes=[("MEMX", c % 2)])
            stt("dve", MEMN[:, c, :], MEMX[c % 2][:, :], prm["mem_g"][:, c:c + 1], MRS[:, :], ALU.mult, ALU.mult,
                [("MEMX", c % 2), "MRS"], ["MEMN"])
        msegs = [(0, 256)]
        si = 0
        for nm in ("w_k", "w_v"):
            for dc in range(16):
                w, k = wget()
                P = PA if dc % 2 == 0 else PB
                proj(w, k, 16, lambda kc, c0, n: MEMN[:, kc, c0:c0 + n], ["MEMN"], P, msegs)
                st = STG[si % 2]
                sk_ = ("STG", si % 2)
                si += 1
                cp("act", st[:, :], P[:, 0:256], pk(P, 0, 256), [sk_])
                if nm == "w_k":
                    cp("dve", KT[:, dc, :], st[:, :], [sk_], ["KT"])
                    kb.dma("sp", o_mk[:, dc, :], st[:, :], ("st_kv", (si - 1) % 2), reads=[sk_])
                else:
                    kb.dma("sp", o_mv[:, dc, :], st[:, :], ("st_kv", (si - 1) % 2), reads=[sk_])
                    for mt in range(2):
                        so, sk = sslot((dc * 2 + mt) % 8)
                        kb.op("pe", lambda e, so=so, st=st, mt=mt: e.transpose(so, st[:, mt * 128:(mt + 1) * 128], ident[:, :]),
                              reads=[sk_, "c_ident"], writes=[sk])
                        cp("act" if mt == 0 else "pool" if False else "dve", VTOK[:, mt, dc * 128:(dc + 1) * 128], so, [sk], ["VTOK"])
        outkeys.extend([("STG", 0), ("STG", 1)])
        sc = 512.0 ** -0.5
        psegs = [(0, 512), (512, 512)]
        li = 0
        for h in range(4):
            for mt in range(2):
                P = PA if mt == 0 else PB
                for kc in range(4):
                    for (c0, n) in psegs:
                        mm(P[:, c0:c0 + n], KT[:, 4 * h + kc, mt * 128:(mt + 1) * 128], MIX[:, 4 * h + kc, c0:c0 + n],
                           kc == 0, kc == 3, ["KT", ("MIX", 4 * h + kc)], pk(P, c0, c0 + n), kc == 3)
            for s in range(NSEQ):
                b = li % 2
                li += 1
                kb.dma("pool", KTS[b][:, :, :], ckT_d[s][:, 4 * h:4 * h + 4, :], ("ld_ck", b), writes=[("KTS", b)])
                for mt in range(2):
                    P = PA if mt == 0 else PB
                    for kc in range(4):
                        mm(P[:, 1024 + 8 * s:1024 + 8 * s + 8], KTS[b][:, kc, mt * 128:(mt + 1) * 128],
                           MIX[:, 4 * h + kc, NP1 + 8 * s:NP1 + 8 * s + 8], kc == 0, kc == 3,
                           [("KTS", b), ("MIX", 4 * h + kc)], pk(P, 1024, 1152), kc == 3)
            for mt in range(2):
                P = PA if mt == 0 else PB
                act(ET[:, mt, 0:NT], P[:, 0:NT], AF.Exp, pk(P, 0, NT), [("ET", mt)], scale=sc)
            for (c0, n) in segs:
                for mt in range(2):
                    mm(PA[:, c0:c0 + n], ones_bf[:, :], ET[:, mt, c0:c0 + n], mt == 0, mt == 1,
                       [("ET", mt), "c_ones_bf"], pk(PA, c0, c0 + n), mt == 1)
            kb.op("dve", lambda e: e.reciprocal(out=RSTD[:, 0:NT], in_=PA[:, 0:NT]), reads=pk(PA, 0, NT), writes=[("T", 0)])
            for s in range(NSEQ):
                b = s % 2
                kb.dma("pool", VS[b][:, :, :], cv_d[s][:, :, h * 512:(h + 1) * 512], ("ld_cv", b), writes=[("VS", b)])
                for dcl in range(4):
                    for mt in range(2):
                        mm(PB[:, 1024 + dcl * 128 + 8 * s:1024 + dcl * 128 + 8 * s + 8], VS[b][:, mt, dcl * 128:(dcl + 1) * 128],
                           ET[:, mt, NP1 + 8 * s:NP1 + 8 * s + 8], mt == 0, mt == 1, [("VS", b), ("ET", mt)],
                           pk(PB, 1024, 1536), mt == 1)
            for dcl in range(4):
                dc = 4 * h + dcl
                tt("dve", MIX[:, dc, NP1:NT2], PB[:, 1024 + dcl * 128:1024 + dcl * 128 + 128], RSTD[:, NP1:NT2], ALU.mult,
                   pk(PB, 1024, 1536) + [("T", 0)], [("MIX", dc)])
            for dcl in range(4):
                dc = 4 * h + dcl
                for mt in range(2):
                    for (c0, n) in psegs:
                        mm(PB[:, c0:c0 + n], VTOK[:, mt, dc * 128:(dc + 1) * 128], ET[:, mt, c0:c0 + n], mt == 0, mt == 1,
                           ["VTOK", ("ET", mt)], pk(PB, c0, c0 + n), mt == 1)
                tt("dve", MIX[:, dc, 0:NP1], PB[:, 0:NP1], RSTD[:, 0:NP1], ALU.mult, pk(PB, 0, NP1) + [("T", 0)], [("MIX", dc)])
        kb.barrier()
        wout_half(0, NT, segs, wname="w_o")
        kb.barrier()

    segs1 = [(0, 512), (512, 512)]
    segs2 = [(0, 512), (512, 512), (1024, 128)]

    def finish(dump=None):
        if dump is not None:
            kb.barrier()
            ncd = NP1 if stage < 4 else NT2
            for c in range(16):
                kb.dma("sp", y_d[:, c, 0:ncd], X[:, c, 0:ncd], "st_dbg", reads=xkeys(c))
            outkeys.extend(xkeys())
            if dump == "xn" or dump == "mix":
                src = XN if dump == "xn" else MIX
                for c in range(16):
                    kb.dma("pool", dbg_d[:, c, 0:ncd], src[:, c, 0:ncd], "st_dbg2", reads=[(dump.upper(), c)])
                outkeys.extend([(dump.upper(), c) for c in range(16)])
        olist = ((o_plc, O_PLC, "O_PLC"), (o_plh, O_PLH, "O_PLH"), (o_psc, O_PSC, "O_PSC"), (o_slh, O_SLH, "O_SLH"))
        if stage in (2, 3):
            olist = ((o_plc, TAIL_L, "TAIL_L"), (o_plh, HEND1, "HEND1"), (o_psc, TAIL_S, "TAIL_S"))
            if stage == 2:
                olist = olist[0:2]
            if stage == 3 and os.environ.get("DBG_SUB") is None:
                kb.dma("sp", o_pshT, hspill, "st_misc", reads=["hspill"])
        for (dst, src, key) in olist:
            kb.dma("sp", dst, src, "st_misc", reads=[key])
            outkeys.append(key)
        outkeys.append("EXTS")
        kb.final_wait("sp", outkeys)
        kb.emit()
        return nc, kb

    dbg_d = dout("dbg", [128, 16, NT2]) if stage < 99 else None
    for c in range(16):
        kb.dma("sp", X[:, c, 0:NP1], xa_d[:, c, :], "ld_x", writes=xkeys(c))
    if stage == 0:
        return finish("x")
    SKIP = os.environ.get("DBG_SKIP") is not None
    if not SKIP:
        ffn("ffn1", "ffn1_g", NP1, segs1)
    if stage == 1:
        return finish("xn")
    norm_to_xn("mix_g", NP1, segs1)
    if not SKIP or os.environ.get("DBG_SKIP") == "f":
        lru_phase(1, NP1, segs1)
    if stage == 2:
        return finish("xn")
    ssd_phase(1, NP1, segs1)
    if stage == 3:
        return finish("xn")
    for c in range(16):
        kb.dma("sp", X[:, c, :], xb_d[:, c, :], "ld_x", writes=xkeys(c))
    ffn("ffn1", "ffn1_g", NT2, segs2)
    if stage == 4:
        return finish("xn")
    norm_to_xn("mix_g", NT2, segs2)
    lru_phase(2, NT2, segs2)
    if stage == 5:
        return finish("mix")
    ssd_phase(2, NT2, segs2)
    if stage == 6:
        return finish("mix")
    xattn(NT2, segs2)
    if stage == 7:
        return finish("mix")
    ffn("ffn2", "ffn2_g", NT2, segs2)
    scratch()
    OST = [sb("OST%d" % i, [128, NT2]) for i in range(3)]
    rms_rstd(lambda c: X[:, c, :], lambda c: xkeys(c), 16, NT2, segs2)
    for c in range(16):
        o = OST[c % 3]
        stt("dve", o[:, :], X[:, c, :], prm["final_g"][:, c:c + 1], RSTD[:, :], ALU.mult, ALU.mult,
            xkeys(c) + [("T", 0)], [("OST", c % 3)])
        kb.dma("sp", y_d[:, c, :], o[:, :], ("st_y", c % 3), reads=[("OST", c % 3)])
    outkeys.extend([("OST", i) for i in range(3)])
    return finish(None)


_CACHE = {}


def _fm(v, shape_tail=()):
    v = np.asarray(v, np.float32)
    C = v.shape[0] // 128
    return np.ascontiguousarray(np.moveaxis(v.reshape((C, 128) + v.shape[1:]), 0, 1))


def _tok_fm(x):
    T_ = x.shape[0]
    return np.ascontiguousarray(x.reshape(T_, 16, 128).transpose(2, 1, 0))


def _fm_tok(y):
    return np.ascontiguousarray(y.transpose(2, 1, 0).reshape(y.shape[2], -1))


def make_in_maps(inp):
    f = lambda k: np.asarray(inp[k], np.float32)

    x_prompt, mem_prompt, x_sample = f("x_prompt"), f("mem_prompt"), f("x_sample")
    ck, cv = f("cache_mem_k")[0], f("cache_mem_v")[0]
    slc, slh, ssc, ssh = f("state_lru_conv")[0], f("state_lru_h")[0], f("state_ssd_conv")[0], f("state_ssd_h")[0]

    shared = {
        "ffn1_wg": f("ffn1_w_gate")[0], "ffn1_wu": f("ffn1_w_up")[0], "ffn1_wd": f("ffn1_w_down")[0],
        "ffn2_wg": f("ffn2_w_gate")[0], "ffn2_wu": f("ffn2_w_up")[0], "ffn2_wd": f("ffn2_w_down")[0],
        "w_in": f("w_in")[0], "w_out": f("w_out")[0],
        "w_q": f("xattn_w_q")[0], "w_k": f("xattn_w_k")[0], "w_v": f("xattn_w_v")[0], "w_o": f("xattn_w_o")[0],
        "lru_wa": f("lru_w_a")[0].reshape(D, 128), "lru_wx": f("lru_w_x")[0].reshape(D, 128),
        "ffn1_g": _fm(f("ffn1_norm_g")[0]), "mix_g": _fm(f("mix_norm_g")[0]), "xattn_g": _fm(f("xattn_norm_g")[0]),
        "mem_g": _fm(f("mem_norm_g")[0]), "ffn2_g": _fm(f("ffn2_norm_g")[0]), "final_g": _fm(f("final_norm_g")),
        "lru_cw": _fm(f("lru_conv_w")[0].T), "lru_cb": _fm(f("lru_conv_b")[0]), "lru_ba": _fm(f("lru_b_a")[0]),
        "lru_bx": _fm(f("lru_b_x")[0]), "lru_lam": _fm(f("lru_lambda")[0]), "lru_og": _fm(f("lru_out_norm_g")[0]),
        "ssd_cw": _fm(f("ssd_conv_w")[0].T), "ssd_cb": _fm(f("ssd_conv_b")[0]), "ssd_og": _fm(f("ssd_out_norm_g")[0]),
        "ssd_dfm": _fm(np.repeat(f("ssd_d")[0], 64)),
        "dtb": f("ssd_dt_bias")[0].reshape(32, 1).copy(), "alog": f("ssd_a_log")[0].reshape(32, 1).copy(),
    }
    j = np.arange(128)
    seq_of = j // 8
    shared["c_ident"] = np.eye(128, dtype=np.float32)
    shared["c_ut"] = (j[:, None] <= j[None, :]).astype(np.float32)
    shared["c_slt"] = (j[:, None] > j[None, :]).astype(np.float32)
    shared["c_same"] = (seq_of[:, None] == seq_of[None, :]).astype(np.float32)
    shared["c_uts"] = shared["c_ut"] * shared["c_same"]
    shared["c_ones"] = np.ones((128, 128), np.float32)
    shared["c_seq"] = (seq_of[:, None] == np.arange(16)[None, :]).astype(np.float32)
    shared = {k: np.ascontiguousarray(v, dtype=np.float32) for k, v in shared.items()}

    in_maps = []
    for c in range(8):
        b, half = c // 2, c % 2
        m = dict(shared)
        m["xa"] = _tok_fm(x_prompt[b, 0:NP1])
        xs = x_sample[16 * c:16 * c + 16].reshape(NS, D)
        m["xb"] = _tok_fm(np.concatenate([x_prompt[b, half * NP1:(half + 1) * NP1], xs], 0))
        m["flag"] = np.full((128, 1), float(half), np.float32)
        m["memT"] = _tok_fm(mem_prompt[b])
        sl = slice(16 * c, 16 * c + 16)
        kk = ck[sl].reshape(NSEQ, 256, 16, 128)
        m["ckT"] = np.ascontiguousarray(kk.transpose(0, 3, 2, 1))
        vv = cv[sl].reshape(NSEQ, 2, 128, D)
        m["cv"] = np.ascontiguousarray(vv.transpose(0, 2, 1, 3))
        m["slc"] = np.ascontiguousarray(slc[sl].reshape(NSEQ, 3, 16, 128).transpose(3, 2, 0, 1))
        m["slh"] = np.ascontiguousarray(slh[sl].reshape(NSEQ, 16, 128).transpose(2, 1, 0))
        m["ssc"] = np.ascontiguousarray(ssc[sl].reshape(NSEQ, 3, 20, 128).transpose(3, 2, 0, 1))
        hh = ssh[sl].reshape(NSEQ, 16, 128, 128)
        m["sshN"] = np.ascontiguousarray(hh.transpose(1, 2, 0, 3))
        m["sshT"] = np.ascontiguousarray(hh.transpose(1, 3, 0, 2))
        in_maps.append(m)

    return in_maps


def kernel(**inp):
    if "nc" not in _CACHE:
        _CACHE["nc"] = build_program()
    nc, kb = _CACHE["nc"]
    in_maps = make_in_maps(inp)
    res = run_bass_kernel_spmd(nc, in_maps, core_ids=list(range(8)))
    R = res.results

    y_prompt = np.zeros((4, 2048, D), np.float32)
    y_sample = np.zeros((128, 8, D), np.float32)
    p_lc = np.zeros((1, 4, 3, D), np.float32)
    p_lh = np.zeros((1, 4, D), np.float32)
    p_sc = np.zeros((1, 4, 3, 2560), np.float32)
    p_sh = np.zeros((1, 4, 32, 64, 128), np.float32)
    p_mk = np.zeros((1, 4, 256, 4, 512), np.float32)
    p_mv = np.zeros((1, 4, 256, 4, 512), np.float32)
    s_lc = np.zeros((1, 128, 3, D), np.float32)
    s_lh = np.zeros((1, 128, D), np.float32)
    s_sc = np.zeros((1, 128, 3, 2560), np.float32)
    s_sh = np.zeros((1, 128, 32, 64, 128), np.float32)
    for c in range(8):
        b, half = c // 2, c % 2
        r = R[c]
        yt = _fm_tok(r["y"])
        y_prompt[b, half * NP1:(half + 1) * NP1] = yt[0:NP1]
        y_sample[16 * c:16 * c + 16] = yt[NP1:].reshape(16, 8, D)
        sl = slice(16 * c, 16 * c + 16)
        s_lc[0, sl] = r["o_slc"].transpose(2, 3, 1, 0).reshape(16, 3, D)
        s_lh[0, sl] = r["o_slh"].transpose(2, 1, 0).reshape(16, D)
        s_sc[0, sl] = r["o_ssc"].transpose(2, 3, 1, 0).reshape(16, 3, 2560)
        s_sh[0, sl] = r["o_ssh"].transpose(2, 0, 1, 3).reshape(16, 32, 64, 128)
        if half == 1:
            p_lc[0, b] = r["o_plc"].transpose(2, 1, 0).reshape(3, D)
            p_lh[0, b] = r["o_plh"].T.reshape(D)
            p_sc[0, b] = r["o_psc"].transpose(2, 1, 0).reshape(3, 2560)
            p_sh[0, b] = r["o_pshT"].transpose(1, 2, 0).reshape(32, 64, 128)
        else:
            p_mk[0, b] = _fm_tok(r["o_mkT"]).reshape(256, 4, 512)
            p_mv[0, b] = _fm_tok(r["o_mvT"]).reshape(256, 4, 512)
    return (y_prompt, y_sample, p_lc, p_lh, p_sc, p_sh, p_mk, p_mv, s_lc, s_lh, s_sc, s_sh)
```

```python
import os
import numpy as np
import concourse.bass as bass
import concourse.mybir as mybir
from concourse.bass_utils import run_bass_kernel_spmd

F32 = mybir.dt.float32
BF16 = mybir.dt.bfloat16
AF = mybir.ActivationFunctionType
ALU = mybir.AluOpType

D = 2048
FF = 5632
NHID = FF // 128
HG = 4
HPG = NHID // HG
DIN = 8736
NP1 = 1024
NS = 128
NT2 = NP1 + NS
NSEQ = 16
EPS = 1e-6


class KB:
    ENG = ("pe", "act", "dve", "pool", "sp")

    def __init__(self, nc):
        self.nc = nc
        self.ops = {e: [] for e in self.ENG}
        self.sem = {e: nc.alloc_semaphore("s_" + e) for e in ("pe", "act", "dve", "pool")}
        self.cnt = {e: 0 for e in ("pe", "act", "dve", "pool")}
        self.waited = {e: {} for e in self.ENG}
        self.lastw = {}
        self.readers = {}
        self.dsem = {}
        self.dcnt = {}
        self.n_inst = 0

    def _deps(self, reads, writes):
        deps = {}

        def add(tok):
            if tok is None:
                return
            s, v = tok
            if s in self.dcnt:
                v = self.dcnt[s]
            if deps.get(s, 0) < v:
                deps[s] = v

        for k in reads:
            add(self.lastw.get(k))
        for k in writes:
            add(self.lastw.get(k))
            for tok in self.readers.get(k, {}).items():
                add(tok)
        return deps

    def _commit(self, tok, reads, writes):
        for k in reads:
            r = self.readers.setdefault(k, {})
            if r.get(tok[0], 0) < tok[1]:
                r[tok[0]] = tok[1]
        for k in writes:
            self.lastw[k] = tok
            self.readers[k] = {}

    def _waits(self, eng, deps):
        ws = []
        for s, v in deps.items():
            if self.waited[eng].get(s, 0) >= v:
                continue
            self.waited[eng][s] = v
            ws.append((s, v))
        return ws

    def _semh(self, s):
        return self.sem[s] if s in self.sem else self.dsem[s]

    def op(self, eng, fn, reads=(), writes=(), inc=True):
        inc = True
        deps = self._deps(reads, writes)
        ws = self._waits(eng, deps)
        if inc:
            self.cnt[eng] += 1
            tok = (eng, self.cnt[eng])
        else:
            tok = (eng, self.cnt[eng] + 1)
        self.ops[eng].append((ws, fn, (self.sem[eng], 1) if inc else None))
        self._commit(tok, reads, writes)
        self.n_inst += 1

    def dma(self, q, out, in_, slot, reads=(), writes=()):
        if slot not in self.dsem:
            self.dsem[slot] = self.nc.alloc_semaphore("d%d" % len(self.dsem))
            self.dcnt[slot] = 0
        deps = self._deps(reads, writes)
        ws = self._waits(q, deps)
        self.dcnt[slot] += 16
        tok = (slot, self.dcnt[slot])
        self.ops[q].append((ws, lambda e, o=out, i=in_: e.dma_start(out=o, in_=i), (self.dsem[slot], 16)))
        self._commit(tok, reads, writes)
        self.n_inst += 1

    def barrier(self):
        toks = {e: v for e, v in self.cnt.items() if v > 0}
        toks.update({s: v for s, v in self.dcnt.items() if v > 0})
        for e in self.ENG:
            ws = self._waits(e, dict(toks))
            if ws:
                self.ops[e].append((ws, None, None))

    def final_wait(self, eng, keys):
        deps = self._deps(keys, keys)
        ws = self._waits(eng, deps)
        self.ops[eng].append((ws, None, None))

    def emit(self):
        nc = self.nc
        hmap = {"pe": "tensor", "act": "scalar", "dve": "vector", "pool": "gpsimd", "sp": "sync"}
        with nc.Block() as block:
            for e in self.ENG:
                ops = self.ops[e]

                def body(eng, ops=ops):
                    for ws, fn, inc in ops:
                        for s, v in ws:
                            eng.wait_ge(self._semh(s), v)
                        if fn is None:
                            continue
                        ins = fn(eng)
                        if inc is not None:
                            ins.then_inc(inc[0], inc[1])

                getattr(block, hmap[e])(body)


W2D = {
    "ffn1_wg": (D, FF), "ffn1_wu": (D, FF), "ffn1_wd": (FF, D),
    "ffn2_wg": (D, FF), "ffn2_wu": (D, FF), "ffn2_wd": (FF, D),
    "w_in": (D, DIN), "w_out": (2 * D, D),
    "w_q": (D, D), "w_k": (D, D), "w_v": (D, D), "w_o": (D, D),
    "lru_wa": (D, 128), "lru_wx": (D, 128),
}
PFM = {
    "ffn1_g": (16,), "mix_g": (16,), "xattn_g": (16,), "mem_g": (16,), "ffn2_g": (16,), "final_g": (16,),
    "lru_cw": (16, 4), "lru_cb": (16,), "lru_ba": (16,), "lru_bx": (16,), "lru_lam": (16,), "lru_og": (16,),
    "ssd_cw": (20, 4), "ssd_cb": (20,), "ssd_og": (16,), "ssd_dfm": (16,),
}
CONSTS = {"c_ident": (128,), "c_ut": (128,), "c_slt": (128,), "c_uts": (128,), "c_same": (128,),
          "c_ones": (128,), "c_seq": (16,)}


def build_program(stage=99):
    nc = bass.Bass("TRN2", target_bir_lowering=False)
    kb = KB(nc)

    kb.inputs = []

    def din(name, shape):
        kb.inputs.append(name)
        return nc.dram_tensor(name, list(shape), F32, kind="ExternalInput").ap()

    def dout(name, shape):
        return nc.dram_tensor(name, list(shape), F32, kind="ExternalOutput").ap()

    def sb(name, shape, dt=F32):
        return nc.alloc_sbuf_tensor(name, list(shape), dt)

    xa_d = din("xa", [128, 16, NP1])
    xb_d = din("xb", [128, 16, NT2])
    flag_d = din("flag", [128, 1])
    mem_d = din("memT", [128, 16, 256])
    ckT_d = din("ckT", [NSEQ, 128, 16, 256]) if stage >= 7 else None
    cv_d = din("cv", [NSEQ, 128, 2, D]) if stage >= 7 else None
    slc_d = din("slc", [128, 16, NSEQ, 3])
    slh_d = din("slh", [128, 16, NSEQ])
    ssc_d = din("ssc", [128, 20, NSEQ, 3])
    sshN_d = din("sshN", [16, 128, NSEQ, 128]) if stage >= 6 else None
    sshT_d = din("sshT", [16, 128, NSEQ, 128]) if stage >= 6 else None
    dtb_d = din("dtb", [32, 1])
    alog_d = din("alog", [32, 1])
    class _LazyW(dict):
        def __missing__(self, k):
            self[k] = din(k, W2D[k])
            return self[k]

    Wd = _LazyW()
    if stage >= 99:
        for k in W2D:
            Wd[k]
    Pd = {k: din(k, (128,) + v) for k, v in PFM.items()}
    Cd = {k: din(k, (128,) + v) for k, v in CONSTS.items()}

    y_d = dout("y", [128, 16, NT2])
    o_plc = dout("o_plc", [128, 16, 3])
    o_plh = dout("o_plh", [128, 16])
    o_psc = dout("o_psc", [128, 20, 3])
    o_pshT = dout("o_pshT", [128, 16, 128])
    o_mk = dout("o_mkT", [128, 16, 256])
    o_mv = dout("o_mvT", [128, 16, 256])
    o_slc = dout("o_slc", [128, 16, NSEQ, 3])
    o_slh = dout("o_slh", [128, 16, NSEQ])
    o_ssc = dout("o_ssc", [128, 20, NSEQ, 3])
    o_ssh = dout("o_ssh", [16, 128, NSEQ, 128])
    outkeys = []

    ARENA_BYTES = 212000
    ARENA = nc.alloc_sbuf_tensor("arena", [128, ARENA_BYTES // 4], F32)
    bump = {"p": 0}
    OFF = {}

    def view(off, shape, dt):
        n = int(np.prod(shape))
        isz = 4 if dt is F32 else 2
        assert off % 4 == 0
        nb = (n * isz + 31) // 32 * 32
        assert off + nb <= ARENA_BYTES, ("SBUF overflow", off, nb)
        v = ARENA[:, off // 4:off // 4 + (n * isz + 3) // 4]
        if dt is not F32:
            v = v.bitcast(dt)[:, 0:n]
        if len(shape) == 2:
            v = v.rearrange("p (a b) -> p a b", b=shape[1])
        elif len(shape) == 3:
            v = v.rearrange("p (a b c) -> p a b c", b=shape[1], c=shape[2])
        return v, nb

    def sb(name, shape, dt=F32):
        shape = list(shape)[1:]
        v, nb = view(bump["p"], shape, dt)
        OFF[name] = bump["p"]
        bump["p"] += nb
        return v

    X = sb("X", [128, 16, NT2])
    XN = sb("XN", [128, 16, NT2], BF16)
    MIX = sb("MIX", [128, 16, NT2], BF16)
    HID = MIX
    NWS = 4
    wslots = [sb("ws%d" % i, [128, 16, 128], BF16) for i in range(NWS)]
    prm = {k: sb("p_" + k, (128,) + v) for k, v in PFM.items()}
    cst = {k: sb(k, (128,) + v) for k, v in CONSTS.items()}
    cst_bf = {k: sb(k + "_bf", (128,) + CONSTS[k], BF16) for k in ("c_ones", "c_ident")}
    FLAG = sb("FLAG", [128, 1])
    EPSC = sb("EPSC", [128, 1])
    ONEC = sb("ONEC", [128, 1])
    LRUC = sb("LRUC", [128, 16])
    LRUC2 = sb("LRUC2", [128, 16])
    TAIL_L = sb("TAIL_L", [128, 16, 3])
    TAIL_S = sb("TAIL_S", [128, 20, 3])
    HEND1 = sb("HEND1", [128, 16])
    HINIT = sb("HINIT", [128, 16])
    H0S_L = sb("H0S_L", [128, 16, NSEQ])
    O_PLC = sb("O_PLC", [128, 16, 3])
    O_PLH = sb("O_PLH", [128, 16])
    O_PSC = sb("O_PSC", [128, 20, 3])
    O_SLH = sb("O_SLH", [128, 16, NSEQ])
    DTBA = sb("DTBA", [128, 2])
    SCR0 = bump["p"]
    T0 = sb("T0", [128, NT2])
    SQ = [sb("SQ%d" % i, [128, 512], BF16) for i in range(2)]
    SCR1 = bump["p"]
    print("SBUF persistent bytes", SCR0, "scratch avail", ARENA_BYTES - SCR1)
    RSTD = T0

    def scratch(base=None):
        bump["p"] = SCR1 if base is None else base

    hspill = nc.dram_tensor("hspill", [128, 16, 128], F32).ap()

    PA = nc.alloc_psum_tensor("PA", [128, 2048], F32)
    PB = nc.alloc_psum_tensor("PB", [128, 2048], F32)

    def pk(P, c0, c1):
        nm = "PA" if P is PA else "PB"
        return [(nm, b) for b in range(c0 // 512, (c1 - 1) // 512 + 1)]

    def sslot(i):
        P = PA if i < 4 else PB
        b = i % 4
        return P[:, b * 512:b * 512 + 128], ("PA" if i < 4 else "PB", b)

    wq = []
    wstate = {"issued": 0, "taken": 0}

    def wplan(name, r0, KC, c0, ncol=128):
        ap = Wd[name][r0:r0 + KC * 128, c0:c0 + ncol].rearrange("(kc p) f -> p kc f", p=128)
        wq.append((ap, KC, ncol))

    def wget():
        while wstate["issued"] < min(len(wq), wstate["taken"] + NWS - 1):
            i = wstate["issued"]
            ap, KC, ncol = wq[i]
            s = i % NWS
            kb.dma("pool", wslots[s][:, 0:KC, 0:ncol], ap, ("wd", s), writes=[("w", s)])
            wstate["issued"] += 1
        assert wstate["taken"] < len(wq)
        s = wstate["taken"] % NWS
        wstate["taken"] += 1
        return wslots[s], ("w", s)

    def mm(out, lhsT, rhs, start, stop, reads, writes, inc):
        kb.op("pe", lambda e: e.matmul(out, lhsT=lhsT, rhs=rhs, start=start, stop=stop),
              reads=reads, writes=writes, inc=inc)

    def proj(w, wkey, KC, rhs_fn, rkeys, P, segs, M=128):
        for kc in range(KC):
            for (c0, n) in segs:
                mm(P[0:M, c0:c0 + n], w[:, kc, 0:M], rhs_fn(kc, c0, n), kc == 0, kc == KC - 1,
                   [wkey] + rkeys, pk(P, c0, c0 + n), kc == KC - 1)

    def act(out, in_, func, reads, writes, bias=None, scale=None, eng="act"):
        kw = {}
        if bias is not None:
            kw["bias"] = bias
        if scale is not None:
            kw["scale"] = scale
        kb.op("act", lambda e: e.activation(out=out, in_=in_, func=func, **kw), reads=reads, writes=writes)

    def tt(eng, out, a, b, op, reads, writes):
        kb.op(eng, lambda e: e.tensor_tensor(out=out, in0=a, in1=b, op=op), reads=reads, writes=writes)

    def ts(eng, out, a, s1, s2, op0, op1, reads, writes):
        if op1 is None:
            kb.op(eng, lambda e: e.tensor_scalar(out=out, in0=a, scalar1=s1, scalar2=None, op0=op0),
                  reads=reads, writes=writes)
        else:
            kb.op(eng, lambda e: e.tensor_scalar(out=out, in0=a, scalar1=s1, scalar2=s2, op0=op0, op1=op1),
                  reads=reads, writes=writes)

    def stt(eng, out, a, s, b, op0, op1, reads, writes):
        eng = "dve"
        kb.op(eng, lambda e: e.scalar_tensor_tensor(out=out, in0=a, scalar=s, in1=b, op0=op0, op1=op1),
              reads=reads, writes=writes)

    def cp(eng, out, in_, reads, writes):
        if eng == "act":
            kb.op("act", lambda e: e.copy(out=out, in_=in_), reads=reads, writes=writes)
        else:
            kb.op(eng, lambda e: e.tensor_copy(out=out, in_=in_), reads=reads, writes=writes)

    def memset(eng, ap, val, writes):
        kb.op(eng, lambda e: e.memset(ap, val), writes=writes)

    for k in PFM:
        kb.dma("sp", prm[k][:], Pd[k], "ld0", writes=["p_" + k])
    for k in CONSTS:
        kb.dma("sp", cst[k][:], Cd[k], "ld0", writes=[k])
    kb.dma("sp", FLAG[:], flag_d, "ld0", writes=["FLAG"])
    kb.dma("sp", DTBA[0:32, 0:1], dtb_d, "ld0", writes=["DTBA"])
    kb.dma("sp", DTBA[0:32, 1:2], alog_d, "ld0", writes=["DTBA"])
    kb.dma("sp", H0S_L[:], slh_d, "ld0", writes=["H0S_L"])
    memset("dve", EPSC[:], EPS, ["EPSC"])
    memset("dve", ONEC[:], 1.0, ["ONEC"])
    for k in ("c_ones", "c_ident"):
        cp("dve", cst_bf[k][:], cst[k][:], [k], [k + "_bf"])
    act(DTBA[0:32, 1:2], DTBA[0:32, 1:2], AF.Exp, ["DTBA"], ["DTBA"])
    ts("dve", DTBA[0:32, 1:2], DTBA[0:32, 1:2], -1.0, None, ALU.mult, None, ["DTBA"], ["DTBA"])
    act(LRUC[:], prm["lru_lam"][:], AF.Exp, ["p_lru_lam"], ["LRUC"], scale=-1.0)
    act(LRUC[:], LRUC[:], AF.Ln, ["LRUC", "ONEC"], ["LRUC"], bias=ONEC[:, 0:1])
    ts("dve", LRUC2[:], LRUC[:], -16.0, None, ALU.mult, None, ["LRUC"], ["LRUC2"])
    ts("dve", LRUC[:], LRUC[:], -8.0, None, ALU.mult, None, ["LRUC"], ["LRUC"])

    ones_bf = cst_bf["c_ones"]

    def xkeys(c=None):
        return [("X", c)] if c is not None else [("X", i) for i in range(16)]

    sqi = {"i": 0}

    def rms_rstd(src_fn, skeys_fn, nchunk, NT, segs, scale=1.0 / D):
        for c in range(nchunk):
            for (c0, n) in segs:
                i = sqi["i"] % 2
                sqi["i"] += 1
                act(SQ[i][:, 0:n], src_fn(c)[:, c0:c0 + n], AF.Square, skeys_fn(c), [("SQ", i)])
                mm(PA[:, c0:c0 + n], ones_bf[:, :], SQ[i][:, 0:n], c == 0, c == nchunk - 1,
                   [("SQ", i), "c_ones_bf"], pk(PA, c0, c0 + n), True)
        act(RSTD[:, 0:NT], PA[:, 0:NT], AF.Sqrt, pk(PA, 0, NT) + ["EPSC"], [("T", 0)], bias=EPSC[:, 0:1], scale=scale)
        kb.op("dve", lambda e: e.reciprocal(out=RSTD[:, 0:NT], in_=RSTD[:, 0:NT]), reads=[("T", 0)], writes=[("T", 0)])

    def norm_to_xn(gname, NT, segs):
        rms_rstd(lambda c: X[:, c, :], lambda c: xkeys(c), 16, NT, segs)
        for c in range(16):
            stt("dve" if c % 2 == 0 else "pool", XN[:, c, 0:NT], X[:, c, 0:NT], prm[gname][:, c:c + 1], RSTD[:, 0:NT],
                ALU.mult, ALU.mult, xkeys(c) + [("T", 0), "p_" + gname], [("XN", c)])

    def xn_rhs(kc, c0, n):
        return XN[:, kc, c0:c0 + n]

    xnkeys = [("XN", c) for c in range(16)]
    mixkeys = [("MIX", c) for c in range(16)]

    def ffn(pre, gname, NT, segs):
        if os.environ.get("DBG_SKIP") == "f":
            return
        scratch()
        T1 = sb("T1", [128, NT2])
        Ts = [T0, T1]
        SUB = int(os.environ.get("DBG_SUB", "99"))
        if SUB == 0:
            rms_rstd(lambda c: X[:, c, :], lambda c: xkeys(c), 16, NT, segs)
            return
        norm_to_xn(gname, NT, segs)
        if SUB == 1:
            return
        for hg in range(HG):
            for j in range(HPG):
                f = hg * HPG + j
                wplan(pre + "_wg", 0, 16, f * 128)
                wplan(pre + "_wu", 0, 16, f * 128)
            for dc in range(16):
                wplan(pre + "_wd", hg * HPG * 128, HPG, dc * 128)
        for hg in range(HG):
            for j in range(HPG):
                wg, kg = wget()
                proj(wg, kg, 16, xn_rhs, xnkeys, PA, segs)
                wu, ku = wget()
                proj(wu, ku, 16, xn_rhs, xnkeys, PB, segs)
                if SUB == 2:
                    return
                sg = Ts[j % 2]
                act(sg[:, 0:NT], PA[:, 0:NT], AF.Silu, pk(PA, 0, NT), [("T", j % 2)])
                tt("dve", HID[:, j, 0:NT], sg[:, 0:NT], PB[:, 0:NT], ALU.mult,
                   [("T", j % 2)] + pk(PB, 0, NT), [("MIX", j)])
                if SUB == 3:
                    return
            if SUB == 4:
                return
            for dc in range(16):
                wd, kd = wget()
                P = PA if dc % 2 == 0 else PB
                for j in range(HPG):
                    for (c0, n) in segs:
                        mm(P[:, c0:c0 + n], wd[:, j, :], HID[:, j, c0:c0 + n], j == 0, j == HPG - 1,
                           [kd, ("MIX", j)], pk(P, c0, c0 + n), j == HPG - 1)
                stt("dve", X[:, dc, 0:NT], P[:, 0:NT], 0.5, X[:, dc, 0:NT], ALU.mult, ALU.add,
                    pk(P, 0, NT) + xkeys(dc), xkeys(dc))
        kb.barrier()

    def conv_silu(P, cw, cb, ci, tail_init, tname, convs, cname, NT, full, out_u, ukey, tail_save, tsname,
                  o_p, opname, o_s, osname, silu, EXTP, EXTS):
        pkeys = pk(P, 0, NT)
        if tail_init is None:
            memset("pool", EXTP[:, 0:3], 0.0, ["EXTP"])
        else:
            ts("pool", EXTP[:, 0:3], tail_init[:, ci, :], FLAG[:, 0:1], None, ALU.mult, None, [tname, "FLAG"], ["EXTP"])
        cp("act", EXTP[:, 3:3 + NP1], P[:, 0:NP1], pkeys, ["EXTP"])
        if tail_save is not None:
            cp("pool", tail_save[:, ci, :], EXTP[:, NP1:NP1 + 3], ["EXTP"], [tsname])
        if o_p is not None:
            cp("pool", o_p[:, ci, :], EXTP[:, NP1:NP1 + 3], ["EXTP"], [opname])
        up = out_u[:, 0:NP1]
        ts("dve", up, EXTP[:, 0:NP1], cw[:, ci, 0:1], cb[:, ci:ci + 1], ALU.mult, ALU.add, ["EXTP"], [ukey])
        for k in range(1, 4):
            stt("dve", up, EXTP[:, k:k + NP1], cw[:, ci, k:k + 1], up, ALU.mult, ALU.add, ["EXTP", ukey], [ukey])
        if full:
            kb.dma("sp", EXTS[:, :, 0:3], convs[:, ci, :, :], "ld_cv3", writes=["EXTS"])
            cp("act", EXTS[:, :, 3:11], P[:, NP1:NT2].rearrange("p (s t) -> p s t", t=8), pkeys, ["EXTS"])
            kb.dma("sp", o_s[:, ci, :, :], EXTS[:, :, 8:11], "st_cv3", reads=["EXTS"])
            us = out_u[:, NP1:NT2].rearrange("p (s t) -> p s t", t=8)
            ts("dve", us, EXTS[:, :, 0:8], cw[:, ci, 0:1], cb[:, ci:ci + 1], ALU.mult, ALU.add, ["EXTS", ukey], [ukey])
            for k in range(1, 4):
                stt("dve", us, EXTS[:, :, k:k + 8], cw[:, ci, k:k + 1], us, ALU.mult, ALU.add, ["EXTS", ukey], [ukey])
        if silu:
            act(out_u[:, 0:NT], out_u[:, 0:NT], AF.Silu, [ukey], [ukey])

    def wout_half(r0, NT, segs, wname="w_out"):
        for dc in range(16):
            wplan(wname, r0, 16, dc * 128)
        for dc in range(16):
            w, k = wget()
            P = PA if dc % 2 == 0 else PB
            proj(w, k, 16, lambda kc, c0, n: MIX[:, kc, c0:c0 + n], mixkeys, P, segs)
            tt("dve", X[:, dc, 0:NT], X[:, dc, 0:NT], P[:, 0:NT], ALU.add, xkeys(dc) + pk(P, 0, NT), xkeys(dc))

    def lru_phase(blk, NT, segs):
        full = blk == 2
        scratch()
        T1 = sb("T1", [128, NT2])
        T2 = sb("T2", [128, NT2])
        T3 = sb("T3", [128, NT2])
        UB = sb("UB", [128, NT2], BF16)
        GL = sb("GL", [128, NT2], BF16)
        EXTP = sb("EXTP", [128, 3 + NP1])
        EXTS = sb("EXTS", [128, NSEQ, 11])
        for j in range(16):
            if full:
                wplan("w_in", 0, 16, 2048 + j * 128)
            wplan("w_in", 0, 16, j * 128)
            wplan("lru_wa", j * 128, 1, 0)
            wplan("lru_wx", j * 128, 1, 0)
        if full:
            ts("dve", HINIT[:, :], HEND1[:, :], FLAG[:, 0:1], None, ALU.mult, None, ["HEND1", "FLAG"], ["HINIT"])
        for j in range(16):
            if full:
                wg_, kg_ = wget()
                proj(wg_, kg_, 16, xn_rhs, xnkeys, PB, segs)
                act(GL[:, 0:NT], PB[:, 0:NT], AF.Gelu, pk(PB, 0, NT), ["GL"])
            wx_, kx_ = wget()
            proj(wx_, kx_, 16, xn_rhs, xnkeys, PA, segs)
            U = T0
            conv_silu(PA, prm["lru_cw"], prm["lru_cb"], j, TAIL_L if full else None, "TAIL_L", slc_d, "CONVS_L", NT, full,
                      U, ("T", 0), None if full else TAIL_L, "TAIL_L", O_PLC if full else None, "O_PLC",
                      o_slc if full else None, "O_SLC", False, EXTP, EXTS)
            cp("act", UB[:, 0:NT], U[:, 0:NT], [("T", 0)], ["UB"])
            wa_, ka_ = wget()
            wxx_, kxx_ = wget()
            for (c0, n) in segs:
                mm(PA[:, c0:c0 + n], wa_[:, 0, :], UB[:, c0:c0 + n], True, True, [ka_, "UB"], pk(PA, c0, c0 + n), True)
            for (c0, n) in segs:
                mm(PB[:, c0:c0 + n], wxx_[:, 0, :], UB[:, c0:c0 + n], True, True, [kxx_, "UB"], pk(PB, c0, c0 + n), True)
            R, I, A, HS = T1, T2, T3, T0
            act(R[:, 0:NT], PA[:, 0:NT], AF.Sigmoid, pk(PA, 0, NT), [("T", 1)], bias=prm["lru_ba"][:, j:j + 1])
            act(I[:, 0:NT], PB[:, 0:NT], AF.Sigmoid, pk(PB, 0, NT), [("T", 2)], bias=prm["lru_bx"][:, j:j + 1])
            act(A[:, 0:NT], R[:, 0:NT], AF.Exp, [("T", 1), "LRUC"], [("T", 3)], scale=LRUC[:, j:j + 1])
            act(R[:, 0:NT], R[:, 0:NT], AF.Exp, [("T", 1), "LRUC2"], [("T", 1)], scale=LRUC2[:, j:j + 1])
            act(R[:, 0:NT], R[:, 0:NT], AF.Sqrt, [("T", 1), "ONEC"], [("T", 1)], bias=ONEC[:, 0:1], scale=-1.0)
            tt("dve", I[:, 0:NT], I[:, 0:NT], U[:, 0:NT], ALU.mult, [("T", 2), ("T", 0)], [("T", 2)])
            tt("dve", R[:, 0:NT], R[:, 0:NT], I[:, 0:NT], ALU.mult, [("T", 1), ("T", 2)], [("T", 1)])
            init = HINIT[:, j:j + 1] if full else 0.0
            kb.op("dve", lambda e, init=init: e.tensor_tensor_scan(out=HS[:, 0:NP1], data0=A[:, 0:NP1], data1=R[:, 0:NP1],
                                                                  initial=init, op0=ALU.mult, op1=ALU.add),
                  reads=[("T", 3), ("T", 1), "HINIT"], writes=[("T", 0)])
            if not full:
                cp("pool", HEND1[:, j:j + 1], HS[:, NP1 - 1:NP1], [("T", 0)], ["HEND1"])
                continue
            cp("pool", O_PLH[:, j:j + 1], HS[:, NP1 - 1:NP1], [("T", 0)], ["O_PLH"])
            for s in range(NSEQ):
                c0 = NP1 + 8 * s
                kb.op("dve", lambda e, c0=c0, s=s, j=j: e.tensor_tensor_scan(
                    out=HS[:, c0:c0 + 8], data0=A[:, c0:c0 + 8], data1=R[:, c0:c0 + 8],
                    initial=H0S_L[:, j, s:s + 1], op0=ALU.mult, op1=ALU.add),
                    reads=[("T", 3), ("T", 1), "H0S_L"], writes=[("T", 0)])
            cp("pool", O_SLH[:, j, :], HS[:, NP1:NT2].rearrange("p (s t) -> p s t", t=8)[:, :, 7], [("T", 0)], ["O_SLH"])
            tt("dve", MIX[:, j, 0:NT], HS[:, 0:NT], GL[:, 0:NT], ALU.mult, [("T", 0), "GL"], [("MIX", j)])
        kb.barrier()
        if not full:
            return
        rms_rstd(lambda c: MIX[:, c, :], lambda c: [("MIX", c)], 16, NT, segs)
        for c in range(16):
            stt("dve" if c % 2 == 0 else "pool", MIX[:, c, 0:NT], MIX[:, c, 0:NT], prm["lru_og"][:, c:c + 1], RSTD[:, 0:NT],
                ALU.mult, ALU.mult, [("MIX", c), ("T", 0)], [("MIX", c)])
        wout_half(0, NT, segs)
        kb.barrier()

    def ssd_phase(blk, NT, segs):
        full = blk == 2
        ntile = NT // 128
        scratch()
        ZS = sb("ZS", [128, NT2], BF16)
        EXTP = sb("EXTP", [128, 3 + NP1])
        EXTS = sb("EXTS", [128, NSEQ, 11])
        CT1 = sb("CT1", [128, NT2], BF16)
        BTOK1 = sb("BTOK1", [128, 9, 128], BF16)
        CBM1 = sb("CBM1", [128, 9, 128], BF16)
        DT_TOK = sb("DT_TOK", [128, 9, 32])
        DA_TOK = sb("DA_TOK", [128, 9, 32])
        CS_TOK = sb("CS_TOK", [128, 9, 32])
        CSL_BC = sb("CSL_BC", [128, 9, 32])
        TOEND = sb("TOEND", [128, 9, 32])
        DEC_BC = sb("DEC_BC", [128, 9, 32])
        XPAD = sb("XPAD", [128, 2, 128], BF16)
        XSC = sb("XSC", [128, 128], BF16)
        XSCM = sb("XSCM", [128, 2, 128], BF16)
        SM = [sb("SM%d" % i, [128, 128]) for i in range(8)]
        SMB = [sb("SMB%d" % i, [128, 128], BF16) for i in range(2)]
        HT = sb("HT", [128, 128])
        HTB = sb("HTB", [128, 128], BF16)
        H0N = sb("H0N", [128, 2, 128])
        H0T = sb("H0T", [128, 2, 128])
        H0TB = sb("H0TB", [128, 2, 128], BF16)
        H1N = sb("H1N", [128, 2, 128])
        DECF = sb("DECF", [128, NSEQ])
        csz = NT2 * 2
        DTFv, _ = view(OFF["MIX"] + 7 * csz, [NT2], F32)
        DAFv, _ = view(OFF["MIX"] + 9 * csz, [NT2], F32)
        BTs = [MIX[:, 11, :], MIX[:, 12, :]]
        CTs = [MIX[:, 13, :], CT1[:, :]]
        BTOKs = [MIX[:, 14, :].rearrange("p (t n) -> p t n", n=128), BTOK1]
        CBMs = [MIX[:, 15, :].rearrange("p (t n) -> p t n", n=128), CBM1]
        kDTF = [("MIX", 7), ("MIX", 8)]
        kDAF = [("MIX", 9), ("MIX", 10)]
        kBT = [[("MIX", 11)], [("MIX", 12)]]
        kCT = [[("MIX", 13)], ["CT1"]]
        kBTOK = [[("MIX", 14)], ["BTOK1"]]
        kCBM = [[("MIX", 15)], ["CBM1"]]
        ut, slt, uts, same, onesf, seqm, ident = (cst["c_ut"], cst["c_slt"], cst["c_uts"], cst["c_same"],
                                                  cst["c_ones"], cst["c_seq"], cst["c_ident"])
        XS = T0
        for ci in (16, 17, 18, 19):
            wplan("w_in", 0, 16, 6144 + ci * 128)
        if os.environ.get("DBG_NODT") is None:
            wplan("w_in", 0, 16, 8704, 32)
        for pr in range(16):
            wplan("w_in", 0, 16, 6144 + pr * 128)
            if full:
                wplan("w_in", 0, 16, 4096 + pr * 128)
        memset("pool", XPAD[:, :, :], 0.0, ["XPAD"])

        def do_conv(P, ci):
            conv_silu(P, prm["ssd_cw"], prm["ssd_cb"], ci, TAIL_S if full else None, "TAIL_S", ssc_d, "CONVS_S", NT, full,
                      XS, ("T", 0), None if full else TAIL_S, "TAIL_S", O_PSC if full else None, "O_PSC",
                      o_ssc if full else None, "O_SSC", True, EXTP, EXTS)

        for ci in (16, 17, 18, 19):
            w, k = wget()
            proj(w, k, 16, xn_rhs, xnkeys, PA, segs)
            do_conv(PA, ci)
            g = (ci - 16) % 2
            if ci < 18:
                cp("pool", BTs[g][:, 0:NT], XS[:, 0:NT], [("T", 0)], kBT[g])
                for t in range(ntile):
                    so, sk = sslot(t % 8)
                    kb.op("pe", lambda e, so=so, t=t: e.transpose(so, XS[:, t * 128:(t + 1) * 128], ident[:, :]),
                          reads=[("T", 0), "c_ident"], writes=[sk])
                    cp("act" if t % 2 == 0 else "dve", BTOKs[g][:, t, :], so, [sk], kBTOK[g])
            else:
                cp("pool", CTs[g][:, 0:NT], XS[:, 0:NT], [("T", 0)], kCT[g])
        SUB = int(os.environ.get("DBG_SUB", "99"))
        if SUB == 0:
            return
        w, k = wget()
        proj(w, k, 16, xn_rhs, xnkeys, PA, segs, M=32)
        if SUB == 1:
            return
        act(DTFv[0:32, 0:NT], PA[0:32, 0:NT], AF.Exp, pk(PA, 0, NT) + ["DTBA"], kDTF, bias=DTBA[0:32, 0:1])
        act(DTFv[0:32, 0:NT], DTFv[0:32, 0:NT], AF.Ln, kDTF + ["ONEC"], kDTF, bias=ONEC[0:32, 0:1])
        ts("dve", DAFv[0:32, 0:NT], DTFv[0:32, 0:NT], DTBA[0:32, 1:2], None, ALU.mult, None, kDTF + ["DTBA"], kDAF)
        if SUB == 2:
            return
        for t in range(ntile):
            so, sk = sslot(t % 8)
            kb.op("pe", lambda e, so=so, t=t: e.transpose(so[:, 0:32], DTFv[0:32, t * 128:(t + 1) * 128], ident[0:32, 0:32]),
                  reads=kDTF + ["c_ident"], writes=[sk], inc=False)
            kb.op("pe", lambda e, so=so, t=t: e.transpose(so[:, 32:64], DAFv[0:32, t * 128:(t + 1) * 128], ident[0:32, 0:32]),
                  reads=kDAF + ["c_ident"], writes=[sk])
            cp("act", DT_TOK[:, t, :], so[:, 0:32], [sk], ["DT_TOK"])
            cp("act", DA_TOK[:, t, :], so[:, 32:64], [sk], ["DA_TOK"])
        if SUB == 3:
            return
        for t in range(ntile):
            samp = full and t == 8
            so, sk = sslot(t % 8)
            mm(so[:, 0:32], (uts if samp else ut)[:, :], DA_TOK[:, t, :], True, True, ["DA_TOK", "c_ut", "c_uts"], [sk], False)
            mm(so[:, 32:64], (same if samp else onesf)[:, :], DA_TOK[:, t, :], True, True, ["DA_TOK", "c_same", "c_ones"], [sk], True)
            cp("act", CS_TOK[:, t, :], so[:, 0:32], [sk], ["CS_TOK"])
            cp("act", CSL_BC[:, t, :], so[:, 32:64], [sk], ["CSL_BC"])
        tt("dve", TOEND[:, 0:ntile, :], CSL_BC[:, 0:ntile, :], CS_TOK[:, 0:ntile, :], ALU.subtract, ["CSL_BC", "CS_TOK"], ["TOEND"])
        act(TOEND[:, 0:ntile, :], TOEND[:, 0:ntile, :], AF.Exp, ["TOEND"], ["TOEND"])
        tt("dve", TOEND[:, 0:ntile, :], TOEND[:, 0:ntile, :], DT_TOK[:, 0:ntile, :], ALU.mult, ["TOEND", "DT_TOK"], ["TOEND"])
        act(DEC_BC[:, 0:ntile, :], CSL_BC[:, 0:ntile, :], AF.Exp, ["CSL_BC"], ["DEC_BC"])
        if SUB == 4:
            return
        if full:
            for g in range(2):
                for t in range(ntile):
                    samp = t == 8
                    so, sk = sslot((g * ntile + t) % 8)
                    mm(so, BTs[g][:, t * 128:(t + 1) * 128], CTs[g][:, t * 128:(t + 1) * 128], True, True,
                       kBT[g] + kCT[g], [sk], True)
                    tt("dve", CBMs[g][:, t, :], so, (uts if samp else ut)[:, :], ALU.mult, [sk, "c_ut", "c_uts"], kCBM[g])
        if SUB == 5:
            return
        for pr in range(16):
            if SUB in (6, 7) and pr == 1:
                return
            g = pr // 8
            h0 = 2 * pr
            w, k = wget()
            proj(w, k, 16, xn_rhs, xnkeys, PA, segs)
            do_conv(PA, pr)
            if full:
                w, k = wget()
                proj(w, k, 16, xn_rhs, xnkeys, PB, segs)
                act(ZS[:, 0:NT], PB[:, 0:NT], AF.Silu, pk(PB, 0, NT), ["ZS"])
                if os.environ.get("DBG_NOHSP"):
                    memset("dve", HT[:, :], 0.0, ["HT"])
                else:
                    kb.dma("sp", HT[:, :], hspill[:, pr, :], "ld_hsp", reads=["hspill"], writes=["HT"])
                    ts("dve", HT[:, :], HT[:, :], FLAG[:, 0:1], None, ALU.mult, None, ["HT", "FLAG"], ["HT"])
            else:
                memset("dve", HT[:, :], 0.0, ["HT"])
            for t in range(ntile):
                samp = full and t == 8
                if samp and SUB == 6:
                    continue
                U_ = uts if samp else ut
                cols = slice(t * 128, (t + 1) * 128)
                sx, kx = sslot(6 + t % 2)
                kb.op("pe", lambda e, sx=sx, t=t: e.transpose(sx, XS[:, t * 128:(t + 1) * 128], ident[:, :]),
                      reads=[("T", 0), "c_ident"], writes=[kx])
                if full and not os.environ.get("DBG_NOXPAD"):
                    cp("act", XPAD[:, 0, 0:64], sx[:, 0:64], [kx], ["XPAD"])
                    cp("act", XPAD[:, 1, 64:128], sx[:, 64:128], [kx], ["XPAD"])
                for hh_ in range(2):
                    act(XSC[:, hh_ * 64:(hh_ + 1) * 64], sx[:, hh_ * 64:(hh_ + 1) * 64], AF.Copy, [kx, "TOEND"], ["XSC"],
                        scale=TOEND[:, t, h0 + hh_:h0 + hh_ + 1])
                if SUB == 61:
                    return
                if full:
                    DABC, ECS = SM[0], SM[1]
                    cp("dve", DABC[:, :].rearrange("p (h q) -> p h q", q=64),
                       DA_TOK[:, t, h0:h0 + 2].unsqueeze(2).to_broadcast([128, 2, 64]), ["DA_TOK"], [("SM", 0)])
                    s0, k0 = sslot(0)
                    mm(s0, DABC[:, :], U_[:, :], True, True, [("SM", 0), "c_ut", "c_uts"], [k0], True)
                    act(ECS[:, :], s0, AF.Exp, [k0], [("SM", 1)])
                    if SUB == 62:
                        return
                    sy, ky = sslot(3)
                    for hh in range(2):
                        LM, EH, WDT = SM[2 + hh], SM[4 + hh], SMB[hh]
                        ts("dve", LM[:, :], slt[:, :], DA_TOK[:, t, h0 + hh:h0 + hh + 1], None, ALU.mult, None,
                           ["c_slt", "DA_TOK"], [("SM", 2 + hh)])
                        sd, kd = sslot(1 + hh)
                        mm(sd, LM[:, :], U_[:, :], True, True, [("SM", 2 + hh), "c_ut", "c_uts"], [kd], True)
                        act(EH[:, :], sd, AF.Exp, [kd], [("SM", 4 + hh)])
                        stt("dve", WDT[:, :], EH[:, :], DT_TOK[:, t, h0 + hh:h0 + hh + 1], CBMs[g][:, t, :], ALU.mult, ALU.mult,
                            [("SM", 4 + hh), "DT_TOK"] + kCBM[g], [("SMB", hh)])
                        mm(sy, XPAD[:, hh, :], WDT[:, :], hh == 0, hh == 1, ["XPAD", ("SMB", hh)], [ky], hh == 1)
                    if SUB == 63:
                        return
                    sr, kr = sslot(4)
                    if not samp:
                        cp("act", HTB[:, :], HT[:, :], ["HT"], ["HTB"])
                        mm(sr, HTB[:, :], CTs[g][:, cols], True, True, ["HTB"] + kCT[g], [kr], True)
                if SUB == 64:
                    return
                if not samp:
                    sh, kh = sslot(5)
                    mm(sh, BTOKs[g][:, t, :], XSC[:, :], True, True, kBTOK[g] + ["XSC"], [kh], True)
                    tt("dve", HT[:, :].rearrange("p (h q) -> p h q", q=64), HT[:, :].rearrange("p (h q) -> p h q", q=64),
                       DEC_BC[:, t, h0:h0 + 2].unsqueeze(2).to_broadcast([128, 2, 64]), ALU.mult, ["HT", "DEC_BC"], ["HT"])
                    tt("dve", HT[:, :], HT[:, :], sh, ALU.add, ["HT", kh], ["HT"])
                else:
                    s5, k5 = sslot(5)
                    mm(s5[:, 0:NSEQ], DABC[:, :], seqm[:, :], True, True, [("SM", 0), "c_seq"], [k5], True)
                    act(DECF[:, :], s5[:, 0:NSEQ], AF.Exp, [k5], ["DECF"])
                    for q in range(8):
                        kb.dma("sp", H0T[:, :, :], sshT_d[pr][:, 2 * q:2 * q + 2, :], "ld_h0t", writes=["H0T"])
                        kb.dma("sp", H0N[:, :, :], sshN_d[pr][:, 2 * q:2 * q + 2, :], "ld_h0n", writes=["H0N"])
                        cp("act", H0TB[:, :, :], H0T[:, :, :], ["H0T"], ["H0TB"])
                        for u in range(2):
                            s = 2 * q + u
                            mm(sr[:, 8 * s:8 * s + 8], H0TB[:, u, :], CTs[g][:, NP1 + 8 * s:NP1 + 8 * s + 8], True, True,
                               ["H0TB"] + kCT[g], [kr], s == NSEQ - 1 or u == 1)
                        tt("dve", XSCM[:, :, :], XSC[:, :].unsqueeze(1).to_broadcast([128, 2, 128]),
                           seqm[:, 2 * q:2 * q + 2].unsqueeze(2).to_broadcast([128, 2, 128]), ALU.mult, ["XSC", "c_seq"], ["XSCM"])
                        bo = (q % 2) * 512
                        for u in range(2):
                            mm(PA[:, bo + u * 128:bo + (u + 1) * 128], XSCM[:, u, :], BTOKs[g][:, 8, :], True, True,
                               ["XSCM"] + kBTOK[g], pk(PA, bo, bo + 512), u == 1)
                        for u in range(2):
                            s = 2 * q + u
                            stt("dve", H1N[:, u, :], H0N[:, u, :], DECF[:, s:s + 1], PA[:, bo + u * 128:bo + (u + 1) * 128],
                                ALU.mult, ALU.add, ["H0N", "DECF"] + pk(PA, bo, bo + 512), ["H1N"])
                        kb.dma("sp", o_ssh[pr][:, 2 * q:2 * q + 2, :], H1N[:, :, :], "st_h1", reads=["H1N"])
                    outkeys.append("H1N")
                if full:
                    T1_, Y2 = SM[6], SM[7]
                    tt("dve", T1_[:, :], sr, ECS[:, :], ALU.mult, [kr, ("SM", 1)], [("SM", 6)])
                    stt("dve", Y2[:, :], XS[:, cols], prm["ssd_dfm"][:, pr:pr + 1], sy, ALU.mult, ALU.add,
                        [("T", 0), ky], [("SM", 7)])
                    tt("dve", Y2[:, :], Y2[:, :], T1_[:, :], ALU.add, [("SM", 7), ("SM", 6)], [("SM", 7)])
                    tt("dve", MIX[:, pr, cols], Y2[:, :], ZS[:, cols], ALU.mult, [("SM", 7), "ZS"], [("MIX", pr)])
            if full:
                kb.dma("sp", o_pshT[:, pr, :], HT[:, :], "st_psh", reads=["HT"])
                outkeys.append("HT")
            else:
                kb.dma("sp", hspill[:, pr, :], HT[:, :], "st_hsp", reads=["HT"], writes=["hspill"])
        kb.barrier()
        if not full:
            return
        for g in range(2):
            rms_rstd(lambda c, g=g: MIX[:, g * 8 + c, :], lambda c, g=g: [("MIX", g * 8 + c)], 8, NT, segs, scale=1.0 / 1024)
            for c in range(8 * g, 8 * g + 8):
                stt("dve" if c % 2 == 0 else "pool", MIX[:, c, 0:NT], MIX[:, c, 0:NT], prm["ssd_og"][:, c:c + 1], RSTD[:, 0:NT],
                    ALU.mult, ALU.mult, [("MIX", c), ("T", 0)], [("MIX", c)])
        wout_half(2048, NT, segs)
        kb.barrier()

    def xattn(NT, segs):
        norm_to_xn("xattn_g", NT, segs)
        for dc in range(16):
            wplan("w_q", 0, 16, dc * 128)
        for nm in ("w_k", "w_v"):
            for dc in range(16):
                wplan(nm, 0, 16, dc * 128)
        ident = cst["c_ident"]
        for dc in range(16):
            w, k = wget()
            P = PA if dc % 2 == 0 else PB
            proj(w, k, 16, xn_rhs, xnkeys, P, segs)
            cp("act" if dc % 2 == 0 else "dve", MIX[:, dc, 0:NT], P[:, 0:NT], pk(P, 0, NT), [("MIX", dc)])
        kb.barrier()
        scratch(OFF["XN"])
        KT = sb("KT", [128, 16, 256], BF16)
        VTOK = sb("VTOK", [128, 2, D], BF16)
        MEMN = sb("MEMN", [128, 16, 256], BF16)
        ET = sb("ET", [128, 2, NT2], BF16)
        assert bump["p"] <= OFF["XN"] + 16 * NT2 * 2
        scratch()
        MRS = sb("MRS", [128, 256])
        MEMX = [sb("MEMX%d" % i, [128, 256]) for i in range(2)]
        STG = [sb("STG%d" % i, [128, 256]) for i in range(2)]
        KTS = [sb("KTS%d" % i, [128, 4, 256], BF16) for i in range(2)]
        VS = [sb("VS%d" % i, [128, 2, 512], BF16) for i in range(2)]
        for c in range(16):
            kb.dma("sp", MEMX[c % 2][:, :], mem_d[:, c, :], ("ld_mem", c % 2), writes=[("MEMX", c % 2)])
            act(SQ[c % 2][:, 0:256], MEMX[c % 2][:, :], AF.Square, [("MEMX", c % 2)], [("SQ", c % 2)])
            mm(PA[:, 0:256], ones_bf[:, :], SQ[c % 2][:, 0:256], c == 0, c == 15, [("SQ", c % 2), "c_ones_bf"], pk(PA, 0, 256), True)
        act(MRS[:, :], PA[:, 0:256], AF.Sqrt, pk(PA, 0, 256) + ["EPSC"], ["MRS"], bias=EPSC[:, 0:1], scale=1.0 / D)
        kb.op("dve", lambda e: e.reciprocal(out=MRS[:, :], in_=MRS[:, :]), reads=["MRS"], writes=["MRS"])
        for c in range(16):
            kb.dma("sp", MEMX[c % 2][:, :], mem_d[:, c, :], ("ld_mem", c % 2), writes=[("MEMX", c % 2)])
            stt("dve", MEMN[:, c, :], MEMX[c % 2][:, :], prm["mem_g"][:, c:c + 1], MRS[:, :], ALU.mult, ALU.mult,
                [("MEMX", c % 2), "MRS"], ["MEMN"])
        msegs = [(0, 256)]
        si = 0
        for nm in ("w_k", "w_v"):
            for dc in range(16):
                w, k = wget()
                P = PA if dc % 2 == 0 else PB
                proj(w, k, 16, lambda kc, c0, n: MEMN[:, kc, c0:c0 + n], ["MEMN"], P, msegs)
                st = STG[si % 2]
                sk_ = ("STG", si % 2)
                si += 1
                cp("act", st[:, :], P[:, 0:256], pk(P, 0, 256), [sk_])
                if nm == "w_k":
                    cp("dve", KT[:, dc, :], st[:, :], [sk_], ["KT"])
                    kb.dma("sp", o_mk[:, dc, :], st[:, :], ("st_kv", (si - 1) % 2), reads=[sk_])
                else:
                    kb.dma("sp", o_mv[:, dc, :], st[:, :], ("st_kv", (si - 1) % 2), reads=[sk_])
                    for mt in range(2):
                        so, sk = sslot((dc * 2 + mt) % 8)
                        kb.op("pe", lambda e, so=so, st=st, mt=mt: e.transpose(so, st[:, mt * 128:(mt + 1) * 128], ident[:, :]),
                              reads=[sk_, "c_ident"], writes=[sk])
                        cp("act" if mt == 0 else "pool" if False else "dve", VTOK[:, mt, dc * 128:(dc + 1) * 128], so, [sk], ["VTOK"])
        outkeys.extend([("STG", 0), ("STG", 1)])
        sc = 512.0 ** -0.5
        psegs = [(0, 512), (512, 512)]
        li = 0
        for h in range(4):
            for mt in range(2):
                P = PA if mt == 0 else PB
                for kc in range(4):
                    for (c0, n) in psegs:
                        mm(P[:, c0:c0 + n], KT[:, 4 * h + kc, mt * 128:(mt + 1) * 128], MIX[:, 4 * h + kc, c0:c0 + n],
                           kc == 0, kc == 3, ["KT", ("MIX", 4 * h + kc)], pk(P, c0, c0 + n), kc == 3)
            for s in range(NSEQ):
                b = li % 2
                li += 1
                kb.dma("pool", KTS[b][:, :, :], ckT_d[s][:, 4 * h:4 * h + 4, :], ("ld_ck", b), writes=[("KTS", b)])
                for mt in range(2):
                    P = PA if mt == 0 else PB
                    for kc in range(4):
                        mm(P[:, 1024 + 8 * s:1024 + 8 * s + 8], KTS[b][:, kc, mt * 128:(mt + 1) * 128],
                           MIX[:, 4 * h + kc, NP1 + 8 * s:NP1 + 8 * s + 8], kc == 0, kc == 3,
                           [("KTS", b), ("MIX", 4 * h + kc)], pk(P, 1024, 1152), kc == 3)
            for mt in range(2):
                P = PA if mt == 0 else PB
                act(ET[:, mt, 0:NT], P[:, 0:NT], AF.Exp, pk(P, 0, NT), [("ET", mt)], scale=sc)
            for (c0, n) in segs:
                for mt in range(2):
                    mm(PA[:, c0:c0 + n], ones_bf[:, :], ET[:, mt, c0:c0 + n], mt == 0, mt == 1,
                       [("ET", mt), "c_ones_bf"], pk(PA, c0, c0 + n), mt == 1)
            kb.op("dve", lambda e: e.reciprocal(out=RSTD[:, 0:NT], in_=PA[:, 0:NT]), reads=pk(PA, 0, NT), writes=[("T", 0)])
            for s in range(NSEQ):
                b = s % 2
                kb.dma("pool", VS[b][:, :, :], cv_d[s][:, :, h * 512:(h + 1) * 512], ("ld_cv", b), writes=[("VS", b)])
                for dcl in range(4):
                    for mt in range(2):
                        mm(PB[:, 1024 + dcl * 128 + 8 * s:1024 + dcl * 128 + 8 * s + 8], VS[b][:, mt, dcl * 128:(dcl + 1) * 128],
                           ET[:, mt, NP1 + 8 * s:NP1 + 8 * s + 8], mt == 0, mt == 1, [("VS", b), ("ET", mt)],
                           pk(PB, 1024, 1536), mt == 1)
            for dcl in range(4):
                dc = 4 * h + dcl
                tt("dve", MIX[:, dc, NP1:NT2], PB[:, 1024 + dcl * 128:1024 + dcl * 128 + 128], RSTD[:, NP1:NT2], ALU.mult,
                   pk(PB, 1024, 1536) + [("T", 0)], [("MIX", dc)])
            for dcl in range(4):
                dc = 4 * h + dcl
                for mt in range(2):
                    for (c0, n) in psegs:
                        mm(PB[:, c0:c0 + n], VTOK[:, mt, dc * 128:(dc + 1) * 128], ET[:, mt, c0:c0 + n], mt == 0, mt == 1,
                           ["VTOK", ("ET", mt)], pk(PB, c0, c0 + n), mt == 1)
                tt("dve", MIX[:, dc, 0:NP1], PB[:, 0:NP1], RSTD[:, 0:NP1], ALU.mult, pk(PB, 0, NP1) + [("T", 0)], [("MIX", dc)])
        kb.barrier()
        wout_half(0, NT, segs, wname="w_o")
        kb.barrier()

    segs1 = [(0, 512), (512, 512)]
    segs2 = [(0, 512), (512, 512), (1024, 128)]

    def finish(dump=None):
        if dump is not None:
            kb.barrier()
            ncd = NP1 if stage < 4 else NT2
            for c in range(16):
                kb.dma("sp", y_d[:, c, 0:ncd], X[:, c, 0:ncd], "st_dbg", reads=xkeys(c))
            outkeys.extend(xkeys())
            if dump == "xn" or dump == "mix":
                src = XN if dump == "xn" else MIX
                for c in range(16):
                    kb.dma("pool", dbg_d[:, c, 0:ncd], src[:, c, 0:ncd], "st_dbg2", reads=[(dump.upper(), c)])
                outkeys.extend([(dump.upper(), c) for c in range(16)])
        olist = ((o_plc, O_PLC, "O_PLC"), (o_plh, O_PLH, "O_PLH"), (o_psc, O_PSC, "O_PSC"), (o_slh, O_SLH, "O_SLH"))
        if stage in (2, 3):
            olist = ((o_plc, TAIL_L, "TAIL_L"), (o_plh, HEND1, "HEND1"), (o_psc, TAIL_S, "TAIL_S"))
            if stage == 2:
                olist = olist[0:2]
            if stage == 3 and os.environ.get("DBG_SUB") is None:
                kb.dma("sp", o_pshT, hspill, "st_misc", reads=["hspill"])
        for (dst, src, key) in olist:
            kb.dma("sp", dst, src, "st_misc", reads=[key])
            outkeys.append(key)
        outkeys.append("EXTS")
        kb.final_wait("sp", outkeys)
        kb.emit()
        return nc, kb

    dbg_d = dout("dbg", [128, 16, NT2]) if stage < 99 else None
    for c in range(16):
        kb.dma("sp", X[:, c, 0:NP1], xa_d[:, c, :], "ld_x", writes=xkeys(c))
    if stage == 0:
        return finish("x")
    SKIP = os.environ.get("DBG_SKIP") is not None
    if not SKIP:
        ffn("ffn1", "ffn1_g", NP1, segs1)
    if stage == 1:
        return finish("xn")
    norm_to_xn("mix_g", NP1, segs1)
    if not SKIP or os.environ.get("DBG_SKIP") == "f":
        lru_phase(1, NP1, segs1)
    if stage == 2:
        return finish("xn")
    ssd_phase(1, NP1, segs1)
    if stage == 3:
        return finish("xn")
    for c in range(16):
        kb.dma("sp", X[:, c, :], xb_d[:, c, :], "ld_x", writes=xkeys(c))
    ffn("ffn1", "ffn1_g", NT2, segs2)
    if stage == 4:
        return finish("xn")
    norm_to_xn("mix_g", NT2, segs2)
    lru_phase(2, NT2, segs2)
    if stage == 5:
        return finish("mix")
    ssd_phase(2, NT2, segs2)
    if stage == 6:
        return finish("mix")
    xattn(NT2, segs2)
    if stage == 7:
        return finish("mix")
    ffn("ffn2", "ffn2_g", NT2, segs2)
    scratch()
    OST = [sb("OST%d" % i, [128, NT2]) for i in range(3)]
    rms_rstd(lambda c: X[:, c, :], lambda c: xkeys(c), 16, NT2, segs2)
    for c in range(16):
        o = OST[c % 3]
        stt("dve", o[:, :], X[:, c, :], prm["final_g"][:, c:c + 1], RSTD[:, :], ALU.mult, ALU.mult,
            xkeys(c) + [("T", 0)], [("OST", c % 3)])
        kb.dma("sp", y_d[:, c, :], o[:, :], ("st_y", c % 3), reads=[("OST", c % 3)])
    outkeys.extend([("OST", i) for i in range(3)])
    return finish(None)


_CACHE = {}


def _fm(v, shape_tail=()):
    v = np.asarray(v, np.float32)
    C = v.shape[0] // 128
    return np.ascontiguousarray(np.moveaxis(v.reshape((C, 128) + v.shape[1:]), 0, 1))


def _tok_fm(x):
    T_ = x.shape[0]
    return np.ascontiguousarray(x.reshape(T_, 16, 128).transpose(2, 1, 0))


def _fm_tok(y):
    return np.ascontiguousarray(y.transpose(2, 1, 0).reshape(y.shape[2], -1))


def make_in_maps(inp):
    f = lambda k: np.asarray(inp[k], np.float32)

    x_prompt, mem_prompt, x_sample = f("x_prompt"), f("mem_prompt"), f("x_sample")
    ck, cv = f("cache_mem_k")[0], f("cache_mem_v")[0]
    slc, slh, ssc, ssh = f("state_lru_conv")[0], f("state_lru_h")[0], f("state_ssd_conv")[0], f("state_ssd_h")[0]

    shared = {
        "ffn1_wg": f("ffn1_w_gate")[0], "ffn1_wu": f("ffn1_w_up")[0], "ffn1_wd": f("ffn1_w_down")[0],
        "ffn2_wg": f("ffn2_w_gate")[0], "ffn2_wu": f("ffn2_w_up")[0], "ffn2_wd": f("ffn2_w_down")[0],
        "w_in": f("w_in")[0], "w_out": f("w_out")[0],
        "w_q": f("xattn_w_q")[0], "w_k": f("xattn_w_k")[0], "w_v": f("xattn_w_v")[0], "w_o": f("xattn_w_o")[0],
        "lru_wa": f("lru_w_a")[0].reshape(D, 128), "lru_wx": f("lru_w_x")[0].reshape(D, 128),
        "ffn1_g": _fm(f("ffn1_norm_g")[0]), "mix_g": _fm(f("mix_norm_g")[0]), "xattn_g": _fm(f("xattn_norm_g")[0]),
        "mem_g": _fm(f("mem_norm_g")[0]), "ffn2_g": _fm(f("ffn2_norm_g")[0]), "final_g": _fm(f("final_norm_g")),
        "lru_cw": _fm(f("lru_conv_w")[0].T), "lru_cb": _fm(f("lru_conv_b")[0]), "lru_ba": _fm(f("lru_b_a")[0]),
        "lru_bx": _fm(f("lru_b_x")[0]), "lru_lam": _fm(f("lru_lambda")[0]), "lru_og": _fm(f("lru_out_norm_g")[0]),
        "ssd_cw": _fm(f("ssd_conv_w")[0].T), "ssd_cb": _fm(f("ssd_conv_b")[0]), "ssd_og": _fm(f("ssd_out_norm_g")[0]),
        "ssd_dfm": _fm(np.repeat(f("ssd_d")[0], 64)),
        "dtb": f("ssd_dt_bias")[0].reshape(32, 1).copy(), "alog": f("ssd_a_log")[0].reshape(32, 1).copy(),
    }
    j = np.arange(128)
    seq_of = j // 8
    shared["c_ident"] = np.eye(128, dtype=np.float32)
    shared["c_ut"] = (j[:, None] <= j[None, :]).astype(np.float32)
    shared["c_slt"] = (j[:, None] > j[None, :]).astype(np.float32)
    shared["c_same"] = (seq_of[:, None] == seq_of[None, :]).astype(np.float32)
    shared["c_uts"] = shared["c_ut"] * shared["c_same"]
    shared["c_ones"] = np.ones((128, 128), np.float32)
    shared["c_seq"] = (seq_of[:, None] == np.arange(16)[None, :]).astype(np.float32)
    shared = {k: np.ascontiguousarray(v, dtype=np.float32) for k, v in shared.items()}

    in_maps = []
    for c in range(8):
        b, half = c // 2, c % 2
        m = dict(shared)
        m["xa"] = _tok_fm(x_prompt[b, 0:NP1])
        xs = x_sample[16 * c:16 * c + 16].reshape(NS, D)
        m["xb"] = _tok_fm(np.concatenate([x_prompt[b, half * NP1:(half + 1) * NP1], xs], 0))
        m["flag"] = np.full((128, 1), float(half), np.float32)
        m["memT"] = _tok_fm(mem_prompt[b])
        sl = slice(16 * c, 16 * c + 16)
        kk = ck[sl].reshape(NSEQ, 256, 16, 128)
        m["ckT"] = np.ascontiguousarray(kk.transpose(0, 3, 2, 1))
        vv = cv[sl].reshape(NSEQ, 2, 128, D)
        m["cv"] = np.ascontiguousarray(vv.transpose(0, 2, 1, 3))
        m["slc"] = np.ascontiguousarray(slc[sl].reshape(NSEQ, 3, 16, 128).transpose(3, 2, 0, 1))
        m["slh"] = np.ascontiguousarray(slh[sl].reshape(NSEQ, 16, 128).transpose(2, 1, 0))
        m["ssc"] = np.ascontiguousarray(ssc[sl].reshape(NSEQ, 3, 20, 128).transpose(3, 2, 0, 1))
        hh = ssh[sl].reshape(NSEQ, 16, 128, 128)
        m["sshN"] = np.ascontiguousarray(hh.transpose(1, 2, 0, 3))
        m["sshT"] = np.ascontiguousarray(hh.transpose(1, 3, 0, 2))
        in_maps.append(m)

    return in_maps


def kernel(**inp):
    if "nc" not in _CACHE:
        _CACHE["nc"] = build_program()
    nc, kb = _CACHE["nc"]
    in_maps = make_in_maps(inp)
    res = run_bass_kernel_spmd(nc, in_maps, core_ids=list(range(8)))
    R = res.results

    y_prompt = np.zeros((4, 2048, D), np.float32)
    y_sample = np.zeros((128, 8, D), np.float32)
    p_lc = np.zeros((1, 4, 3, D), np.float32)
    p_lh = np.zeros((1, 4, D), np.float32)
    p_sc = np.zeros((1, 4, 3, 2560), np.float32)
    p_sh = np.zeros((1, 4, 32, 64, 128), np.float32)
    p_mk = np.zeros((1, 4, 256, 4, 512), np.float32)
    p_mv = np.zeros((1, 4, 256, 4, 512), np.float32)
    s_lc = np.zeros((1, 128, 3, D), np.float32)
    s_lh = np.zeros((1, 128, D), np.float32)
    s_sc = np.zeros((1, 128, 3, 2560), np.float32)
    s_sh = np.zeros((1, 128, 32, 64, 128), np.float32)
    for c in range(8):
        b, half = c // 2, c % 2
        r = R[c]
        yt = _fm_tok(r["y"])
        y_prompt[b, half * NP1:(half + 1) * NP1] = yt[0:NP1]
        y_sample[16 * c:16 * c + 16] = yt[NP1:].reshape(16, 8, D)
        sl = slice(16 * c, 16 * c + 16)
        s_lc[0, sl] = r["o_slc"].transpose(2, 3, 1, 0).reshape(16, 3, D)
        s_lh[0, sl] = r["o_slh"].transpose(2, 1, 0).reshape(16, D)
        s_sc[0, sl] = r["o_ssc"].transpose(2, 3, 1, 0).reshape(16, 3, 2560)
        s_sh[0, sl] = r["o_ssh"].transpose(2, 0, 1, 3).reshape(16, 32, 64, 128)
        if half == 1:
            p_lc[0, b] = r["o_plc"].transpose(2, 1, 0).reshape(3, D)
            p_lh[0, b] = r["o_plh"].T.reshape(D)
            p_sc[0, b] = r["o_psc"].transpose(2, 1, 0).reshape(3, 2560)
            p_sh[0, b] = r["o_pshT"].transpose(1, 2, 0).reshape(32, 64, 128)
        else:
            p_mk[0, b] = _fm_tok(r["o_mkT"]).reshape(256, 4, 512)
            p_mv[0, b] = _fm_tok(r["o_mvT"]).reshape(256, 4, 512)
    return (y_prompt, y_sample, p_lc, p_lh, p_sc, p_sh, p_mk, p_mv, s_lc, s_lh, s_sc, s_sh)
```
